# Optimizing a Trainium2 kernel written in Bass

```python
import jax, jax.numpy as jnp
from jax import lax
import numpy as np

D_MODEL = 1024
BATCH = 8
SEQ = 4096
DEPTH = 1

CHUNK = 64
POOL_WIDTH = D_MODEL // 2
POOL_GROUPS = 4
POOL_GROUP_DIM = POOL_WIDTH // POOL_GROUPS
POOL_WINDOWS = (2, 4, 8, 16)
CONV_WIDTH = D_MODEL // 2
CONV_K = 31
D_FF = 4 * D_MODEL
IN_COLS = POOL_WIDTH + 2 * CONV_WIDTH + 2 * D_MODEL
EPS = 1e-6

kernel_name = "hybrid_pool_conformer_conv_gated_block"


def rmsnorm(x, g):
    xf = x.astype(jnp.float32)
    y = xf * lax.rsqrt(jnp.mean(xf * xf, axis=-1, keepdims=True) + EPS)
    return (y * g.astype(jnp.float32)).astype(x.dtype)


def layernorm(x, g, b):
    xf = x.astype(jnp.float32)
    mu = jnp.mean(xf, axis=-1, keepdims=True)
    var = jnp.mean(jnp.square(xf - mu), axis=-1, keepdims=True)
    y = (xf - mu) * lax.rsqrt(var + EPS)
    return (y * g.astype(jnp.float32) + b.astype(jnp.float32)).astype(x.dtype)


def trailing_mean_minus_self(u, window):
    uf = u.astype(jnp.float32)
    c = jnp.cumsum(uf, axis=1)
    c_prev = jnp.pad(c, ((0, 0), (window, 0), (0, 0)))[:, : u.shape[1], :]
    count = jnp.minimum(jnp.arange(1, u.shape[1] + 1, dtype=jnp.float32), float(window))
    mean = (c - c_prev) / count[None, :, None]
    return (mean - uf).astype(u.dtype)


def pool_mixer(u, pool_w, pool_scale, w_pool_out):
    groups = jnp.split(u, POOL_GROUPS, axis=-1)
    pooled = jnp.stack([trailing_mean_minus_self(g, w) for g, w in zip(groups, POOL_WINDOWS)], axis=2)
    mixed = jnp.einsum('bsgc,gcd->bsgd', pooled, pool_w)
    mixed = mixed.reshape(u.shape) * pool_scale
    return mixed @ w_pool_out


def conformer_conv(glu_in, conv_w, conv_b, ln_g, ln_b, w_conv_out):
    a, gate = jnp.split(glu_in, 2, axis=-1)
    v = a * jax.nn.sigmoid(gate)
    rhs = conv_w.astype(v.dtype)[:, None, :]
    v = lax.conv_general_dilated(v, rhs, window_strides=(1,), padding=[(CONV_K - 1, 0)],
                                 dimension_numbers=('NWC', 'WIO', 'NWC'),
                                 feature_group_count=CONV_WIDTH) + conv_b
    v = layernorm(v, ln_g, ln_b)
    v = jax.nn.silu(v)
    return v @ w_conv_out


def setup_inputs(seed: int = 0) -> dict:
    key = jax.random.key(seed)
    ks = jax.random.split(key, 20)
    L, D = DEPTH, D_MODEL
    nrm = lambda k, shape, fan_in: jax.random.normal(k, shape, jnp.float32) * (fan_in ** -0.5)
    gain = lambda k, shape: 1.0 + 0.05 * jax.random.normal(k, shape, jnp.float32)
    return {
        "x": jax.random.normal(ks[0], (BATCH, SEQ, D), jnp.float32),
        "norm_mix_pre": gain(ks[1], (L, D)),
        "w_in": nrm(ks[2], (L, D, IN_COLS), D),
        "pool_w": nrm(ks[3], (L, POOL_GROUPS, POOL_GROUP_DIM, POOL_GROUP_DIM), POOL_GROUP_DIM),
        "pool_scale": gain(ks[4], (L, POOL_WIDTH)),
        "w_pool_out": nrm(ks[5], (L, POOL_WIDTH, D), POOL_WIDTH),
        "conv_w": nrm(ks[6], (L, CONV_K, CONV_WIDTH), CONV_K),
        "conv_b": 0.02 * jax.random.normal(ks[7], (L, CONV_WIDTH), jnp.float32),
        "conv_ln_g": gain(ks[8], (L, CONV_WIDTH)),
        "conv_ln_b": 0.02 * jax.random.normal(ks[9], (L, CONV_WIDTH), jnp.float32),
        "w_conv_out": nrm(ks[10], (L, CONV_WIDTH, D), CONV_WIDTH),
        "w_o": nrm(ks[11], (L, D, D), D),
        "norm_mix_post": gain(ks[12], (L, D)),
        "norm_mlp_pre": gain(ks[13], (L, D)),
        "w_up": nrm(ks[14], (L, D, D_FF), D),
        "w_down": nrm(ks[15], (L, D_FF, D), D_FF),
        "norm_mlp_post": gain(ks[16], (L, D)),
    }


def reference(x, norm_mix_pre, w_in, pool_w, pool_scale, w_pool_out, conv_w, conv_b,
              conv_ln_g, conv_ln_b, w_conv_out, w_o, norm_mix_post, norm_mlp_pre,
              w_up, w_down, norm_mlp_post):
    for l in range(DEPTH):
        h = rmsnorm(x, norm_mix_pre[l])
        proj = h @ w_in[l]
        u_pool, u_glu, g_a, g_b = jnp.split(
            proj, [POOL_WIDTH, POOL_WIDTH + 2 * CONV_WIDTH, POOL_WIDTH + 2 * CONV_WIDTH + D_MODEL], axis=-1)
        branch_a = pool_mixer(u_pool, pool_w[l], pool_scale[l], w_pool_out[l])
        branch_b = conformer_conv(u_glu, conv_w[l], conv_b[l], conv_ln_g[l], conv_ln_b[l], w_conv_out[l])
        merged = jax.nn.sigmoid(g_a) * branch_a + jax.nn.sigmoid(g_b) * branch_b
        x = x + rmsnorm(merged @ w_o[l], norm_mix_post[l])
        h = rmsnorm(x, norm_mlp_pre[l])
        f = jnp.square(jax.nn.relu(h @ w_up[l])) @ w_down[l]
        x = x + rmsnorm(f, norm_mlp_post[l])
    return x
```

```python
import numpy as np
from contextlib import ExitStack
import concourse.bass as bass
import concourse.mybir as mybir
from concourse.bass_utils import run_bass_kernel_spmd

F32 = mybir.dt.float32
BF16 = mybir.dt.bfloat16
I32 = mybir.dt.int32
AF = mybir.ActivationFunctionType
ALU = mybir.AluOpType

D = 1024
SEQ = 4096
T = 512
NKC = 8
DFF = 4096
NFC = 32
EPS = 1e-6
CONV_K = 31
HIST_U = 16
HIST_V = 32
NSLOT = 8
SLOTW = 2048

PE, ACT, DVE, POOL, SP = "pe", "act", "dve", "pool", "sp"
ENGINES = (PE, ACT, DVE, POOL, SP)

C_NMP, C_NMPOST, C_NLP, C_NLPOST = 0, 8, 16, 24
C_PSC, C_CB, C_LG, C_LB, C_CW = 32, 36, 40, 44, 48
NSMALL = C_CW + 4 * CONV_K

B_IN, B_PW, B_PO, B_CO, B_WO, B_UP, B_DN = 0, 14, 15, 19, 23, 27, 43
NB_W = 59
NB_ALL = 59


def blk_ncols(blk):
    if blk == B_PW:
        return 512
    if B_PO <= blk < B_WO:
        return 1024
    return SLOTW


class Buf:
    __slots__ = ("name", "last_w", "readers")

    def __init__(self, name):
        self.name = name
        self.last_w = None
        self.readers = []


class Op:
    __slots__ = ("eng", "fn", "pos", "waits", "signals", "dma_sem", "dma_val", "count")


class Prog:
    def __init__(self):
        self.ops = {e: [] for e in ENGINES}
        self.waited = {e: {} for e in ENGINES}
        self.dma_n = {}

    def add(self, eng, fn, reads=(), writes=(), dma_sem=None):
        op = Op()
        op.eng, op.fn, op.pos = eng, fn, len(self.ops[eng])
        op.waits, op.signals, op.dma_sem, op.dma_val, op.count = [], False, dma_sem, 0, 0
        deps = []
        for b in reads:
            if b.last_w is not None:
                deps.append(b.last_w)
        for b in writes:
            if b.last_w is not None:
                deps.append(b.last_w)
            deps.extend(b.readers)
        w = self.waited[eng]
        best = {}
        dma_deps = []
        for d in deps:
            if d.dma_sem is not None:
                dma_deps.append(d)
            elif d.eng not in best or best[d.eng].pos < d.pos:
                best[d.eng] = d
        deps = dma_deps + list(best.values())
        for d in deps:
            if d.dma_sem is not None:
                key = ("dma", id(d.dma_sem))
                if w.get(key, 0) >= d.dma_val:
                    continue
                w[key] = d.dma_val
                op.waits.append(d)
            else:
                if d.eng == eng and eng in (PE, SP):
                    continue
                if w.get(d.eng, -1) >= d.pos:
                    continue
                w[d.eng] = d.pos
                d.signals = True
                op.waits.append(d)
        if dma_sem is not None:
            n = self.dma_n.get(id(dma_sem), 0) + 1
            self.dma_n[id(dma_sem)] = n
            op.dma_val = 16 * n
        for b in reads:
            b.readers.append(op)
        for b in writes:
            b.last_w = op
            b.readers = []
        self.ops[eng].append(op)
        return op

    def finalize(self):
        for e in ENGINES:
            c = 0
            for op in self.ops[e]:
                if op.signals and op.dma_sem is None:
                    c += 1
                op.count = c

    def run(self, eng_name, eng, sems, final_waits=()):
        for op in self.ops[eng_name]:
            for d in op.waits:
                if d.dma_sem is not None:
                    eng.wait_ge(d.dma_sem, d.dma_val)
                else:
                    eng.wait_ge(sems[d.eng], d.count)
            ins = op.fn(eng)
            if op.dma_sem is not None:
                ins.then_inc(op.dma_sem, 16)
            elif op.signals:
                ins.then_inc(sems[eng_name], 1)
        for sem, val in final_waits:
            eng.wait_ge(sem, val)


class WS:
    def __init__(self, order=None):
        self.order = order
        self.rec = []
        self.next_load = 0
        self.acq = 0
        self.emit_load = None

    def acquire(self, blk):
        if self.order is None:
            self.rec.append(blk)
            return (len(self.rec) - 1) % NSLOT
        sq_n = self.acq
        assert self.order[sq_n] == blk, (sq_n, self.order[sq_n], blk)
        assert sq_n < self.next_load, (sq_n, self.next_load)
        self.acq += 1
        return sq_n % NSLOT

    def start(self):
        if self.order is None:
            return
        while self.next_load < min(NSLOT, len(self.order)):
            self.emit_load(self.next_load)
            self.next_load += 1

    def release(self, n=1):
        if self.order is None:
            return
        for _ in range(n):
            if self.next_load < len(self.order):
                self.emit_load(self.next_load)
                self.next_load += 1


def build_nc(n_tiles=SEQ // T):
    seq = n_tiles * T
    nc = bass.Bass("TRN2", target_bir_lowering=False)
    xT = nc.dram_tensor("xT", [D, seq], F32, kind="ExternalInput").ap()
    wimg = nc.dram_tensor("wimg", [128, NB_W * SLOTW], F32, kind="ExternalInput").ap()
    smalld = nc.dram_tensor("small", [128, NSMALL], F32, kind="ExternalInput").ap()
    outT = nc.dram_tensor("outT", [D, seq], F32, kind="ExternalOutput").ap()
    wscr = nc.dram_tensor("wscr", [128, NB_ALL * SLOTW], BF16, kind="Internal").ap()
    xT3 = xT.rearrange("(kc p) t -> p kc t", p=128)
    outT3 = outT.rearrange("(kc p) t -> p kc t", p=128)

    with ExitStack() as es:
        def sb(name, shape, dt):
            return es.enter_context(nc.sbuf_tensor(name, shape, dt))

        def sem(name):
            return es.enter_context(nc.semaphore(name))

        xb = [sb(f"xb{i}", [128, NKC, T], F32) for i in range(2)]
        h = sb("h", [128, NKC, T], BF16)
        h2 = sb("h2", [128, NKC, T], BF16)
        NSQ = 4
        sq = sb("sq", [128, NSQ, T], BF16)
        ybuf = sb("ybuf", [128, NKC, T], F32)
        NST = 4
        stt = sb("stt", [128, NST, T], F32)
        onesD = sb("onesD", [128, 128], BF16)
        onesC = sb("onesC", [128, 128], BF16)
        small = sb("smallsb", [128, NSMALL], F32)
        epsT = sb("epsT", [128, 1], F32)
        invci = sb("invci", [128, HIST_U], I32)
        invc = sb("invc", [128, 4, HIST_U], F32)
        ring = [sb(f"ring{i}", [128, SLOTW], BF16) for i in range(NSLOT)]
        upool = sb("upool", [128, 4, HIST_U + T], F32)
        ptmp = sb("ptmp", [128, 2, HIST_U + T], F32)
        pooled = sb("pooled", [128, 4, T], BF16)
        mixed = sb("mixed", [128, 4, T], BF16)
        vb = sb("vb", [128, 4, HIST_V + T], BF16)
        sg = sb("sg", [128, 2, T], F32)
        cc = sb("cc", [128, 4, T], F32)
        cbf = sb("cbf", [128, 2, T], BF16)
        csq = sb("csq", [128, 2, T], BF16)
        lnt = sb("lnt", [128, 2, T], F32)
        convact = sb("convact", [128, 4, T], BF16)
        NMT = 4
        mtmp = sb("mtmp", [128, NMT, T], F32)
        fbuf = sb("fbuf", [128, NFC, T], BF16)
        MG0 = NFC - NKC

        banks = [es.enter_context(nc.psum_tensor(f"ps{i}", [128, T], F32)) for i in range(8)]

        sems = {e: sem(f"s_{e}") for e in (PE, ACT, DVE, POOL)}
        ring_ld = [sem(f"rl{i}") for i in range(NSLOT)]
        ring_st = [sem(f"rs{i}") for i in range(NSLOT)]
        stage_ld = [sem(f"sl{i}") for i in range(2)]
        x_ld = [sem(f"xl{i}") for i in range(2)]
        x_st = [sem(f"xs{i}") for i in range(2)]
        small_ld = sem("smld")
        pc_st = [sem(f"pc{i}") for i in range(2)]

        def emit(P, ws):
            xb_b = [[Buf(f"xb{i}_{k}") for k in range(NKC)] for i in range(2)]
            h_b = [Buf(f"h{k}") for k in range(NKC)]
            h2_b = [Buf(f"h2{k}") for k in range(NKC)]
            sq_b = [Buf(f"sq{k}") for k in range(NSQ)]
            y_b = [Buf(f"y{k}") for k in range(NKC)]
            st_b = [Buf(f"st{k}") for k in range(NST)]
            const_b = Buf("consts")
            ring_b = [Buf(f"ring{i}") for i in range(NSLOT)]
            up_b = [Buf(f"up{g}") for g in range(4)]
            pt_b = [Buf("ptA"), Buf("ptB")]
            pl_b = [Buf(f"pl{g}") for g in range(4)]
            mx_b = [Buf(f"mx{g}") for g in range(4)]
            vb_b = [Buf(f"vb{c}") for c in range(4)]
            sg_b = [Buf("sg0"), Buf("sg1")]
            cc_b = [Buf(f"cc{c}") for c in range(4)]
            cbf_b = [Buf("cbf0"), Buf("cbf1")]
            csq_b = [Buf("csq0"), Buf("csq1")]
            ln_b = [Buf("ln0"), Buf("ln1")]
            ca_b = [Buf(f"ca{c}") for c in range(4)]
            mt_b = [Buf(f"mt{k}") for k in range(NMT)]
            f_b = [Buf(f"f{k}") for k in range(NFC)]
            bank_b = [Buf(f"ps{i}") for i in range(8)]
            scr_b = [Buf(f"scr{i}") for i in range(NB_ALL)]
            bank_i = [0]
            rr = {}

            def rot(key, n):
                i = rr.get(key, 0)
                rr[key] = i + 1
                return i % n

            def new_bank():
                i = bank_i[0] % 5
                bank_i[0] += 1
                return banks[i], bank_b[i]

            def stat_bank(i):
                return banks[6 + i], bank_b[6 + i]

            def rms_bank():
                return banks[5], bank_b[5]

            merged_ap = lambda kc: fbuf[:, MG0 + kc, :]
            mg_b = f_b[MG0:]

            P.add(SP, lambda e: e.dma_start(out=small[:, :], in_=smalld[:, :]), writes=[const_b], dma_sem=small_ld)
            P.add(POOL, lambda e: e.memset(onesD[:, :], 1.0 / D), writes=[const_b])
            P.add(POOL, lambda e: e.memset(onesC[:, :], 1.0 / 512.0), writes=[const_b])
            P.add(POOL, lambda e: e.memset(epsT[:, :], EPS), writes=[const_b])
            P.add(POOL, lambda e: e.iota(invci[:, :], pattern=[[1, HIST_U]], base=1, channel_multiplier=0), writes=[const_b])
            for g in range(4):
                wdw = float(2 ** (g + 1))
                P.add(DVE, lambda e, g=g, wdw=wdw: e.tensor_single_scalar(out=invc[:, g, :], in_=invci[:, :], scalar=wdw, op=ALU.min),
                      reads=[const_b], writes=[const_b])
            P.add(DVE, lambda e: e.reciprocal(out=invc[:, :, :], in_=invc[:, :, :]), reads=[const_b], writes=[const_b])
            for g in range(4):
                P.add(POOL, lambda e, g=g: e.memset(upool[:, g, 0:HIST_U], 0.0), writes=[up_b[g]])
                P.add(POOL, lambda e, g=g: e.memset(vb[:, g, 0:HIST_V], 0.0), writes=[vb_b[g]])

            converted = set()
            pend_cast = []

            def stage_ap(s):
                return xb[1][:, s * 4:s * 4 + 4, :]

            def stage_bufs(s):
                return xb_b[1][s * 4:s * 4 + 4]

            def emit_stage_load(blk):
                s = rot("stage", 2)
                src = wimg[:, blk * SLOTW:(blk + 1) * SLOTW].rearrange("p (a b) -> p a b", b=T)
                P.add(SP, lambda e, s=s, src=src: e.dma_start(out=stage_ap(s), in_=src),
                      writes=stage_bufs(s), dma_sem=stage_ld[s])
                return s

            def next_unconverted_weight(from_seq):
                for q in range(from_seq, len(ws.order)):
                    b_ = ws.order[q]
                    if b_ < NB_W and b_ not in converted and all(b_ != pb for pb, _ in pend_cast):
                        return b_
                return None

            def emit_load(sq_n):
                blk = ws.order[sq_n]
                slot = sq_n % NSLOT
                if blk in converted:
                    ncols = blk_ncols(blk)
                    P.add(SP, lambda e, slot=slot, blk=blk, ncols=ncols: e.dma_start(
                        out=ring[slot][:, 0:ncols], in_=wscr[:, blk * SLOTW:blk * SLOTW + ncols]),
                        reads=[scr_b[blk]], writes=[ring_b[slot]], dma_sem=ring_ld[slot])
                    return
                converted.add(blk)
                if True:
                    if pend_cast and pend_cast[0][0] == blk:
                        _, s = pend_cast.pop(0)
                    else:
                        s = emit_stage_load(blk)
                    if not pend_cast:
                        nb = next_unconverted_weight(sq_n + 1)
                        if nb is not None:
                            pend_cast.append((nb, emit_stage_load(nb)))
                    dst = ring[slot][:, :].rearrange("p (a b) -> p a b", b=T)
                    if rot("cast", 3) == 2:
                        P.add(ACT, lambda e, s=s, dst=dst: e.activation(out=dst, in_=stage_ap(s), func=AF.Copy),
                              reads=stage_bufs(s), writes=[ring_b[slot]])
                    else:
                        P.add(DVE, lambda e, s=s, dst=dst: e.tensor_copy(out=dst, in_=stage_ap(s)),
                              reads=stage_bufs(s), writes=[ring_b[slot]])
                P.add(SP, lambda e, slot=slot, blk=blk: e.dma_start(out=wscr[:, blk * SLOTW:(blk + 1) * SLOTW], in_=ring[slot][:, :]),
                      reads=[ring_b[slot]], writes=[scr_b[blk]], dma_sem=ring_st[slot])

            def preconvert(blks):
                pend_cast.clear()
                for n, blk in enumerate(blks):
                    s_ = emit_stage_load(blk)
                    r = n % 2
                    bb = f_b[r * 4:(r + 1) * 4]
                    dst = fbuf[:, r * 4:(r + 1) * 4, :]
                    P.add(ACT, lambda e, s_=s_, dst=dst: e.activation(out=dst, in_=stage_ap(s_), func=AF.Copy),
                          reads=stage_bufs(s_), writes=bb)
                    P.add(SP, lambda e, blk=blk, dst=dst: e.dma_start(
                        out=wscr[:, blk * SLOTW:(blk + 1) * SLOTW].rearrange("p (a b) -> p a b", b=T), in_=dst),
                        reads=bb, writes=[scr_b[blk]], dma_sem=pc_st[r])
                    converted.add(blk)

            ws.emit_load = emit_load

            def blockref(blk):
                s = ws.acquire(blk)
                return ring[s], ring_b[s]

            def load_x(tile):
                b = tile % 2
                P.add(SP, lambda e, b=b, tile=tile: e.dma_start(out=xb[b][:, :, :], in_=xT3[:, :, tile * T:(tile + 1) * T]),
                      writes=xb_b[b], dma_sem=x_ld[b])

            def store_x(tile):
                b = tile % 2
                P.add(SP, lambda e, b=b, tile=tile: e.dma_start(out=outT3[:, :, tile * T:(tile + 1) * T], in_=xb[b][:, :, :]),
                      reads=xb_b[b], dma_sem=x_st[b])

            def rms_rstd(ps_t, ps_b):
                i1 = rot("st", NST)
                P.add(ACT, lambda e, i1=i1, ps_t=ps_t: e.activation(out=stt[:, i1, :], in_=ps_t[:, :], func=AF.Ln, bias=epsT[:, 0:1]),
                      reads=[ps_b, const_b], writes=[st_b[i1]])
                i2 = rot("st", NST)
                P.add(ACT, lambda e, i1=i1, i2=i2: e.activation(out=stt[:, i2, :], in_=stt[:, i1, :], func=AF.Exp, scale=-0.5),
                      reads=[st_b[i1]], writes=[st_b[i2]])
                return i2

            def norm_stats(b):
                ps_t, ps_b = rms_bank()
                for kc in range(NKC):
                    i = rot("sq", NSQ)
                    P.add(ACT, lambda e, i=i, kc=kc: e.activation(out=sq[:, i, :], in_=xb[b][:, kc, :], func=AF.Square),
                          reads=[xb_b[b][kc]], writes=[sq_b[i]])
                    P.add(PE, lambda e, i=i, kc=kc, ps_t=ps_t: e.matmul(ps_t[:, :], onesD[:, :], sq[:, i, :], start=(kc == 0), stop=(kc == NKC - 1)),
                          reads=[sq_b[i], const_b], writes=[ps_b])
                return rms_rstd(ps_t, ps_b)

            def norm_apply(b, ir, gcol, ht, hb, bg=None):
                for kc in range(NKC):
                    if bg is not None:
                        bg.append(lambda kc=kc: norm_chunk(b, ir, gcol, ht, hb, kc))
                    else:
                        norm_chunk(b, ir, gcol, ht, hb, kc)

            def norm_chunk(b, ir, gcol, ht, hb, kc):
                if True:
                    P.add(DVE, lambda e, kc=kc: e.scalar_tensor_tensor(
                        out=ht[:, kc, :], in0=xb[b][:, kc, :], scalar=small[:, gcol + kc:gcol + kc + 1], in1=stt[:, ir, :],
                        op0=ALU.mult, op1=ALU.mult),
                        reads=[xb_b[b][kc], st_b[ir], const_b], writes=[hb[kc]])

            def proj_group(slot_t, slot_b, col0, nk, rhs_fn, rhs_bufs):
                ps_t, ps_b = new_bank()
                for kc in range(nk):
                    P.add(PE, lambda e, kc=kc, ps_t=ps_t: e.matmul(
                        ps_t[:, :], slot_t[:, col0 + kc * 128:col0 + (kc + 1) * 128], rhs_fn(kc), start=(kc == 0), stop=(kc == nk - 1)),
                        reads=[slot_b, rhs_bufs[kc]], writes=[ps_b])
                return ps_t, ps_b

            def drain(bg, n=1):
                for _ in range(n):
                    if bg:
                        bg.pop(0)()

            def post_groups(yps, bg=None):
                st_t, st_bk = rms_bank()
                pend = None
                for jo in range(NKC):
                    if bg is not None and jo >= 1:
                        drain(bg, 1)
                    ps_t, ps_b = yps(jo)
                    i = rot("sq", NSQ)
                    P.add(ACT, lambda e, jo=jo, ps_t=ps_t: e.activation(out=ybuf[:, jo, :], in_=ps_t[:, :], func=AF.Copy),
                          reads=[ps_b], writes=[y_b[jo]])
                    P.add(ACT, lambda e, i=i, jo=jo: e.activation(out=sq[:, i, :], in_=ybuf[:, jo, :], func=AF.Square),
                          reads=[y_b[jo]], writes=[sq_b[i]])
                    if pend is not None:
                        pi, pj = pend
                        P.add(PE, lambda e, pi=pi, pj=pj: e.matmul(st_t[:, :], onesD[:, :], sq[:, pi, :], start=(pj == 0), stop=False),
                              reads=[sq_b[pi], const_b], writes=[st_bk])
                    pend = (i, jo)
                pi, pj = pend
                P.add(PE, lambda e, pi=pi, pj=pj: e.matmul(st_t[:, :], onesD[:, :], sq[:, pi, :], start=False, stop=True),
                      reads=[sq_b[pi], const_b], writes=[st_bk])
                return rms_rstd(st_t, st_bk)

            def post_chain(b, ir, gcol, bg=None):
                for jo in range(NKC):
                    if bg is not None:
                        bg.append(lambda jo=jo: post_chunk(b, ir, gcol, jo))
                    else:
                        post_chunk(b, ir, gcol, jo)

            def post_chunk(b, ir, gcol, jo):
                if True:
                    P.add(DVE, lambda e, jo=jo: e.scalar_tensor_tensor(
                        out=ybuf[:, jo, :], in0=ybuf[:, jo, :], scalar=small[:, gcol + jo:gcol + jo + 1], in1=stt[:, ir, :],
                        op0=ALU.mult, op1=ALU.mult),
                        reads=[y_b[jo], st_b[ir], const_b], writes=[y_b[jo]])
                    P.add(POOL, lambda e, jo=jo: e.tensor_tensor(out=xb[b][:, jo, :], in0=xb[b][:, jo, :], in1=ybuf[:, jo, :], op=ALU.add),
                          reads=[xb_b[b][jo], y_b[jo]], writes=[xb_b[b][jo]])

            hrhs = lambda kc: h[:, kc, :]
            h2rhs = lambda kc: h2[:, kc, :]

            def fe_proj(tile):
                slots_in = [blockref(B_IN + i) for i in range(6)]
                for j in range(4):
                    st_, sb_ = slots_in[j // 2]
                    ps_t, ps_b = proj_group(st_, sb_, (j % 2) * 1024, NKC, hrhs, h_b)
                    P.add(ACT, lambda e, j=j, ps_t=ps_t: e.activation(out=upool[:, j, HIST_U:HIST_U + T], in_=ps_t[:, :], func=AF.Copy),
                          reads=[ps_b], writes=[up_b[j]])
                ws.release(2)
                for jj in range(4):
                    jg, ja = 8 + jj, 4 + jj
                    st_, sb_ = slots_in[jg // 2]
                    psg_t, psg_b = proj_group(st_, sb_, (jg % 2) * 1024, NKC, hrhs, h_b)
                    st_, sb_ = slots_in[ja // 2]
                    psa_t, psa_b = proj_group(st_, sb_, (ja % 2) * 1024, NKC, hrhs, h_b)
                    i = rot("sg", 2)
                    P.add(ACT, lambda e, i=i, psg_t=psg_t: e.activation(out=sg[:, i, :], in_=psg_t[:, :], func=AF.Sigmoid),
                          reads=[psg_b], writes=[sg_b[i]])
                    P.add(DVE, lambda e, i=i, jj=jj, psa_t=psa_t: e.tensor_tensor(
                        out=vb[:, jj, HIST_V:HIST_V + T], in0=psa_t[:, :], in1=sg[:, i, :], op=ALU.mult),
                        reads=[psa_b, sg_b[i]], writes=[vb_b[jj]])
                ws.release(4)
                L = HIST_U + T
                for g in range(4):
                    cur = None
                    for s_ in range(g + 1):
                        sh = 1 << s_
                        lo = (1 << (s_ + 1)) - 1
                        dst_i = s_ % 2
                        if cur is None:
                            in_a = lambda e_lo, e_hi, g=g: upool[:, g, e_lo:e_hi]
                            in_buf = up_b[g]
                        else:
                            in_a = lambda e_lo, e_hi, ci=cur: ptmp[:, ci, e_lo:e_hi]
                            in_buf = pt_b[cur]
                        P.add(POOL, lambda e, in_a=in_a, dst_i=dst_i, lo=lo, sh=sh: e.tensor_tensor(
                            out=ptmp[:, dst_i, lo:L], in0=in_a(lo, L), in1=in_a(lo - sh, L - sh), op=ALU.add),
                            reads=[in_buf], writes=[pt_b[dst_i]])
                        cur = dst_i
                    wdw = float(2 ** (g + 1))
                    P.add(DVE, lambda e, g=g, cur=cur, wdw=wdw: e.scalar_tensor_tensor(
                        out=pooled[:, g, :], in0=ptmp[:, cur, HIST_U:L], scalar=1.0 / wdw, in1=upool[:, g, HIST_U:L],
                        op0=ALU.mult, op1=ALU.subtract),
                        reads=[pt_b[cur], up_b[g]], writes=[pl_b[g]])
                    if tile == 0:
                        P.add(POOL, lambda e, g=g, cur=cur: e.tensor_tensor(
                            out=ptmp[:, cur, HIST_U:2 * HIST_U], in0=ptmp[:, cur, HIST_U:2 * HIST_U], in1=invc[:, g, :], op=ALU.mult),
                            reads=[pt_b[cur], const_b], writes=[pt_b[cur]])
                        P.add(POOL, lambda e, g=g, cur=cur: e.tensor_tensor(
                            out=pooled[:, g, 0:HIST_U], in0=ptmp[:, cur, HIST_U:2 * HIST_U], in1=upool[:, g, HIST_U:2 * HIST_U], op=ALU.subtract),
                            reads=[pt_b[cur], up_b[g], pl_b[g]], writes=[pl_b[g]])
                    P.add(POOL, lambda e, g=g: e.tensor_copy(out=upool[:, g, 0:HIST_U], in_=upool[:, g, T:T + HIST_U]),
                          reads=[up_b[g]], writes=[up_b[g]])

            def fe_conv(tile):
                mu_t, mu_b = stat_bank(0)
                e2_t, e2_b = stat_bank(1)
                pm = []

                def t_pmap_all():
                    pw_t, pw_b = blockref(B_PW)
                    for g in range(4):
                        ps_t, ps_b = new_bank()
                        P.add(PE, lambda e, g=g, ps_t=ps_t: e.matmul(ps_t[:, :], pw_t[:, g * 128:(g + 1) * 128], pooled[:, g, :], start=True, stop=True),
                              reads=[pw_b, pl_b[g]], writes=[ps_b])
                        P.add(ACT, lambda e, g=g, ps_t=ps_t: e.activation(out=mixed[:, g, :], in_=ps_t[:, :], func=AF.Copy,
                                                                             scale=small[:, C_PSC + g:C_PSC + g + 1]),
                              reads=[ps_b, const_b], writes=[mx_b[g]])
                    ws.release(1)
                pm.append(t_pmap_all)

                th = []
                idxs = {}

                def t_tap(c, k):
                    src = vb[:, c, 2 + k:2 + k + T]
                    wcol = small[:, C_CW + c * CONV_K + k:C_CW + c * CONV_K + k + 1]
                    if k == 0:
                        P.add(DVE, lambda e, c=c: e.tensor_scalar(out=cc[:, c, :], in0=src, scalar1=wcol,
                                                                  scalar2=small[:, C_CB + c:C_CB + c + 1], op0=ALU.mult, op1=ALU.add),
                              reads=[vb_b[c], const_b], writes=[cc_b[c]])
                    else:
                        P.add(DVE, lambda e, c=c: e.scalar_tensor_tensor(out=cc[:, c, :], in0=src, scalar=wcol, in1=cc[:, c, :],
                                                                         op0=ALU.mult, op1=ALU.add),
                              reads=[vb_b[c], cc_b[c], const_b], writes=[cc_b[c]])
                    if k == CONV_K - 1:
                        P.add(POOL, lambda e, c=c: e.tensor_copy(out=vb[:, c, 0:HIST_V], in_=vb[:, c, T:T + HIST_V]),
                              reads=[vb_b[c]], writes=[vb_b[c]])
                for k in range(CONV_K):
                    for c in range(4):
                        th.append(lambda c=c, k=k: t_tap(c, k))
                taps = th
                th = list(pm)

                def t_cast(c):
                    i = rot("cb", 2)
                    idxs[c] = i
                    P.add(ACT, lambda e, c=c, i=i: e.activation(out=csq[:, i, :], in_=cc[:, c, :], func=AF.Square),
                          reads=[cc_b[c]], writes=[csq_b[i]])
                    P.add(DVE, lambda e, c=c, i=i: e.tensor_copy(out=cbf[:, i, :], in_=cc[:, c, :]),
                          reads=[cc_b[c]], writes=[cbf_b[i]])

                def t_stat(c):
                    i = idxs[c]
                    P.add(PE, lambda e, c=c, i=i: e.matmul(mu_t[:, :], onesC[:, :], cbf[:, i, :], start=(c == 0), stop=(c == 3)),
                          reads=[cbf_b[i], const_b], writes=[mu_b])
                    P.add(PE, lambda e, c=c, i=i: e.matmul(e2_t[:, :], onesC[:, :], csq[:, i, :], start=(c == 0), stop=(c == 3)),
                          reads=[csq_b[i], const_b], writes=[e2_b])
                for c in range(4):
                    th.append(lambda c=c: t_cast(c))
                    if c >= 1:
                        th.append(lambda c=c: t_stat(c - 1))
                th.append(lambda: t_stat(3))
                th.append(lambda: P.add(ACT, lambda e: e.activation(out=lnt[:, 0, :], in_=mu_t[:, :], func=AF.Copy), reads=[mu_b], writes=[ln_b[0]]))
                th.append(lambda: P.add(POOL, lambda e: e.tensor_tensor(out=lnt[:, 1, :], in0=lnt[:, 0, :], in1=lnt[:, 0, :], op=ALU.mult),
                                        reads=[ln_b[0]], writes=[ln_b[1]]))
                th.append(lambda: P.add(DVE, lambda e: e.scalar_tensor_tensor(out=lnt[:, 1, :], in0=e2_t[:, :], scalar=EPS, in1=lnt[:, 1, :],
                                                                              op0=ALU.add, op1=ALU.subtract),
                                        reads=[e2_b, ln_b[1]], writes=[ln_b[1]]))
                th.append(lambda: P.add(ACT, lambda e: e.activation(out=lnt[:, 1, :], in_=lnt[:, 1, :], func=AF.Ln), reads=[ln_b[1]], writes=[ln_b[1]]))
                th.append(lambda: P.add(ACT, lambda e: e.activation(out=lnt[:, 1, :], in_=lnt[:, 1, :], func=AF.Exp, scale=-0.5), reads=[ln_b[1]], writes=[ln_b[1]]))

                def t_sub(c):
                    P.add(DVE, lambda e, c=c: e.tensor_tensor(out=cc[:, c, :], in0=cc[:, c, :], in1=lnt[:, 0, :], op=ALU.subtract),
                          reads=[cc_b[c], ln_b[0]], writes=[cc_b[c]])

                def t_mul(c):
                    P.add(DVE, lambda e, c=c: e.tensor_tensor(out=cc[:, c, :], in0=cc[:, c, :], in1=lnt[:, 1, :], op=ALU.mult),
                          reads=[cc_b[c], ln_b[1]], writes=[cc_b[c]])

                def t_act(c):
                    i = rot("sg", 2)
                    idxs[("s", c)] = i
                    P.add(ACT, lambda e, c=c, i=i: e.activation(out=sg[:, i, :], in_=cc[:, c, :], func=AF.Sigmoid,
                                                                   bias=small[:, C_LB + c:C_LB + c + 1], scale=small[:, C_LG + c:C_LG + c + 1]),
                          reads=[cc_b[c], const_b], writes=[sg_b[i]])

                def t_aff(c):
                    i = idxs[("s", c)]
                    P.add(DVE, lambda e, c=c: e.tensor_scalar(out=cc[:, c, :], in0=cc[:, c, :], scalar1=small[:, C_LG + c:C_LG + c + 1],
                                                              scalar2=small[:, C_LB + c:C_LB + c + 1], op0=ALU.mult, op1=ALU.add),
                          reads=[cc_b[c], const_b, sg_b[i]], writes=[cc_b[c]])

                def t_out(c):
                    i = idxs[("s", c)]
                    P.add(POOL, lambda e, c=c, i=i: e.tensor_tensor(out=convact[:, c, :], in0=cc[:, c, :], in1=sg[:, i, :], op=ALU.mult),
                          reads=[cc_b[c], sg_b[i]], writes=[ca_b[c]])
                for c in range(4):
                    th.append(lambda c=c: t_sub(c))
                for c in range(4):
                    th.append(lambda c=c: t_mul(c))
                for c0 in (0, 2):
                    for c in (c0, c0 + 1):
                        th.append(lambda c=c: t_act(c))
                    for c in (c0, c0 + 1):
                        th.append(lambda c=c: t_aff(c))
                    for c in (c0, c0 + 1):
                        th.append(lambda c=c: t_out(c))
                return taps, th

            def mix_E(tile, jps, bg=None, nd=2):
                for jp in jps:
                    ga = blockref(B_IN + 6 + jp)
                    po = blockref(B_PO + jp)
                    gb = blockref(B_IN + 10 + jp)
                    co = blockref(B_CO + jp)
                    for jj in range(2):
                        j = jp * 2 + jj
                        pga_t, pga_b = proj_group(ga[0], ga[1], jj * 1024, NKC, hrhs, h_b)
                        pA_t, pA_b = proj_group(po[0], po[1], jj * 512, 4, lambda kc: mixed[:, kc, :], mx_b)
                        pgb_t, pgb_b = proj_group(gb[0], gb[1], jj * 1024, NKC, hrhs, h_b)
                        pB_t, pB_b = proj_group(co[0], co[1], jj * 512, 4, lambda kc: convact[:, kc, :], ca_b)
                        ia = rot("mt", NMT)
                        ib = rot("mt", NMT)
                        P.add(ACT, lambda e, ia=ia, pga_t=pga_t: e.activation(out=mtmp[:, ia, :], in_=pga_t[:, :], func=AF.Sigmoid),
                              reads=[pga_b], writes=[mt_b[ia]])
                        P.add(ACT, lambda e, ib=ib, pgb_t=pgb_t: e.activation(out=mtmp[:, ib, :], in_=pgb_t[:, :], func=AF.Sigmoid),
                              reads=[pgb_b], writes=[mt_b[ib]])
                        P.add(DVE, lambda e, ia=ia, pA_t=pA_t: e.tensor_tensor(out=mtmp[:, ia, :], in0=pA_t[:, :], in1=mtmp[:, ia, :], op=ALU.mult),
                              reads=[pA_b, mt_b[ia]], writes=[mt_b[ia]])
                        P.add(DVE, lambda e, ib=ib, pB_t=pB_t: e.tensor_tensor(out=mtmp[:, ib, :], in0=pB_t[:, :], in1=mtmp[:, ib, :], op=ALU.mult),
                              reads=[pB_b, mt_b[ib]], writes=[mt_b[ib]])
                        P.add(POOL, lambda e, ia=ia, ib=ib, j=j: e.tensor_tensor(out=merged_ap(j), in0=mtmp[:, ia, :], in1=mtmp[:, ib, :], op=ALU.add),
                              reads=[mt_b[ia], mt_b[ib]], writes=[mg_b[j]])
                        if bg is not None:
                            drain(bg, nd)
                    ws.release(4)

            def mix_wo(tile, bg=None):
                wo_slots = [blockref(B_WO + i) for i in range(4)]

                def y_wo(jo):
                    st_, sb_ = wo_slots[jo // 2]
                    r_ = proj_group(st_, sb_, (jo % 2) * 1024, NKC, merged_ap, mg_b)
                    if jo % 2 == 1:
                        ws.release(1)
                    return r_
                return post_groups(y_wo, bg)

            def mlp_up(tile, blks, bg=None):
                for blk in blks:
                    st_, sb_ = blockref(B_UP + blk)
                    for jj in range(2):
                        if bg is not None:
                            drain(bg, 4)
                        jf = blk * 2 + jj
                        ps_t, ps_b = proj_group(st_, sb_, jj * 1024, NKC, h2rhs, h2_b)
                        i = rot("mt", NMT)
                        P.add(ACT, lambda e, i=i, ps_t=ps_t: e.activation(out=mtmp[:, i, :], in_=ps_t[:, :], func=AF.Relu),
                              reads=[ps_b], writes=[mt_b[i]])
                        P.add(ACT, lambda e, i=i, jf=jf: e.activation(out=fbuf[:, jf, :], in_=mtmp[:, i, :], func=AF.Square),
                              reads=[mt_b[i]], writes=[f_b[jf]])
                    ws.release(1)

            def mlp_down(tile, bg=None):
                def y_dn(jo):
                    ps_t, ps_b = new_bank()
                    for half in range(2):
                        if bg is not None:
                            drain(bg[0], 8)
                            if jo * 2 + half >= 5:
                                drain(bg[1], 4)
                        st_, sb_ = blockref(B_DN + jo * 2 + half)
                        for kk in range(16):
                            kc = half * 16 + kk
                            P.add(PE, lambda e, kk=kk, kc=kc, st_=st_, ps_t=ps_t: e.matmul(
                                ps_t[:, :], st_[:, kk * 128:(kk + 1) * 128], fbuf[:, kc, :], start=(kc == 0), stop=(kc == NFC - 1)),
                                reads=[sb_, f_b[kc]], writes=[ps_b])
                        ws.release(1)
                    return ps_t, ps_b
                return post_groups(y_dn)

            def flush(bg):
                while bg:
                    bg.pop(0)()

            ws.start()
            load_x(0)
            ir = norm_stats(0)
            norm_apply(0, ir, C_NMP, h, h_b)
            fe_proj(0)
            if n_tiles > 1:
                preconvert([B_UP + i for i in range(16)] + [B_DN + i for i in range(16)])
            taps, lnth = fe_conv(0)
            flush(taps)
            flush(lnth)
            bg_post = []
            for tile in range(n_tiles):
                b = tile % 2
                nxt = tile + 1 if tile + 1 < n_tiles else None
                overlap = nxt is not None
                mix_E(tile, (0, 1), bg_post, 2)
                flush(bg_post)
                if tile >= 1:
                    store_x(tile - 1)
                    if nxt is not None:
                        load_x(nxt)
                mix_E(tile, (2, 3))
                if tile == 0 and nxt is not None:
                    load_x(nxt)
                bg_h = []
                if overlap:
                    irn = norm_stats(nxt % 2)
                    norm_apply(nxt % 2, irn, C_NMP, h, h_b, bg_h)
                ir1 = mix_wo(tile, bg_h)
                flush(bg_h)
                post_chain(b, ir1, C_NMPOST)
                if overlap:
                    fe_proj(nxt)
                ir2 = norm_stats(b)
                norm_apply(b, ir2, C_NLP, h2, h2_b)
                bgs = fe_conv(nxt) if overlap else ([], [])
                mlp_up(tile, range(16), bgs[0])
                ir3 = mlp_down(tile, bgs)
                flush(bgs[0])
                flush(bgs[1])
                bg_post = []
                post_chain(b, ir3, C_NLPOST, bg_post)
            flush(bg_post)
            store_x(n_tiles - 1)
            return None

        rec = WS()
        emit(Prog(), rec)
        ws = WS(order=list(rec.rec))
        P = Prog()
        info = emit(P, ws)
        assert ws.acq == len(ws.order)
        P.finalize()
        final_waits = []
        for bb in range(2):
            n = P.dma_n.get(id(x_st[bb]), 0)
            if n:
                final_waits.append((x_st[bb], 16 * n))

        block = es.enter_context(nc.Block())

        @block.sync
        def _(eng):
            P.run(SP, eng, sems, final_waits)

        @block.tensor
        def _(eng):
            P.run(PE, eng, sems)

        @block.scalar
        def _(eng):
            P.run(ACT, eng, sems)

        @block.vector
        def _(eng):
            P.run(DVE, eng, sems)

        @block.gpsimd
        def _(eng):
            P.run(POOL, eng, sems)

    return nc


def _img(Wm):
    K, N = Wm.shape
    nk, nn = K // 128, N // 128
    return np.ascontiguousarray(Wm.reshape(nk, 128, nn, 128).transpose(1, 2, 0, 3).reshape(128, nn * nk * 128))


def _cols(v, n):
    return np.ascontiguousarray(np.asarray(v, np.float32).reshape(n, 128).T)


def pack_weights(w_in, pool_w, w_pool_out, w_conv_out, w_o, w_up, w_down):
    wimg = np.zeros((128, NB_W * SLOTW), np.float32)
    wimg[:, B_IN * SLOTW:B_IN * SLOTW + 28672] = _img(w_in)
    wimg[:, B_PW * SLOTW:B_PW * SLOTW + 512] = np.asarray(pool_w).transpose(1, 0, 2).reshape(128, 512)
    ipo, ico = _img(w_pool_out), _img(w_conv_out)
    for q in range(4):
        wimg[:, (B_PO + q) * SLOTW:(B_PO + q) * SLOTW + 1024] = ipo[:, q * 1024:(q + 1) * 1024]
        wimg[:, (B_CO + q) * SLOTW:(B_CO + q) * SLOTW + 1024] = ico[:, q * 1024:(q + 1) * 1024]
    wimg[:, B_WO * SLOTW:B_WO * SLOTW + 8192] = _img(w_o)
    wimg[:, B_UP * SLOTW:B_UP * SLOTW + 32768] = _img(w_up)
    wimg[:, B_DN * SLOTW:B_DN * SLOTW + 32768] = _img(w_down)
    return wimg


def pack_small(norm_mix_pre, norm_mix_post, norm_mlp_pre, norm_mlp_post, pool_scale, conv_b, conv_ln_g, conv_ln_b, conv_w):
    small = np.zeros((128, NSMALL), np.float32)
    small[:, C_NMP:C_NMP + 8] = _cols(norm_mix_pre, 8)
    small[:, C_NMPOST:C_NMPOST + 8] = _cols(norm_mix_post, 8)
    small[:, C_NLP:C_NLP + 8] = _cols(norm_mlp_pre, 8)
    small[:, C_NLPOST:C_NLPOST + 8] = _cols(norm_mlp_post, 8)
    small[:, C_PSC:C_PSC + 4] = _cols(pool_scale, 4)
    small[:, C_CB:C_CB + 4] = _cols(conv_b, 4)
    small[:, C_LG:C_LG + 4] = _cols(conv_ln_g, 4)
    small[:, C_LB:C_LB + 4] = _cols(conv_ln_b, 4)
    cw = np.asarray(conv_w, np.float32).reshape(CONV_K, 4, 128)
    small[:, C_CW:C_CW + 4 * CONV_K] = cw.transpose(2, 1, 0).reshape(128, 4 * CONV_K)
    return small


def kernel(x, norm_mix_pre, w_in, pool_w, pool_scale, w_pool_out, conv_w, conv_b,
           conv_ln_g, conv_ln_b, w_conv_out, w_o, norm_mix_post, norm_mlp_pre,
           w_up, w_down, norm_mlp_post):
    x = np.asarray(x, np.float32)
    B = x.shape[0]
    f = lambda a: np.asarray(a, np.float32)[0]
    wimg = pack_weights(f(w_in), f(pool_w), f(w_pool_out), f(w_conv_out), f(w_o), f(w_up), f(w_down))
    small = pack_small(f(norm_mix_pre), f(norm_mix_post), f(norm_mlp_pre), f(norm_mlp_post), f(pool_scale),
                       f(conv_b), f(conv_ln_g), f(conv_ln_b), f(conv_w))
    nc = build_nc(x.shape[1] // T)
    in_maps = [{"xT": np.ascontiguousarray(x[b].T), "wimg": wimg, "small": small} for b in range(B)]
    res = run_bass_kernel_spmd(nc, in_maps, core_ids=list(range(B)))
    out = np.stack([np.ascontiguousarray(res.results[b]["outT"].T) for b in range(B)], axis=0)
    return out.astype(np.float32)
```

```python
import numpy as np
from contextlib import ExitStack
import concourse.bass as bass
import concourse.mybir as mybir
from concourse.bass_utils import run_bass_kernel_spmd

F32 = mybir.dt.float32
BF16 = mybir.dt.bfloat16
I32 = mybir.dt.int32
AF = mybir.ActivationFunctionType
ALU = mybir.AluOpType

D = 1024
SEQ = 4096
T = 512
NKC = 8
DFF = 4096
NFC = 32
EPS = 1e-6
CONV_K = 31
HIST_U = 16
HIST_V = 32
NSLOT = 8
SLOTW = 2048

PE, ACT, DVE, POOL, SP = "pe", "act", "dve", "pool", "sp"
ENGINES = (PE, ACT, DVE, POOL, SP)

C_NMP, C_NMPOST, C_NLP, C_NLPOST = 0, 8, 16, 24
C_PSC, C_CB, C_LG, C_LB, C_CW = 32, 36, 40, 44, 48
NSMALL = C_CW + 4 * CONV_K

B_IN, B_PW, B_PO, B_CO, B_WO, B_UP, B_DN = 0, 14, 15, 19, 23, 27, 43
NB_W = 59
NB_ALL = 59


def blk_ncols(blk):
    if blk == B_PW:
        return 512
    if B_PO <= blk < B_WO:
        return 1024
    return SLOTW


class Buf:
    __slots__ = ("name", "last_w", "readers")

    def __init__(self, name):
        self.name = name
        self.last_w = None
        self.readers = []


class Op:
    __slots__ = ("eng", "fn", "pos", "waits", "signals", "dma_sem", "dma_val", "count")


class Prog:
    def __init__(self):
        self.ops = {e: [] for e in ENGINES}
        self.waited = {e: {} for e in ENGINES}
        self.dma_n = {}

    def add(self, eng, fn, reads=(), writes=(), dma_sem=None):
        op = Op()
        op.eng, op.fn, op.pos = eng, fn, len(self.ops[eng])
        op.waits, op.signals, op.dma_sem, op.dma_val, op.count = [], False, dma_sem, 0, 0
        deps = []
        for b in reads:
            if b.last_w is not None:
                deps.append(b.last_w)
        for b in writes:
            if b.last_w is not None:
                deps.append(b.last_w)
            deps.extend(b.readers)
        w = self.waited[eng]
        best = {}
        dma_deps = []
        for d in deps:
            if d.dma_sem is not None:
                dma_deps.append(d)
            elif d.eng not in best or best[d.eng].pos < d.pos:
                best[d.eng] = d
        deps = dma_deps + list(best.values())
        for d in deps:
            if d.dma_sem is not None:
                key = ("dma", id(d.dma_sem))
                if w.get(key, 0) >= d.dma_val:
                    continue
                w[key] = d.dma_val
                op.waits.append(d)
            else:
                if d.eng == eng and eng in (PE, SP):
                    continue
                if w.get(d.eng, -1) >= d.pos:
                    continue
                w[d.eng] = d.pos
                d.signals = True
                op.waits.append(d)
        if dma_sem is not None:
            n = self.dma_n.get(id(dma_sem), 0) + 1
            self.dma_n[id(dma_sem)] = n
            op.dma_val = 16 * n
        for b in reads:
            b.readers.append(op)
        for b in writes:
            b.last_w = op
            b.readers = []
        self.ops[eng].append(op)
        return op

    def finalize(self):
        for e in ENGINES:
            c = 0
            for op in self.ops[e]:
                if op.signals and op.dma_sem is None:
                    c += 1
                op.count = c

    def run(self, eng_name, eng, sems, final_waits=()):
        for op in self.ops[eng_name]:
            for d in op.waits:
                if d.dma_sem is not None:
                    eng.wait_ge(d.dma_sem, d.dma_val)
                else:
                    eng.wait_ge(sems[d.eng], d.count)
            ins = op.fn(eng)
            if op.dma_sem is not None:
                ins.then_inc(op.dma_sem, 16)
            elif op.signals:
                ins.then_inc(sems[eng_name], 1)
        for sem, val in final_waits:
            eng.wait_ge(sem, val)


class WS:
    def __init__(self, order=None):
        self.order = order
        self.rec = []
        self.next_load = 0
        self.acq = 0
        self.emit_load = None

    def acquire(self, blk):
        if self.order is None:
            self.rec.append(blk)
            return (len(self.rec) - 1) % NSLOT
        sq_n = self.acq
        assert self.order[sq_n] == blk, (sq_n, self.order[sq_n], blk)
        assert sq_n < self.next_load, (sq_n, self.next_load)
        self.acq += 1
        return sq_n % NSLOT

    def start(self):
        if self.order is None:
            return
        while self.next_load < min(NSLOT, len(self.order)):
            self.emit_load(self.next_load)
            self.next_load += 1

    def release(self, n=1):
        if self.order is None:
            return
        for _ in range(n):
            if self.next_load < len(self.order):
                self.emit_load(self.next_load)
                self.next_load += 1


def build_nc(n_tiles=SEQ // T):
    seq = n_tiles * T
    nc = bass.Bass("TRN2", target_bir_lowering=False)
    xT = nc.dram_tensor("xT", [D, seq], F32, kind="ExternalInput").ap()
    wimg = nc.dram_tensor("wimg", [128, NB_W * SLOTW], F32, kind="ExternalInput").ap()
    smalld = nc.dram_tensor("small", [128, NSMALL], F32, kind="ExternalInput").ap()
    outT = nc.dram_tensor("outT", [D, seq], F32, kind="ExternalOutput").ap()
    wscr = nc.dram_tensor("wscr", [128, NB_ALL * SLOTW], BF16, kind="Internal").ap()
    xT3 = xT.rearrange("(kc p) t -> p kc t", p=128)
    outT3 = outT.rearrange("(kc p) t -> p kc t", p=128)

    with ExitStack() as es:
        def sb(name, shape, dt):
            return es.enter_context(nc.sbuf_tensor(name, shape, dt))

        def sem(name):
            return es.enter_context(nc.semaphore(name))

        xb = [sb(f"xb{i}", [128, NKC, T], F32) for i in range(2)]
        h = sb("h", [128, NKC, T], BF16)
        h2 = sb("h2", [128, NKC, T], BF16)
        NSQ = 4
        sq = sb("sq", [128, NSQ, T], BF16)
        ybuf = sb("ybuf", [128, NKC, T], F32)
        NST = 4
        stt = sb("stt", [128, NST, T], F32)
        onesD = sb("onesD", [128, 128], BF16)
        onesC = sb("onesC", [128, 128], BF16)
        small = sb("smallsb", [128, NSMALL], F32)
        epsT = sb("epsT", [128, 1], F32)
        invci = sb("invci", [128, HIST_U], I32)
        invc = sb("invc", [128, 4, HIST_U], F32)
        fix16 = sb("fix16", [128, 4, HIST_U], F32)
        ring = [sb(f"ring{i}", [128, SLOTW], BF16) for i in range(NSLOT)]
        upool = sb("upool", [128, 4, HIST_U + T], F32)
        ptmp = sb("ptmp", [128, 2, HIST_U + T], F32)
        pooled = sb("pooled", [128, 4, T], BF16)
        mixed = sb("mixed", [128, 4, T], BF16)
        vb = sb("vb", [128, 4, HIST_V + T], BF16)
        sg = sb("sg", [128, 2, T], F32)
        cc = sb("cc", [128, 4, T], F32)
        cbf = sb("cbf", [128, 2, T], BF16)
        csq = sb("csq", [128, 2, T], BF16)
        lnt = sb("lnt", [128, 2, T], F32)
        convact = sb("convact", [128, 4, T], BF16)
        NMT = 4
        mtmp = sb("mtmp", [128, NMT, T], F32)
        fbuf = sb("fbuf", [128, NFC, T], BF16)
        MG0 = NFC - NKC

        banks = [es.enter_context(nc.psum_tensor(f"ps{i}", [128, T], F32)) for i in range(8)]

        sems = {e: sem(f"s_{e}") for e in (PE, ACT, DVE, POOL)}
        ring_ld = [sem(f"rl{i}") for i in range(NSLOT)]
        ring_st = [sem(f"rs{i}") for i in range(NSLOT)]
        stage_ld = [sem(f"sl{i}") for i in range(2)]
        x_ld = [sem(f"xl{i}") for i in range(2)]
        x_st = [sem(f"xs{i}") for i in range(2)]
        small_ld = sem("smld")
        pc_st = [sem(f"pc{i}") for i in range(2)]

        def emit(P, ws):
            xb_b = [[Buf(f"xb{i}_{k}") for k in range(NKC)] for i in range(2)]
            h_b = [Buf(f"h{k}") for k in range(NKC)]
            h2_b = [Buf(f"h2{k}") for k in range(NKC)]
            sq_b = [Buf(f"sq{k}") for k in range(NSQ)]
            y_b = [Buf(f"y{k}") for k in range(NKC)]
            st_b = [Buf(f"st{k}") for k in range(NST)]
            const_b = Buf("consts")
            fix_b = Buf("fix16")
            ring_b = [Buf(f"ring{i}") for i in range(NSLOT)]
            up_b = [Buf(f"up{g}") for g in range(4)]
            pt_b = [Buf("ptA"), Buf("ptB")]
            pl_b = [Buf(f"pl{g}") for g in range(4)]
            mx_b = [Buf(f"mx{g}") for g in range(4)]
            vb_b = [Buf(f"vb{c}") for c in range(4)]
            sg_b = [Buf("sg0"), Buf("sg1")]
            cc_b = [Buf(f"cc{c}") for c in range(4)]
            cbf_b = [Buf("cbf0"), Buf("cbf1")]
            csq_b = [Buf("csq0"), Buf("csq1")]
            ln_b = [Buf("ln0"), Buf("ln1")]
            ca_b = [Buf(f"ca{c}") for c in range(4)]
            mt_b = [Buf(f"mt{k}") for k in range(NMT)]
            f_b = [Buf(f"f{k}") for k in range(NFC)]
            bank_b = [Buf(f"ps{i}") for i in range(8)]
            scr_b = [Buf(f"scr{i}") for i in range(NB_ALL)]
            bank_i = [0]
            rr = {}

            def rot(key, n):
                i = rr.get(key, 0)
                rr[key] = i + 1
                return i % n

            def new_bank():
                i = bank_i[0] % 5
                bank_i[0] += 1
                return banks[i], bank_b[i]

            def stat_bank(i):
                return banks[6 + i], bank_b[6 + i]

            def rms_bank():
                return banks[5], bank_b[5]

            merged_ap = lambda kc: fbuf[:, MG0 + kc, :]
            mg_b = f_b[MG0:]

            P.add(SP, lambda e: e.dma_start(out=small[:, :], in_=smalld[:, :]), writes=[const_b], dma_sem=small_ld)
            P.add(POOL, lambda e: e.memset(onesD[:, :], 1.0 / D), writes=[const_b])
            P.add(POOL, lambda e: e.memset(onesC[:, :], 1.0 / 512.0), writes=[const_b])
            P.add(POOL, lambda e: e.memset(epsT[:, :], EPS), writes=[const_b])
            P.add(POOL, lambda e: e.iota(invci[:, :], pattern=[[1, HIST_U]], base=1, channel_multiplier=0), writes=[const_b])
            for g in range(4):
                wdw = float(2 ** (g + 1))
                P.add(DVE, lambda e, g=g, wdw=wdw: e.tensor_single_scalar(out=invc[:, g, :], in_=invci[:, :], scalar=wdw, op=ALU.min),
                      reads=[const_b], writes=[const_b])
            P.add(DVE, lambda e: e.reciprocal(out=invc[:, :, :], in_=invc[:, :, :]), reads=[const_b], writes=[const_b])
            for g in range(4):
                P.add(POOL, lambda e, g=g: e.memset(upool[:, g, 0:HIST_U], 0.0), writes=[up_b[g]])
                P.add(POOL, lambda e, g=g: e.memset(vb[:, g, 0:HIST_V], 0.0), writes=[vb_b[g]])

            converted = set()
            pend_cast = []

            def stage_ap(s):
                return xb[1][:, s * 4:s * 4 + 4, :]

            def stage_bufs(s):
                return xb_b[1][s * 4:s * 4 + 4]

            def emit_stage_load(blk):
                s = rot("stage", 2)
                src = wimg[:, blk * SLOTW:(blk + 1) * SLOTW].rearrange("p (a b) -> p a b", b=T)
                P.add(SP, lambda e, s=s, src=src: e.dma_start(out=stage_ap(s), in_=src),
                      writes=stage_bufs(s), dma_sem=stage_ld[s])
                return s

            def next_unconverted_weight(from_seq):
                for q in range(from_seq, len(ws.order)):
                    b_ = ws.order[q]
                    if b_ < NB_W and b_ not in converted and all(b_ != pb for pb, _ in pend_cast):
                        return b_
                return None

            def emit_load(sq_n):
                blk = ws.order[sq_n]
                slot = sq_n % NSLOT
                if blk in converted:
                    ncols = blk_ncols(blk)
                    P.add(SP, lambda e, slot=slot, blk=blk, ncols=ncols: e.dma_start(
                        out=ring[slot][:, 0:ncols], in_=wscr[:, blk * SLOTW:blk * SLOTW + ncols]),
                        reads=[scr_b[blk]], writes=[ring_b[slot]], dma_sem=ring_ld[slot])
                    return
                converted.add(blk)
                if True:
                    if pend_cast and pend_cast[0][0] == blk:
                        _, s = pend_cast.pop(0)
                    else:
                        s = emit_stage_load(blk)
                    if not pend_cast:
                        nb = next_unconverted_weight(sq_n + 1)
                        if nb is not None:
                            pend_cast.append((nb, emit_stage_load(nb)))
                    dst = ring[slot][:, :].rearrange("p (a b) -> p a b", b=T)
                    if rot("cast", 3) == 2:
                        P.add(ACT, lambda e, s=s, dst=dst: e.activation(out=dst, in_=stage_ap(s), func=AF.Copy),
                              reads=stage_bufs(s), writes=[ring_b[slot]])
                    else:
                        P.add(DVE, lambda e, s=s, dst=dst: e.tensor_copy(out=dst, in_=stage_ap(s)),
                              reads=stage_bufs(s), writes=[ring_b[slot]])
                P.add(SP, lambda e, slot=slot, blk=blk: e.dma_start(out=wscr[:, blk * SLOTW:(blk + 1) * SLOTW], in_=ring[slot][:, :]),
                      reads=[ring_b[slot]], writes=[scr_b[blk]], dma_sem=ring_st[slot])

            def preconvert(blks):
                pend_cast.clear()
                st_of = {}
                for n in range(min(2, len(blks))):
                    st_of[n] = emit_stage_load(blks[n])
                for n, blk in enumerate(blks):
                    s_ = st_of[n]
                    r = n % 2
                    bb = f_b[r * 4:(r + 1) * 4]
                    dst = fbuf[:, r * 4:(r + 1) * 4, :]
                    P.add(ACT, lambda e, s_=s_, dst=dst: e.activation(out=dst, in_=stage_ap(s_), func=AF.Copy),
                          reads=stage_bufs(s_), writes=bb)
                    P.add(SP, lambda e, blk=blk, dst=dst: e.dma_start(
                        out=wscr[:, blk * SLOTW:(blk + 1) * SLOTW].rearrange("p (a b) -> p a b", b=T), in_=dst),
                        reads=bb, writes=[scr_b[blk]], dma_sem=pc_st[r])
                    converted.add(blk)
                    if n + 2 < len(blks):
                        st_of[n + 2] = emit_stage_load(blks[n + 2])

            ws.emit_load = emit_load

            def blockref(blk):
                s = ws.acquire(blk)
                return ring[s], ring_b[s]

            def load_x(tile):
                b = tile % 2
                P.add(SP, lambda e, b=b, tile=tile: e.dma_start(out=xb[b][:, :, :], in_=xT3[:, :, tile * T:(tile + 1) * T]),
                      writes=xb_b[b], dma_sem=x_ld[b])

            def store_x(tile):
                b = tile % 2
                P.add(SP, lambda e, b=b, tile=tile: e.dma_start(out=outT3[:, :, tile * T:(tile + 1) * T], in_=xb[b][:, :, :]),
                      reads=xb_b[b], dma_sem=x_st[b])

            def rms_rstd(ps_t, ps_b):
                i1 = rot("st", NST)
                P.add(ACT, lambda e, i1=i1, ps_t=ps_t: e.activation(out=stt[:, i1, :], in_=ps_t[:, :], func=AF.Ln, bias=epsT[:, 0:1]),
                      reads=[ps_b, const_b], writes=[st_b[i1]])
                i2 = rot("st", NST)
                P.add(ACT, lambda e, i1=i1, i2=i2: e.activation(out=stt[:, i2, :], in_=stt[:, i1, :], func=AF.Exp, scale=-0.5),
                      reads=[st_b[i1]], writes=[st_b[i2]])
                return i2

            def norm_stats(b):
                ps_t, ps_b = rms_bank()
                for kc in range(NKC):
                    i = rot("sq", NSQ)
                    P.add(ACT, lambda e, i=i, kc=kc: e.activation(out=sq[:, i, :], in_=xb[b][:, kc, :], func=AF.Square),
                          reads=[xb_b[b][kc]], writes=[sq_b[i]])
                    P.add(PE, lambda e, i=i, kc=kc, ps_t=ps_t: e.matmul(ps_t[:, :], onesD[:, :], sq[:, i, :], start=(kc == 0), stop=(kc == NKC - 1)),
                          reads=[sq_b[i], const_b], writes=[ps_b])
                return rms_rstd(ps_t, ps_b)

            def norm_apply(b, ir, gcol, ht, hb, bg=None):
                for kc in range(NKC):
                    if bg is not None:
                        bg.append(lambda kc=kc: norm_chunk(b, ir, gcol, ht, hb, kc))
                    else:
                        norm_chunk(b, ir, gcol, ht, hb, kc)

            def norm_chunk(b, ir, gcol, ht, hb, kc):
                if True:
                    P.add(DVE, lambda e, kc=kc: e.scalar_tensor_tensor(
                        out=ht[:, kc, :], in0=xb[b][:, kc, :], scalar=small[:, gcol + kc:gcol + kc + 1], in1=stt[:, ir, :],
                        op0=ALU.mult, op1=ALU.mult),
                        reads=[xb_b[b][kc], st_b[ir], const_b], writes=[hb[kc]])

            def proj_group(slot_t, slot_b, col0, nk, rhs_fn, rhs_bufs):
                ps_t, ps_b = new_bank()
                for kc in range(nk):
                    P.add(PE, lambda e, kc=kc, ps_t=ps_t: e.matmul(
                        ps_t[:, :], slot_t[:, col0 + kc * 128:col0 + (kc + 1) * 128], rhs_fn(kc), start=(kc == 0), stop=(kc == nk - 1)),
                        reads=[slot_b, rhs_bufs[kc]], writes=[ps_b])
                return ps_t, ps_b

            def drain(bg, n=1):
                for _ in range(n):
                    if bg:
                        bg.pop(0)()

            def post_groups(yps, bg=None):
                st_t, st_bk = rms_bank()
                pend = None
                for jo in range(NKC):
                    if bg is not None and jo >= 1:
                        drain(bg, 1)
                    ps_t, ps_b = yps(jo)
                    i = rot("sq", NSQ)
                    P.add(ACT, lambda e, jo=jo, ps_t=ps_t: e.activation(out=ybuf[:, jo, :], in_=ps_t[:, :], func=AF.Copy),
                          reads=[ps_b], writes=[y_b[jo]])
                    P.add(ACT, lambda e, i=i, jo=jo: e.activation(out=sq[:, i, :], in_=ybuf[:, jo, :], func=AF.Square),
                          reads=[y_b[jo]], writes=[sq_b[i]])
                    if pend is not None:
                        pi, pj = pend
                        P.add(PE, lambda e, pi=pi, pj=pj: e.matmul(st_t[:, :], onesD[:, :], sq[:, pi, :], start=(pj == 0), stop=False),
                              reads=[sq_b[pi], const_b], writes=[st_bk])
                    pend = (i, jo)
                pi, pj = pend
                P.add(PE, lambda e, pi=pi, pj=pj: e.matmul(st_t[:, :], onesD[:, :], sq[:, pi, :], start=False, stop=True),
                      reads=[sq_b[pi], const_b], writes=[st_bk])
                return rms_rstd(st_t, st_bk)

            def post_chain(b, ir, gcol, bg=None):
                for jo in range(NKC):
                    if bg is not None:
                        bg.append(lambda jo=jo: post_chunk(b, ir, gcol, jo))
                    else:
                        post_chunk(b, ir, gcol, jo)

            def post_chunk(b, ir, gcol, jo):
                if True:
                    P.add(DVE, lambda e, jo=jo: e.scalar_tensor_tensor(
                        out=ybuf[:, jo, :], in0=ybuf[:, jo, :], scalar=small[:, gcol + jo:gcol + jo + 1], in1=stt[:, ir, :],
                        op0=ALU.mult, op1=ALU.mult),
                        reads=[y_b[jo], st_b[ir], const_b], writes=[y_b[jo]])
                    P.add(POOL, lambda e, jo=jo: e.tensor_tensor(out=xb[b][:, jo, :], in0=xb[b][:, jo, :], in1=ybuf[:, jo, :], op=ALU.add),
                          reads=[xb_b[b][jo], y_b[jo]], writes=[xb_b[b][jo]])

            hrhs = lambda kc: h[:, kc, :]
            h2rhs = lambda kc: h2[:, kc, :]

            def fe_proj(tile):
                slots_in = [blockref(B_IN + i) for i in range(6)]
                for j in range(4):
                    st_, sb_ = slots_in[j // 2]
                    ps_t, ps_b = proj_group(st_, sb_, (j % 2) * 1024, NKC, hrhs, h_b)
                    P.add(ACT, lambda e, j=j, ps_t=ps_t: e.activation(out=upool[:, j, HIST_U:HIST_U + T], in_=ps_t[:, :], func=AF.Copy),
                          reads=[ps_b], writes=[up_b[j]])
                ws.release(2)
                for jj in range(4):
                    jg, ja = 8 + jj, 4 + jj
                    st_, sb_ = slots_in[jg // 2]
                    psg_t, psg_b = proj_group(st_, sb_, (jg % 2) * 1024, NKC, hrhs, h_b)
                    st_, sb_ = slots_in[ja // 2]
                    psa_t, psa_b = proj_group(st_, sb_, (ja % 2) * 1024, NKC, hrhs, h_b)
                    i = rot("sg", 2)
                    P.add(ACT, lambda e, i=i, psg_t=psg_t: e.activation(out=sg[:, i, :], in_=psg_t[:, :], func=AF.Sigmoid),
                          reads=[psg_b], writes=[sg_b[i]])
                    P.add(DVE, lambda e, i=i, jj=jj, psa_t=psa_t: e.tensor_tensor(
                        out=vb[:, jj, HIST_V:HIST_V + T], in0=psa_t[:, :], in1=sg[:, i, :], op=ALU.mult),
                        reads=[psa_b, sg_b[i]], writes=[vb_b[jj]])
                ws.release(4)
                L = HIST_U + T
                for g in range(4):
                    cur = None
                    for s_ in range(g + 1):
                        sh = 1 << s_
                        lo = (1 << (s_ + 1)) - 1
                        dst_i = s_ % 2
                        if cur is None:
                            in_a = lambda e_lo, e_hi, g=g: upool[:, g, e_lo:e_hi]
                            in_buf = up_b[g]
                        else:
                            in_a = lambda e_lo, e_hi, ci=cur: ptmp[:, ci, e_lo:e_hi]
                            in_buf = pt_b[cur]
                        P.add(POOL, lambda e, in_a=in_a, dst_i=dst_i, lo=lo, sh=sh: e.tensor_tensor(
                            out=ptmp[:, dst_i, lo:L], in0=in_a(lo, L), in1=in_a(lo - sh, L - sh), op=ALU.add),
                            reads=[in_buf], writes=[pt_b[dst_i]])
                        cur = dst_i
                    wdw = float(2 ** (g + 1))
                    if tile == 0:
                        P.add(POOL, lambda e, g=g, cur=cur: e.tensor_tensor(
                            out=fix16[:, g, :], in0=ptmp[:, cur, HIST_U:2 * HIST_U], in1=invc[:, g, :], op=ALU.mult),
                            reads=[pt_b[cur], const_b], writes=[fix_b])
                    P.add(POOL, lambda e, cur=cur, wdw=wdw: e.tensor_scalar_mul(
                        out=ptmp[:, cur, HIST_U:L], in0=ptmp[:, cur, HIST_U:L], scalar1=1.0 / wdw),
                        reads=[pt_b[cur]], writes=[pt_b[cur]])
                    P.add(POOL, lambda e, g=g, cur=cur: e.tensor_tensor(
                        out=pooled[:, g, :], in0=ptmp[:, cur, HIST_U:L], in1=upool[:, g, HIST_U:L], op=ALU.subtract),
                        reads=[pt_b[cur], up_b[g]], writes=[pl_b[g]])
                    if tile == 0:
                        P.add(POOL, lambda e, g=g: e.tensor_tensor(
                            out=pooled[:, g, 0:HIST_U], in0=fix16[:, g, :], in1=upool[:, g, HIST_U:2 * HIST_U], op=ALU.subtract),
                            reads=[fix_b, up_b[g], pl_b[g]], writes=[pl_b[g]])
                    P.add(POOL, lambda e, g=g: e.tensor_copy(out=upool[:, g, 0:HIST_U], in_=upool[:, g, T:T + HIST_U]),
                          reads=[up_b[g]], writes=[up_b[g]])

            def fe_conv(tile):
                mu_t, mu_b = stat_bank(0)
                e2_t, e2_b = stat_bank(1)
                pm = []

                def t_pmap_all():
                    pw_t, pw_b = blockref(B_PW)
                    for g in range(4):
                        ps_t, ps_b = new_bank()
                        P.add(PE, lambda e, g=g, ps_t=ps_t: e.matmul(ps_t[:, :], pw_t[:, g * 128:(g + 1) * 128], pooled[:, g, :], start=True, stop=True),
                              reads=[pw_b, pl_b[g]], writes=[ps_b])
                        P.add(ACT, lambda e, g=g, ps_t=ps_t: e.activation(out=mixed[:, g, :], in_=ps_t[:, :], func=AF.Copy,
                                                                             scale=small[:, C_PSC + g:C_PSC + g + 1]),
                              reads=[ps_b, const_b], writes=[mx_b[g]])
                    ws.release(1)
                pm.append(t_pmap_all)

                th = []
                idxs = {}

                def t_tap(c, k):
                    src = vb[:, c, 2 + k:2 + k + T]
                    wcol = small[:, C_CW + c * CONV_K + k:C_CW + c * CONV_K + k + 1]
                    if k == 0:
                        P.add(DVE, lambda e, c=c: e.tensor_scalar(out=cc[:, c, :], in0=src, scalar1=wcol,
                                                                  scalar2=small[:, C_CB + c:C_CB + c + 1], op0=ALU.mult, op1=ALU.add),
                              reads=[vb_b[c], const_b], writes=[cc_b[c]])
                    else:
                        P.add(DVE, lambda e, c=c: e.scalar_tensor_tensor(out=cc[:, c, :], in0=src, scalar=wcol, in1=cc[:, c, :],
                                                                         op0=ALU.mult, op1=ALU.add),
                              reads=[vb_b[c], cc_b[c], const_b], writes=[cc_b[c]])
                    if k == CONV_K - 1:
                        P.add(POOL, lambda e, c=c: e.tensor_copy(out=vb[:, c, 0:HIST_V], in_=vb[:, c, T:T + HIST_V]),
                              reads=[vb_b[c]], writes=[vb_b[c]])
                for k in range(CONV_K):
                    for c in range(4):
                        th.append(lambda c=c, k=k: t_tap(c, k))
                taps = th
                th = list(pm)

                def t_cast(c):
                    i = rot("cb", 2)
                    idxs[c] = i
                    P.add(ACT, lambda e, c=c, i=i: e.activation(out=csq[:, i, :], in_=cc[:, c, :], func=AF.Square),
                          reads=[cc_b[c]], writes=[csq_b[i]])
                    P.add(DVE, lambda e, c=c, i=i: e.tensor_copy(out=cbf[:, i, :], in_=cc[:, c, :]),
                          reads=[cc_b[c]], writes=[cbf_b[i]])

                def t_stat(c):
                    i = idxs[c]
                    P.add(PE, lambda e, c=c, i=i: e.matmul(mu_t[:, :], onesC[:, :], cbf[:, i, :], start=(c == 0), stop=(c == 3)),
                          reads=[cbf_b[i], const_b], writes=[mu_b])
                    P.add(PE, lambda e, c=c, i=i: e.matmul(e2_t[:, :], onesC[:, :], csq[:, i, :], start=(c == 0), stop=(c == 3)),
                          reads=[csq_b[i], const_b], writes=[e2_b])
                for c in range(4):
                    th.append(lambda c=c: t_cast(c))
                    if c >= 1:
                        th.append(lambda c=c: t_stat(c - 1))
                th.append(lambda: t_stat(3))
                th.append(lambda: P.add(ACT, lambda e: e.activation(out=lnt[:, 0, :], in_=mu_t[:, :], func=AF.Copy), reads=[mu_b], writes=[ln_b[0]]))
                th.append(lambda: P.add(POOL, lambda e: e.tensor_tensor(out=lnt[:, 1, :], in0=lnt[:, 0, :], in1=lnt[:, 0, :], op=ALU.mult),
                                        reads=[ln_b[0]], writes=[ln_b[1]]))
                th.append(lambda: P.add(DVE, lambda e: e.scalar_tensor_tensor(out=lnt[:, 1, :], in0=e2_t[:, :], scalar=EPS, in1=lnt[:, 1, :],
                                                                              op0=ALU.add, op1=ALU.subtract),
                                        reads=[e2_b, ln_b[1]], writes=[ln_b[1]]))
                th.append(lambda: P.add(ACT, lambda e: e.activation(out=lnt[:, 1, :], in_=lnt[:, 1, :], func=AF.Ln), reads=[ln_b[1]], writes=[ln_b[1]]))
                th.append(lambda: P.add(ACT, lambda e: e.activation(out=lnt[:, 1, :], in_=lnt[:, 1, :], func=AF.Exp, scale=-0.5), reads=[ln_b[1]], writes=[ln_b[1]]))

                def t_sub(c):
                    P.add(DVE, lambda e, c=c: e.tensor_tensor(out=cc[:, c, :], in0=cc[:, c, :], in1=lnt[:, 0, :], op=ALU.subtract),
                          reads=[cc_b[c], ln_b[0]], writes=[cc_b[c]])

                def t_mul(c):
                    P.add(DVE, lambda e, c=c: e.tensor_tensor(out=cc[:, c, :], in0=cc[:, c, :], in1=lnt[:, 1, :], op=ALU.mult),
                          reads=[cc_b[c], ln_b[1]], writes=[cc_b[c]])

                def t_act(c):
                    i = rot("sg", 2)
                    idxs[("s", c)] = i
                    P.add(ACT, lambda e, c=c, i=i: e.activation(out=sg[:, i, :], in_=cc[:, c, :], func=AF.Sigmoid,
                                                                   bias=small[:, C_LB + c:C_LB + c + 1], scale=small[:, C_LG + c:C_LG + c + 1]),
                          reads=[cc_b[c], const_b], writes=[sg_b[i]])

                def t_aff(c):
                    i = idxs[("s", c)]
                    P.add(DVE, lambda e, c=c: e.tensor_scalar(out=cc[:, c, :], in0=cc[:, c, :], scalar1=small[:, C_LG + c:C_LG + c + 1],
                                                              scalar2=small[:, C_LB + c:C_LB + c + 1], op0=ALU.mult, op1=ALU.add),
                          reads=[cc_b[c], const_b, sg_b[i]], writes=[cc_b[c]])

                def t_out(c):
                    i = idxs[("s", c)]
                    P.add(POOL, lambda e, c=c, i=i: e.tensor_tensor(out=convact[:, c, :], in0=cc[:, c, :], in1=sg[:, i, :], op=ALU.mult),
                          reads=[cc_b[c], sg_b[i]], writes=[ca_b[c]])
                for c in range(4):
                    th.append(lambda c=c: t_sub(c))
                for c in range(4):
                    th.append(lambda c=c: t_mul(c))
                for c0 in (0, 2):
                    for c in (c0, c0 + 1):
                        th.append(lambda c=c: t_act(c))
                    for c in (c0, c0 + 1):
                        th.append(lambda c=c: t_aff(c))
                    for c in (c0, c0 + 1):
                        th.append(lambda c=c: t_out(c))
                return taps, th

            def mix_E(tile, jps, bg=None, nd=2):
                for jp in jps:
                    ga = blockref(B_IN + 6 + jp)
                    po = blockref(B_PO + jp)
                    gb = blockref(B_IN + 10 + jp)
                    co = blockref(B_CO + jp)
                    for jj in range(2):
                        j = jp * 2 + jj
                        pga_t, pga_b = proj_group(ga[0], ga[1], jj * 1024, NKC, hrhs, h_b)
                        pA_t, pA_b = proj_group(po[0], po[1], jj * 512, 4, lambda kc: mixed[:, kc, :], mx_b)
                        pgb_t, pgb_b = proj_group(gb[0], gb[1], jj * 1024, NKC, hrhs, h_b)
                        pB_t, pB_b = proj_group(co[0], co[1], jj * 512, 4, lambda kc: convact[:, kc, :], ca_b)
                        ia = rot("mt", NMT)
                        ib = rot("mt", NMT)
                        P.add(ACT, lambda e, ia=ia, pga_t=pga_t: e.activation(out=mtmp[:, ia, :], in_=pga_t[:, :], func=AF.Sigmoid),
                              reads=[pga_b], writes=[mt_b[ia]])
                        P.add(ACT, lambda e, ib=ib, pgb_t=pgb_t: e.activation(out=mtmp[:, ib, :], in_=pgb_t[:, :], func=AF.Sigmoid),
                              reads=[pgb_b], writes=[mt_b[ib]])
                        P.add(DVE, lambda e, ia=ia, pA_t=pA_t: e.tensor_tensor(out=mtmp[:, ia, :], in0=pA_t[:, :], in1=mtmp[:, ia, :], op=ALU.mult),
                              reads=[pA_b, mt_b[ia]], writes=[mt_b[ia]])
                        P.add(DVE, lambda e, ib=ib, pB_t=pB_t: e.tensor_tensor(out=mtmp[:, ib, :], in0=pB_t[:, :], in1=mtmp[:, ib, :], op=ALU.mult),
                              reads=[pB_b, mt_b[ib]], writes=[mt_b[ib]])
                        P.add(POOL, lambda e, ia=ia, ib=ib, j=j: e.tensor_tensor(out=merged_ap(j), in0=mtmp[:, ia, :], in1=mtmp[:, ib, :], op=ALU.add),
                              reads=[mt_b[ia], mt_b[ib]], writes=[mg_b[j]])
                        if bg is not None:
                            drain(bg, nd)
                    ws.release(4)

            def mix_wo(tile, bg=None):
                wo_slots = [blockref(B_WO + i) for i in range(4)]

                def y_wo(jo):
                    st_, sb_ = wo_slots[jo // 2]
                    r_ = proj_group(st_, sb_, (jo % 2) * 1024, NKC, merged_ap, mg_b)
                    if jo % 2 == 1:
                        ws.release(1)
                    return r_
                return post_groups(y_wo, bg)

            def mlp_up(tile, blks, bg=None):
                for blk in blks:
                    st_, sb_ = blockref(B_UP + blk)
                    for jj in range(2):
                        if bg is not None:
                            drain(bg, 4)
                        jf = blk * 2 + jj
                        ps_t, ps_b = proj_group(st_, sb_, jj * 1024, NKC, h2rhs, h2_b)
                        i = rot("mt", NMT)
                        P.add(ACT, lambda e, i=i, ps_t=ps_t: e.activation(out=mtmp[:, i, :], in_=ps_t[:, :], func=AF.Relu),
                              reads=[ps_b], writes=[mt_b[i]])
                        P.add(ACT, lambda e, i=i, jf=jf: e.activation(out=fbuf[:, jf, :], in_=mtmp[:, i, :], func=AF.Square),
                              reads=[mt_b[i]], writes=[f_b[jf]])
                    ws.release(1)

            def mlp_down(tile, bg=None):
                def y_dn(jo):
                    ps_t, ps_b = new_bank()
                    for half in range(2):
                        if bg is not None:
                            drain(bg[0], 8)
                            if jo * 2 + half >= 5:
                                drain(bg[1], 4)
                        st_, sb_ = blockref(B_DN + jo * 2 + half)
                        for kk in range(16):
                            kc = half * 16 + kk
                            P.add(PE, lambda e, kk=kk, kc=kc, st_=st_, ps_t=ps_t: e.matmul(
                                ps_t[:, :], st_[:, kk * 128:(kk + 1) * 128], fbuf[:, kc, :], start=(kc == 0), stop=(kc == NFC - 1)),
                                reads=[sb_, f_b[kc]], writes=[ps_b])
                        ws.release(1)
                    return ps_t, ps_b
                return post_groups(y_dn)

            def flush(bg):
                while bg:
                    bg.pop(0)()

            ws.start()
            load_x(0)
            ir = norm_stats(0)
            norm_apply(0, ir, C_NMP, h, h_b)
            fe_proj(0)
            if n_tiles > 1:
                preconvert([B_UP + i for i in range(16)] + [B_DN + i for i in range(16)])
            taps, lnth = fe_conv(0)
            flush(taps)
            flush(lnth)
            bg_post = []
            for tile in range(n_tiles):
                b = tile % 2
                nxt = tile + 1 if tile + 1 < n_tiles else None
                overlap = nxt is not None
                mix_E(tile, (0, 1), bg_post, 2)
                flush(bg_post)
                if tile >= 1:
                    store_x(tile - 1)
                    if nxt is not None:
                        load_x(nxt)
                mix_E(tile, (2, 3))
                if tile == 0 and nxt is not None:
                    load_x(nxt)
                bg_h = []
                if overlap:
                    irn = norm_stats(nxt % 2)
                    norm_apply(nxt % 2, irn, C_NMP, h, h_b, bg_h)
                ir1 = mix_wo(tile, bg_h)
                flush(bg_h)
                post_chain(b, ir1, C_NMPOST)
                if overlap:
                    fe_proj(nxt)
                ir2 = norm_stats(b)
                norm_apply(b, ir2, C_NLP, h2, h2_b)
                bgs = fe_conv(nxt) if overlap else ([], [])
                mlp_up(tile, range(16), bgs[0])
                ir3 = mlp_down(tile, bgs)
                flush(bgs[0])
                flush(bgs[1])
                bg_post = []
                post_chain(b, ir3, C_NLPOST, bg_post)
            flush(bg_post)
            store_x(n_tiles - 1)
            return None

        rec = WS()
        emit(Prog(), rec)
        ws = WS(order=list(rec.rec))
        P = Prog()
        info = emit(P, ws)
        assert ws.acq == len(ws.order)
        P.finalize()
        final_waits = []
        for bb in range(2):
            n = P.dma_n.get(id(x_st[bb]), 0)
            if n:
                final_waits.append((x_st[bb], 16 * n))

        block = es.enter_context(nc.Block())

        @block.sync
        def _(eng):
            P.run(SP, eng, sems, final_waits)

        @block.tensor
        def _(eng):
            P.run(PE, eng, sems)

        @block.scalar
        def _(eng):
            P.run(ACT, eng, sems)

        @block.vector
        def _(eng):
            P.run(DVE, eng, sems)

        @block.gpsimd
        def _(eng):
            P.run(POOL, eng, sems)

    return nc


def _img(Wm):
    K, N = Wm.shape
    nk, nn = K // 128, N // 128
    return np.ascontiguousarray(Wm.reshape(nk, 128, nn, 128).transpose(1, 2, 0, 3).reshape(128, nn * nk * 128))


def _cols(v, n):
    return np.ascontiguousarray(np.asarray(v, np.float32).reshape(n, 128).T)


def pack_weights(w_in, pool_w, w_pool_out, w_conv_out, w_o, w_up, w_down):
    wimg = np.zeros((128, NB_W * SLOTW), np.float32)
    wimg[:, B_IN * SLOTW:B_IN * SLOTW + 28672] = _img(w_in)
    wimg[:, B_PW * SLOTW:B_PW * SLOTW + 512] = np.asarray(pool_w).transpose(1, 0, 2).reshape(128, 512)
    ipo, ico = _img(w_pool_out), _img(w_conv_out)
    for q in range(4):
        wimg[:, (B_PO + q) * SLOTW:(B_PO + q) * SLOTW + 1024] = ipo[:, q * 1024:(q + 1) * 1024]
        wimg[:, (B_CO + q) * SLOTW:(B_CO + q) * SLOTW + 1024] = ico[:, q * 1024:(q + 1) * 1024]
    wimg[:, B_WO * SLOTW:B_WO * SLOTW + 8192] = _img(w_o)
    wimg[:, B_UP * SLOTW:B_UP * SLOTW + 32768] = _img(w_up)
    wimg[:, B_DN * SLOTW:B_DN * SLOTW + 32768] = _img(w_down)
    return wimg


def pack_small(norm_mix_pre, norm_mix_post, norm_mlp_pre, norm_mlp_post, pool_scale, conv_b, conv_ln_g, conv_ln_b, conv_w):
    small = np.zeros((128, NSMALL), np.float32)
    small[:, C_NMP:C_NMP + 8] = _cols(norm_mix_pre, 8)
    small[:, C_NMPOST:C_NMPOST + 8] = _cols(norm_mix_post, 8)
    small[:, C_NLP:C_NLP + 8] = _cols(norm_mlp_pre, 8)
    small[:, C_NLPOST:C_NLPOST + 8] = _cols(norm_mlp_post, 8)
    small[:, C_PSC:C_PSC + 4] = _cols(pool_scale, 4)
    small[:, C_CB:C_CB + 4] = _cols(conv_b, 4)
    small[:, C_LG:C_LG + 4] = _cols(conv_ln_g, 4)
    small[:, C_LB:C_LB + 4] = _cols(conv_ln_b, 4)
    cw = np.asarray(conv_w, np.float32).reshape(CONV_K, 4, 128)
    small[:, C_CW:C_CW + 4 * CONV_K] = cw.transpose(2, 1, 0).reshape(128, 4 * CONV_K)
    return small


def kernel(x, norm_mix_pre, w_in, pool_w, pool_scale, w_pool_out, conv_w, conv_b,
           conv_ln_g, conv_ln_b, w_conv_out, w_o, norm_mix_post, norm_mlp_pre,
           w_up, w_down, norm_mlp_post):
    x = np.asarray(x, np.float32)
    B = x.shape[0]
    f = lambda a: np.asarray(a, np.float32)[0]
    wimg = pack_weights(f(w_in), f(pool_w), f(w_pool_out), f(w_conv_out), f(w_o), f(w_up), f(w_down))
    small = pack_small(f(norm_mix_pre), f(norm_mix_post), f(norm_mlp_pre), f(norm_mlp_post), f(pool_scale),
                       f(conv_b), f(conv_ln_g), f(conv_ln_b), f(conv_w))
    nc = build_nc(x.shape[1] // T)
    in_maps = [{"xT": np.ascontiguousarray(x[b].T), "wimg": wimg, "small": small} for b in range(B)]
    res = run_bass_kernel_spmd(nc, in_maps, core_ids=list(range(B)))
    out = np.stack([np.ascontiguousarray(res.results[b]["outT"].T) for b in range(B)], axis=0)
    return out.astype(np.float32)
```

```python
import numpy as np
from contextlib import ExitStack
import concourse.bass as bass
import concourse.mybir as mybir
from concourse.bass_utils import run_bass_kernel_spmd

F32 = mybir.dt.float32
BF16 = mybir.dt.bfloat16
I32 = mybir.dt.int32
AF = mybir.ActivationFunctionType
ALU = mybir.AluOpType

D = 1024
SEQ = 4096
T = 512
NKC = 8
DFF = 4096
NFC = 32
EPS = 1e-6
CONV_K = 31
HIST_U = 16
HIST_V = 32
NSLOT = 8
SLOTW = 2048

PE, ACT, DVE, POOL, SP = "pe", "act", "dve", "pool", "sp"
ENGINES = (PE, ACT, DVE, POOL, SP)

C_NMP, C_NMPOST, C_NLP, C_NLPOST = 0, 8, 16, 24
C_PSC, C_CB, C_LG, C_LB, C_CW = 32, 36, 40, 44, 48
NSMALL = C_CW + 4 * CONV_K

B_IN, B_PW, B_PO, B_CO, B_WO, B_UP, B_DN = 0, 14, 15, 19, 23, 27, 43
NB_W = 59
NB_ALL = 59


def blk_ncols(blk):
    if blk == B_PW:
        return 512
    if B_PO <= blk < B_WO:
        return 1024
    return SLOTW


class Buf:
    __slots__ = ("name", "last_w", "readers")

    def __init__(self, name):
        self.name = name
        self.last_w = None
        self.readers = []


class Op:
    __slots__ = ("eng", "fn", "pos", "waits", "signals", "dma_sem", "dma_val", "count")


class Prog:
    def __init__(self):
        self.ops = {e: [] for e in ENGINES}
        self.waited = {e: {} for e in ENGINES}
        self.dma_n = {}

    def add(self, eng, fn, reads=(), writes=(), dma_sem=None):
        op = Op()
        op.eng, op.fn, op.pos = eng, fn, len(self.ops[eng])
        op.waits, op.signals, op.dma_sem, op.dma_val, op.count = [], False, dma_sem, 0, 0
        deps = []
        for b in reads:
            if b.last_w is not None:
                deps.append(b.last_w)
        for b in writes:
            if b.last_w is not None:
                deps.append(b.last_w)
            deps.extend(b.readers)
        w = self.waited[eng]
        best = {}
        dma_deps = []
        for d in deps:
            if d.dma_sem is not None:
                dma_deps.append(d)
            elif d.eng not in best or best[d.eng].pos < d.pos:
                best[d.eng] = d
        deps = dma_deps + list(best.values())
        for d in deps:
            if d.dma_sem is not None:
                key = ("dma", id(d.dma_sem))
                if w.get(key, 0) >= d.dma_val:
                    continue
                w[key] = d.dma_val
                op.waits.append(d)
            else:
                if d.eng == eng and eng in (PE, SP):
                    continue
                if w.get(d.eng, -1) >= d.pos:
                    continue
                w[d.eng] = d.pos
                d.signals = True
                op.waits.append(d)
        if dma_sem is not None:
            n = self.dma_n.get(id(dma_sem), 0) + 1
            self.dma_n[id(dma_sem)] = n
            op.dma_val = 16 * n
        for b in reads:
            b.readers.append(op)
        for b in writes:
            b.last_w = op
            b.readers = []
        self.ops[eng].append(op)
        return op

    def finalize(self):
        for e in ENGINES:
            c = 0
            for op in self.ops[e]:
                if op.signals and op.dma_sem is None:
                    c += 1
                op.count = c

    def run(self, eng_name, eng, sems, final_waits=()):
        for op in self.ops[eng_name]:
            for d in op.waits:
                if d.dma_sem is not None:
                    eng.wait_ge(d.dma_sem, d.dma_val)
                else:
                    eng.wait_ge(sems[d.eng], d.count)
            ins = op.fn(eng)
            if op.dma_sem is not None:
                ins.then_inc(op.dma_sem, 16)
            elif op.signals:
                ins.then_inc(sems[eng_name], 1)
        for sem, val in final_waits:
            eng.wait_ge(sem, val)


class WS:
    def __init__(self, order=None):
        self.order = order
        self.rec = []
        self.next_load = 0
        self.acq = 0
        self.emit_load = None

    def acquire(self, blk):
        if self.order is None:
            self.rec.append(blk)
            return (len(self.rec) - 1) % NSLOT
        sq_n = self.acq
        assert self.order[sq_n] == blk, (sq_n, self.order[sq_n], blk)
        assert sq_n < self.next_load, (sq_n, self.next_load)
        self.acq += 1
        return sq_n % NSLOT

    def start(self):
        if self.order is None:
            return
        while self.next_load < min(NSLOT, len(self.order)):
            self.emit_load(self.next_load)
            self.next_load += 1

    def release(self, n=1):
        if self.order is None:
            return
        for _ in range(n):
            if self.next_load < len(self.order):
                self.emit_load(self.next_load)
                self.next_load += 1


def build_nc(n_tiles=SEQ // T):
    seq = n_tiles * T
    nc = bass.Bass("TRN2", target_bir_lowering=False)
    xT = nc.dram_tensor("xT", [D, seq], F32, kind="ExternalInput").ap()
    wimg = nc.dram_tensor("wimg", [128, NB_W * SLOTW], F32, kind="ExternalInput").ap()
    smalld = nc.dram_tensor("small", [128, NSMALL], F32, kind="ExternalInput").ap()
    outT = nc.dram_tensor("outT", [D, seq], F32, kind="ExternalOutput").ap()
    wscr = nc.dram_tensor("wscr", [128, NB_ALL * SLOTW], BF16, kind="Internal").ap()
    xT3 = xT.rearrange("(kc p) t -> p kc t", p=128)
    outT3 = outT.rearrange("(kc p) t -> p kc t", p=128)

    with ExitStack() as es:
        def sb(name, shape, dt):
            return es.enter_context(nc.sbuf_tensor(name, shape, dt))

        def sem(name):
            return es.enter_context(nc.semaphore(name))

        xb = [sb(f"xb{i}", [128, NKC, T], F32) for i in range(2)]
        h = sb("h", [128, NKC, T], BF16)
        h2 = sb("h2", [128, NKC, T], BF16)
        NSQ = 4
        sq = sb("sq", [128, NSQ, T], BF16)
        ybuf = sb("ybuf", [128, NKC, T], F32)
        NST = 4
        stt = sb("stt", [128, NST, T], F32)
        onesD = sb("onesD", [128, 128], BF16)
        onesC = sb("onesC", [128, 128], BF16)
        small = sb("smallsb", [128, NSMALL], F32)
        epsT = sb("epsT", [128, 1], F32)
        invci = sb("invci", [128, HIST_U], I32)
        invc = sb("invc", [128, 4, HIST_U], F32)
        fix16 = sb("fix16", [128, 4, HIST_U], F32)
        ring = [sb(f"ring{i}", [128, SLOTW], BF16) for i in range(NSLOT)]
        upool = sb("upool", [128, 4, HIST_U + T], F32)
        ptmp = sb("ptmp", [128, 2, HIST_U + T], F32)
        pooled = sb("pooled", [128, 4, T], BF16)
        mixed = sb("mixed", [128, 4, T], BF16)
        vb = sb("vb", [128, 4, HIST_V + T], BF16)
        sg = sb("sg", [128, 2, T], F32)
        cc = sb("cc", [128, 4, T], F32)
        cbf = sb("cbf", [128, 2, T], BF16)
        csq = sb("csq", [128, 2, T], BF16)
        lnt = sb("lnt", [128, 2, T], F32)
        convact = sb("convact", [128, 4, T], BF16)
        NMT = 4
        mtmp = sb("mtmp", [128, NMT, T], F32)
        fbuf = sb("fbuf", [128, NFC, T], BF16)
        MG0 = NFC - NKC

        banks = [es.enter_context(nc.psum_tensor(f"ps{i}", [128, T], F32)) for i in range(8)]

        sems = {e: sem(f"s_{e}") for e in (PE, ACT, DVE, POOL)}
        ring_ld = [sem(f"rl{i}") for i in range(NSLOT)]
        ring_st = [sem(f"rs{i}") for i in range(NSLOT)]
        stage_ld = [sem(f"sl{i}") for i in range(2)]
        x_ld = [sem(f"xl{i}") for i in range(2)]
        x_st = [sem(f"xs{i}") for i in range(2)]
        small_ld = sem("smld")
        pc_st = [sem(f"pc{i}") for i in range(2)]

        def emit(P, ws):
            xb_b = [[Buf(f"xb{i}_{k}") for k in range(NKC)] for i in range(2)]
            h_b = [Buf(f"h{k}") for k in range(NKC)]
            h2_b = [Buf(f"h2{k}") for k in range(NKC)]
            sq_b = [Buf(f"sq{k}") for k in range(NSQ)]
            y_b = [Buf(f"y{k}") for k in range(NKC)]
            st_b = [Buf(f"st{k}") for k in range(NST)]
            const_b = Buf("consts")
            fix_b = Buf("fix16")
            ring_b = [Buf(f"ring{i}") for i in range(NSLOT)]
            up_b = [Buf(f"up{g}") for g in range(4)]
            pt_b = [Buf("ptA"), Buf("ptB")]
            pl_b = [Buf(f"pl{g}") for g in range(4)]
            mx_b = [Buf(f"mx{g}") for g in range(4)]
            vb_b = [Buf(f"vb{c}") for c in range(4)]
            sg_b = [Buf("sg0"), Buf("sg1")]
            cc_b = [Buf(f"cc{c}") for c in range(4)]
            cbf_b = [Buf("cbf0"), Buf("cbf1")]
            csq_b = [Buf("csq0"), Buf("csq1")]
            ln_b = [Buf("ln0"), Buf("ln1")]
            ca_b = [Buf(f"ca{c}") for c in range(4)]
            mt_b = [Buf(f"mt{k}") for k in range(NMT)]
            f_b = [Buf(f"f{k}") for k in range(NFC)]
            bank_b = [Buf(f"ps{i}") for i in range(8)]
            scr_b = [Buf(f"scr{i}") for i in range(NB_ALL)]
            bank_i = [0]
            rr = {}

            def rot(key, n):
                i = rr.get(key, 0)
                rr[key] = i + 1
                return i % n

            def new_bank():
                i = bank_i[0] % 5
                bank_i[0] += 1
                return banks[i], bank_b[i]

            def stat_bank(i):
                return banks[6 + i], bank_b[6 + i]

            def rms_bank():
                return banks[5], bank_b[5]

            merged_ap = lambda kc: fbuf[:, MG0 + kc, :]
            mg_b = f_b[MG0:]

            P.add(SP, lambda e: e.dma_start(out=small[:, :], in_=smalld[:, :]), writes=[const_b], dma_sem=small_ld)
            P.add(POOL, lambda e: e.memset(onesD[:, :], 1.0 / D), writes=[const_b])
            P.add(POOL, lambda e: e.memset(onesC[:, :], 1.0 / 512.0), writes=[const_b])
            P.add(POOL, lambda e: e.memset(epsT[:, :], EPS), writes=[const_b])
            P.add(POOL, lambda e: e.iota(invci[:, :], pattern=[[1, HIST_U]], base=1, channel_multiplier=0), writes=[const_b])
            for g in range(4):
                wdw = float(2 ** (g + 1))
                P.add(DVE, lambda e, g=g, wdw=wdw: e.tensor_single_scalar(out=invc[:, g, :], in_=invci[:, :], scalar=wdw, op=ALU.min),
                      reads=[const_b], writes=[const_b])
            P.add(DVE, lambda e: e.reciprocal(out=invc[:, :, :], in_=invc[:, :, :]), reads=[const_b], writes=[const_b])
            for g in range(4):
                P.add(POOL, lambda e, g=g: e.memset(upool[:, g, 0:HIST_U], 0.0), writes=[up_b[g]])
                P.add(POOL, lambda e, g=g: e.memset(vb[:, g, 0:HIST_V], 0.0), writes=[vb_b[g]])

            converted = set()
            pend_cast = []

            def stage_ap(s):
                return xb[1][:, s * 4:s * 4 + 4, :]

            def stage_bufs(s):
                return xb_b[1][s * 4:s * 4 + 4]

            def emit_stage_load(blk):
                s = rot("stage", 2)
                src = wimg[:, blk * SLOTW:(blk + 1) * SLOTW].rearrange("p (a b) -> p a b", b=T)
                P.add(SP, lambda e, s=s, src=src: e.dma_start(out=stage_ap(s), in_=src),
                      writes=stage_bufs(s), dma_sem=stage_ld[s])
                return s

            def next_unconverted_weight(from_seq):
                for q in range(from_seq, len(ws.order)):
                    b_ = ws.order[q]
                    if b_ < NB_W and b_ not in converted and all(b_ != pb for pb, _ in pend_cast):
                        return b_
                return None

            def emit_load(sq_n):
                blk = ws.order[sq_n]
                slot = sq_n % NSLOT
                if blk in converted:
                    ncols = blk_ncols(blk)
                    P.add(SP, lambda e, slot=slot, blk=blk, ncols=ncols: e.dma_start(
                        out=ring[slot][:, 0:ncols], in_=wscr[:, blk * SLOTW:blk * SLOTW + ncols]),
                        reads=[scr_b[blk]], writes=[ring_b[slot]], dma_sem=ring_ld[slot])
                    return
                converted.add(blk)
                if True:
                    if pend_cast and pend_cast[0][0] == blk:
                        _, s = pend_cast.pop(0)
                    else:
                        s = emit_stage_load(blk)
                    if not pend_cast:
                        nb = next_unconverted_weight(sq_n + 1)
                        if nb is not None:
                            pend_cast.append((nb, emit_stage_load(nb)))
                    dst = ring[slot][:, :].rearrange("p (a b) -> p a b", b=T)
                    if rot("cast", 3) == 2:
                        P.add(ACT, lambda e, s=s, dst=dst: e.activation(out=dst, in_=stage_ap(s), func=AF.Copy),
                              reads=stage_bufs(s), writes=[ring_b[slot]])
                    else:
                        P.add(DVE, lambda e, s=s, dst=dst: e.tensor_copy(out=dst, in_=stage_ap(s)),
                              reads=stage_bufs(s), writes=[ring_b[slot]])
                P.add(SP, lambda e, slot=slot, blk=blk: e.dma_start(out=wscr[:, blk * SLOTW:(blk + 1) * SLOTW], in_=ring[slot][:, :]),
                      reads=[ring_b[slot]], writes=[scr_b[blk]], dma_sem=ring_st[slot])

            def preconvert(blks):
                pend_cast.clear()
                st_of = {}
                for n in range(min(2, len(blks))):
                    st_of[n] = emit_stage_load(blks[n])
                for n, blk in enumerate(blks):
                    s_ = st_of[n]
                    r = n % 2
                    bb = f_b[r * 4:(r + 1) * 4]
                    dst = fbuf[:, r * 4:(r + 1) * 4, :]
                    P.add(ACT, lambda e, s_=s_, dst=dst: e.activation(out=dst, in_=stage_ap(s_), func=AF.Copy),
                          reads=stage_bufs(s_), writes=bb)
                    P.add(SP, lambda e, blk=blk, dst=dst: e.dma_start(
                        out=wscr[:, blk * SLOTW:(blk + 1) * SLOTW].rearrange("p (a b) -> p a b", b=T), in_=dst),
                        reads=bb, writes=[scr_b[blk]], dma_sem=pc_st[r])
                    converted.add(blk)
                    if n + 2 < len(blks):
                        st_of[n + 2] = emit_stage_load(blks[n + 2])

            ws.emit_load = emit_load

            def blockref(blk):
                s = ws.acquire(blk)
                return ring[s], ring_b[s]

            def load_x(tile):
                b = tile % 2
                P.add(SP, lambda e, b=b, tile=tile: e.dma_start(out=xb[b][:, :, :], in_=xT3[:, :, tile * T:(tile + 1) * T]),
                      writes=xb_b[b], dma_sem=x_ld[b])

            def store_x(tile):
                b = tile % 2
                P.add(SP, lambda e, b=b, tile=tile: e.dma_start(out=outT3[:, :, tile * T:(tile + 1) * T], in_=xb[b][:, :, :]),
                      reads=xb_b[b], dma_sem=x_st[b])

            def rms_rstd(ps_t, ps_b):
                i1 = rot("st", NST)
                P.add(ACT, lambda e, i1=i1, ps_t=ps_t: e.activation(out=stt[:, i1, :], in_=ps_t[:, :], func=AF.Ln, bias=epsT[:, 0:1]),
                      reads=[ps_b, const_b], writes=[st_b[i1]])
                i2 = rot("st", NST)
                P.add(ACT, lambda e, i1=i1, i2=i2: e.activation(out=stt[:, i2, :], in_=stt[:, i1, :], func=AF.Exp, scale=-0.5),
                      reads=[st_b[i1]], writes=[st_b[i2]])
                return i2

            def norm_stats(b):
                ps_t, ps_b = rms_bank()
                for kc in range(NKC):
                    i = rot("sq", NSQ)
                    P.add(ACT, lambda e, i=i, kc=kc: e.activation(out=sq[:, i, :], in_=xb[b][:, kc, :], func=AF.Square),
                          reads=[xb_b[b][kc]], writes=[sq_b[i]])
                    P.add(PE, lambda e, i=i, kc=kc, ps_t=ps_t: e.matmul(ps_t[:, :], onesD[:, :], sq[:, i, :], start=(kc == 0), stop=(kc == NKC - 1)),
                          reads=[sq_b[i], const_b], writes=[ps_b])
                return rms_rstd(ps_t, ps_b)

            def norm_apply(b, ir, gcol, ht, hb, bg=None):
                for kc in range(NKC):
                    if bg is not None:
                        bg.append(lambda kc=kc: norm_chunk(b, ir, gcol, ht, hb, kc))
                    else:
                        norm_chunk(b, ir, gcol, ht, hb, kc)

            def norm_chunk(b, ir, gcol, ht, hb, kc):
                if True:
                    P.add(DVE, lambda e, kc=kc: e.scalar_tensor_tensor(
                        out=ht[:, kc, :], in0=xb[b][:, kc, :], scalar=small[:, gcol + kc:gcol + kc + 1], in1=stt[:, ir, :],
                        op0=ALU.mult, op1=ALU.mult),
                        reads=[xb_b[b][kc], st_b[ir], const_b], writes=[hb[kc]])

            def proj_group(slot_t, slot_b, col0, nk, rhs_fn, rhs_bufs):
                ps_t, ps_b = new_bank()
                for kc in range(nk):
                    P.add(PE, lambda e, kc=kc, ps_t=ps_t: e.matmul(
                        ps_t[:, :], slot_t[:, col0 + kc * 128:col0 + (kc + 1) * 128], rhs_fn(kc), start=(kc == 0), stop=(kc == nk - 1)),
                        reads=[slot_b, rhs_bufs[kc]], writes=[ps_b])
                return ps_t, ps_b

            def drain(bg, n=1):
                for _ in range(n):
                    if bg:
                        bg.pop(0)()

            def post_groups(yps, bg=None):
                st_t, st_bk = rms_bank()
                pend = None
                for jo in range(NKC):
                    if bg is not None and jo >= 1:
                        drain(bg, 1)
                    ps_t, ps_b = yps(jo)
                    i = rot("sq", NSQ)
                    P.add(ACT, lambda e, jo=jo, ps_t=ps_t: e.activation(out=ybuf[:, jo, :], in_=ps_t[:, :], func=AF.Copy),
                          reads=[ps_b], writes=[y_b[jo]])
                    P.add(ACT, lambda e, i=i, jo=jo: e.activation(out=sq[:, i, :], in_=ybuf[:, jo, :], func=AF.Square),
                          reads=[y_b[jo]], writes=[sq_b[i]])
                    if pend is not None:
                        pi, pj = pend
                        P.add(PE, lambda e, pi=pi, pj=pj: e.matmul(st_t[:, :], onesD[:, :], sq[:, pi, :], start=(pj == 0), stop=False),
                              reads=[sq_b[pi], const_b], writes=[st_bk])
                    pend = (i, jo)
                pi, pj = pend
                P.add(PE, lambda e, pi=pi, pj=pj: e.matmul(st_t[:, :], onesD[:, :], sq[:, pi, :], start=False, stop=True),
                      reads=[sq_b[pi], const_b], writes=[st_bk])
                return rms_rstd(st_t, st_bk)

            def post_chain(b, ir, gcol, bg=None):
                for jo in range(NKC):
                    if bg is not None:
                        bg.append(lambda jo=jo: post_chunk(b, ir, gcol, jo))
                    else:
                        post_chunk(b, ir, gcol, jo)

            def post_chunk(b, ir, gcol, jo):
                if True:
                    P.add(DVE, lambda e, jo=jo: e.scalar_tensor_tensor(
                        out=ybuf[:, jo, :], in0=ybuf[:, jo, :], scalar=small[:, gcol + jo:gcol + jo + 1], in1=stt[:, ir, :],
                        op0=ALU.mult, op1=ALU.mult),
                        reads=[y_b[jo], st_b[ir], const_b], writes=[y_b[jo]])
                    P.add(POOL, lambda e, jo=jo: e.tensor_tensor(out=xb[b][:, jo, :], in0=xb[b][:, jo, :], in1=ybuf[:, jo, :], op=ALU.add),
                          reads=[xb_b[b][jo], y_b[jo]], writes=[xb_b[b][jo]])

            hrhs = lambda kc: h[:, kc, :]
            h2rhs = lambda kc: h2[:, kc, :]

            def fe_proj(tile):
                slots_in = [blockref(B_IN + i) for i in range(6)]
                for j in range(4):
                    st_, sb_ = slots_in[j // 2]
                    ps_t, ps_b = proj_group(st_, sb_, (j % 2) * 1024, NKC, hrhs, h_b)
                    P.add(ACT, lambda e, j=j, ps_t=ps_t: e.activation(out=upool[:, j, HIST_U:HIST_U + T], in_=ps_t[:, :], func=AF.Copy),
                          reads=[ps_b], writes=[up_b[j]])
                ws.release(2)
                for jj in range(4):
                    jg, ja = 8 + jj, 4 + jj
                    st_, sb_ = slots_in[jg // 2]
                    psg_t, psg_b = proj_group(st_, sb_, (jg % 2) * 1024, NKC, hrhs, h_b)
                    st_, sb_ = slots_in[ja // 2]
                    psa_t, psa_b = proj_group(st_, sb_, (ja % 2) * 1024, NKC, hrhs, h_b)
                    i = rot("sg", 2)
                    P.add(ACT, lambda e, i=i, psg_t=psg_t: e.activation(out=sg[:, i, :], in_=psg_t[:, :], func=AF.Sigmoid),
                          reads=[psg_b], writes=[sg_b[i]])
                    P.add(DVE, lambda e, i=i, jj=jj, psa_t=psa_t: e.tensor_tensor(
                        out=vb[:, jj, HIST_V:HIST_V + T], in0=psa_t[:, :], in1=sg[:, i, :], op=ALU.mult),
                        reads=[psa_b, sg_b[i]], writes=[vb_b[jj]])
                ws.release(4)

            def fe_pool(tile):
                L = HIST_U + T
                for g in range(4):
                    cur = None
                    for s_ in range(g + 1):
                        sh = 1 << s_
                        lo = (1 << (s_ + 1)) - 1
                        dst_i = s_ % 2
                        if cur is None:
                            in_a = lambda e_lo, e_hi, g=g: upool[:, g, e_lo:e_hi]
                            in_buf = up_b[g]
                        else:
                            in_a = lambda e_lo, e_hi, ci=cur: ptmp[:, ci, e_lo:e_hi]
                            in_buf = pt_b[cur]
                        P.add(POOL, lambda e, in_a=in_a, dst_i=dst_i, lo=lo, sh=sh: e.tensor_tensor(
                            out=ptmp[:, dst_i, lo:L], in0=in_a(lo, L), in1=in_a(lo - sh, L - sh), op=ALU.add),
                            reads=[in_buf], writes=[pt_b[dst_i]])
                        cur = dst_i
                    wdw = float(2 ** (g + 1))
                    P.add(DVE, lambda e, g=g, cur=cur, wdw=wdw: e.scalar_tensor_tensor(
                        out=pooled[:, g, :], in0=ptmp[:, cur, HIST_U:L], scalar=1.0 / wdw, in1=upool[:, g, HIST_U:L],
                        op0=ALU.mult, op1=ALU.subtract),
                        reads=[pt_b[cur], up_b[g]], writes=[pl_b[g]])
                    if tile == 0:
                        P.add(POOL, lambda e, g=g, cur=cur: e.tensor_tensor(
                            out=ptmp[:, cur, HIST_U:2 * HIST_U], in0=ptmp[:, cur, HIST_U:2 * HIST_U], in1=invc[:, g, :], op=ALU.mult),
                            reads=[pt_b[cur], const_b], writes=[pt_b[cur]])
                        P.add(POOL, lambda e, g=g, cur=cur: e.tensor_tensor(
                            out=pooled[:, g, 0:HIST_U], in0=ptmp[:, cur, HIST_U:2 * HIST_U], in1=upool[:, g, HIST_U:2 * HIST_U], op=ALU.subtract),
                            reads=[pt_b[cur], up_b[g], pl_b[g]], writes=[pl_b[g]])
                    P.add(POOL, lambda e, g=g: e.tensor_copy(out=upool[:, g, 0:HIST_U], in_=upool[:, g, T:T + HIST_U]),
                          reads=[up_b[g]], writes=[up_b[g]])

            def fe_conv(tile):
                mu_t, mu_b = stat_bank(0)
                e2_t, e2_b = stat_bank(1)
                pm = []

                def t_pmap_all():
                    pw_t, pw_b = blockref(B_PW)
                    for g in range(4):
                        ps_t, ps_b = new_bank()
                        P.add(PE, lambda e, g=g, ps_t=ps_t: e.matmul(ps_t[:, :], pw_t[:, g * 128:(g + 1) * 128], pooled[:, g, :], start=True, stop=True),
                              reads=[pw_b, pl_b[g]], writes=[ps_b])
                        P.add(ACT, lambda e, g=g, ps_t=ps_t: e.activation(out=mixed[:, g, :], in_=ps_t[:, :], func=AF.Copy,
                                                                             scale=small[:, C_PSC + g:C_PSC + g + 1]),
                              reads=[ps_b, const_b], writes=[mx_b[g]])
                    ws.release(1)
                pm.append(t_pmap_all)

                th = []
                idxs = {}

                def t_tap(c, k):
                    src = vb[:, c, 2 + k:2 + k + T]
                    wcol = small[:, C_CW + c * CONV_K + k:C_CW + c * CONV_K + k + 1]
                    if k == 0:
                        P.add(DVE, lambda e, c=c: e.tensor_scalar(out=cc[:, c, :], in0=src, scalar1=wcol,
                                                                  scalar2=small[:, C_CB + c:C_CB + c + 1], op0=ALU.mult, op1=ALU.add),
                              reads=[vb_b[c], const_b], writes=[cc_b[c]])
                    else:
                        P.add(DVE, lambda e, c=c: e.scalar_tensor_tensor(out=cc[:, c, :], in0=src, scalar=wcol, in1=cc[:, c, :],
                                                                         op0=ALU.mult, op1=ALU.add),
                              reads=[vb_b[c], cc_b[c], const_b], writes=[cc_b[c]])
                    if k == CONV_K - 1:
                        P.add(POOL, lambda e, c=c: e.tensor_copy(out=vb[:, c, 0:HIST_V], in_=vb[:, c, T:T + HIST_V]),
                              reads=[vb_b[c]], writes=[vb_b[c]])
                for k in range(CONV_K):
                    for c in range(4):
                        th.append(lambda c=c, k=k: t_tap(c, k))
                taps = th
                th = list(pm)

                def t_cast(c):
                    i = rot("cb", 2)
                    idxs[c] = i
                    P.add(ACT, lambda e, c=c, i=i: e.activation(out=csq[:, i, :], in_=cc[:, c, :], func=AF.Square),
                          reads=[cc_b[c]], writes=[csq_b[i]])
                    P.add(DVE, lambda e, c=c, i=i: e.tensor_copy(out=cbf[:, i, :], in_=cc[:, c, :]),
                          reads=[cc_b[c]], writes=[cbf_b[i]])

                def t_stat(c):
                    i = idxs[c]
                    P.add(PE, lambda e, c=c, i=i: e.matmul(mu_t[:, :], onesC[:, :], cbf[:, i, :], start=(c == 0), stop=(c == 3)),
                          reads=[cbf_b[i], const_b], writes=[mu_b])
                    P.add(PE, lambda e, c=c, i=i: e.matmul(e2_t[:, :], onesC[:, :], csq[:, i, :], start=(c == 0), stop=(c == 3)),
                          reads=[csq_b[i], const_b], writes=[e2_b])
                for c in range(4):
                    th.append(lambda c=c: t_cast(c))
                    if c >= 1:
                        th.append(lambda c=c: t_stat(c - 1))
                th.append(lambda: t_stat(3))
                th.append(lambda: P.add(ACT, lambda e: e.activation(out=lnt[:, 0, :], in_=mu_t[:, :], func=AF.Copy), reads=[mu_b], writes=[ln_b[0]]))
                th.append(lambda: P.add(POOL, lambda e: e.tensor_tensor(out=lnt[:, 1, :], in0=lnt[:, 0, :], in1=lnt[:, 0, :], op=ALU.mult),
                                        reads=[ln_b[0]], writes=[ln_b[1]]))
                th.append(lambda: P.add(DVE, lambda e: e.scalar_tensor_tensor(out=lnt[:, 1, :], in0=e2_t[:, :], scalar=EPS, in1=lnt[:, 1, :],
                                                                              op0=ALU.add, op1=ALU.subtract),
                                        reads=[e2_b, ln_b[1]], writes=[ln_b[1]]))
                th.append(lambda: P.add(ACT, lambda e: e.activation(out=lnt[:, 1, :], in_=lnt[:, 1, :], func=AF.Ln), reads=[ln_b[1]], writes=[ln_b[1]]))
                th.append(lambda: P.add(ACT, lambda e: e.activation(out=lnt[:, 1, :], in_=lnt[:, 1, :], func=AF.Exp, scale=-0.5), reads=[ln_b[1]], writes=[ln_b[1]]))

                def t_sub(c):
                    P.add(DVE, lambda e, c=c: e.tensor_tensor(out=cc[:, c, :], in0=cc[:, c, :], in1=lnt[:, 0, :], op=ALU.subtract),
                          reads=[cc_b[c], ln_b[0]], writes=[cc_b[c]])

                def t_mul(c):
                    P.add(DVE, lambda e, c=c: e.tensor_tensor(out=cc[:, c, :], in0=cc[:, c, :], in1=lnt[:, 1, :], op=ALU.mult),
                          reads=[cc_b[c], ln_b[1]], writes=[cc_b[c]])

                def t_act(c):
                    i = rot("sg", 2)
                    idxs[("s", c)] = i
                    P.add(ACT, lambda e, c=c, i=i: e.activation(out=sg[:, i, :], in_=cc[:, c, :], func=AF.Sigmoid,
                                                                   bias=small[:, C_LB + c:C_LB + c + 1], scale=small[:, C_LG + c:C_LG + c + 1]),
                          reads=[cc_b[c], const_b], writes=[sg_b[i]])

                def t_aff(c):
                    i = idxs[("s", c)]
                    P.add(DVE, lambda e, c=c: e.tensor_scalar(out=cc[:, c, :], in0=cc[:, c, :], scalar1=small[:, C_LG + c:C_LG + c + 1],
                                                              scalar2=small[:, C_LB + c:C_LB + c + 1], op0=ALU.mult, op1=ALU.add),
                          reads=[cc_b[c], const_b, sg_b[i]], writes=[cc_b[c]])

                def t_out(c):
                    i = idxs[("s", c)]
                    P.add(POOL, lambda e, c=c, i=i: e.tensor_tensor(out=convact[:, c, :], in0=cc[:, c, :], in1=sg[:, i, :], op=ALU.mult),
                          reads=[cc_b[c], sg_b[i]], writes=[ca_b[c]])
                for c in range(4):
                    th.append(lambda c=c: t_sub(c))
                for c in range(4):
                    th.append(lambda c=c: t_mul(c))
                for c0 in (0, 2):
                    for c in (c0, c0 + 1):
                        th.append(lambda c=c: t_act(c))
                    for c in (c0, c0 + 1):
                        th.append(lambda c=c: t_aff(c))
                    for c in (c0, c0 + 1):
                        th.append(lambda c=c: t_out(c))
                return taps, th

            def mix_E(tile, jps, bg=None, nd=2):
                for jp in jps:
                    ga = blockref(B_IN + 6 + jp)
                    po = blockref(B_PO + jp)
                    gb = blockref(B_IN + 10 + jp)
                    co = blockref(B_CO + jp)
                    for jj in range(2):
                        j = jp * 2 + jj
                        pga_t, pga_b = proj_group(ga[0], ga[1], jj * 1024, NKC, hrhs, h_b)
                        pA_t, pA_b = proj_group(po[0], po[1], jj * 512, 4, lambda kc: mixed[:, kc, :], mx_b)
                        pgb_t, pgb_b = proj_group(gb[0], gb[1], jj * 1024, NKC, hrhs, h_b)
                        pB_t, pB_b = proj_group(co[0], co[1], jj * 512, 4, lambda kc: convact[:, kc, :], ca_b)
                        ia = rot("mt", NMT)
                        ib = rot("mt", NMT)
                        P.add(ACT, lambda e, ia=ia, pga_t=pga_t: e.activation(out=mtmp[:, ia, :], in_=pga_t[:, :], func=AF.Sigmoid),
                              reads=[pga_b], writes=[mt_b[ia]])
                        P.add(ACT, lambda e, ib=ib, pgb_t=pgb_t: e.activation(out=mtmp[:, ib, :], in_=pgb_t[:, :], func=AF.Sigmoid),
                              reads=[pgb_b], writes=[mt_b[ib]])
                        P.add(DVE, lambda e, ia=ia, pA_t=pA_t: e.tensor_tensor(out=mtmp[:, ia, :], in0=pA_t[:, :], in1=mtmp[:, ia, :], op=ALU.mult),
                              reads=[pA_b, mt_b[ia]], writes=[mt_b[ia]])
                        P.add(DVE, lambda e, ib=ib, pB_t=pB_t: e.tensor_tensor(out=mtmp[:, ib, :], in0=pB_t[:, :], in1=mtmp[:, ib, :], op=ALU.mult),
                              reads=[pB_b, mt_b[ib]], writes=[mt_b[ib]])
                        P.add(POOL, lambda e, ia=ia, ib=ib, j=j: e.tensor_tensor(out=merged_ap(j), in0=mtmp[:, ia, :], in1=mtmp[:, ib, :], op=ALU.add),
                              reads=[mt_b[ia], mt_b[ib]], writes=[mg_b[j]])
                        if bg is not None:
                            drain(bg, nd)
                    ws.release(4)

            def mix_wo(tile, bg=None):
                wo_slots = [blockref(B_WO + i) for i in range(4)]

                def y_wo(jo):
                    st_, sb_ = wo_slots[jo // 2]
                    r_ = proj_group(st_, sb_, (jo % 2) * 1024, NKC, merged_ap, mg_b)
                    if jo % 2 == 1:
                        ws.release(1)
                    return r_
                return post_groups(y_wo, bg)

            def mlp_up(tile, blks, bg=None):
                for blk in blks:
                    st_, sb_ = blockref(B_UP + blk)
                    for jj in range(2):
                        if bg is not None:
                            drain(bg, 4)
                        jf = blk * 2 + jj
                        ps_t, ps_b = proj_group(st_, sb_, jj * 1024, NKC, h2rhs, h2_b)
                        i = rot("mt", NMT)
                        P.add(ACT, lambda e, i=i, ps_t=ps_t: e.activation(out=mtmp[:, i, :], in_=ps_t[:, :], func=AF.Relu),
                              reads=[ps_b], writes=[mt_b[i]])
                        P.add(ACT, lambda e, i=i, jf=jf: e.activation(out=fbuf[:, jf, :], in_=mtmp[:, i, :], func=AF.Square),
                              reads=[mt_b[i]], writes=[f_b[jf]])
                    ws.release(1)

            def mlp_down(tile, bg=None):
                def y_dn(jo):
                    ps_t, ps_b = new_bank()
                    for half in range(2):
                        if bg is not None:
                            drain(bg[0], 8)
                            if jo * 2 + half >= 6:
                                drain(bg[1], 5)
                        st_, sb_ = blockref(B_DN + jo * 2 + half)
                        for kk in range(16):
                            kc = half * 16 + kk
                            P.add(PE, lambda e, kk=kk, kc=kc, st_=st_, ps_t=ps_t: e.matmul(
                                ps_t[:, :], st_[:, kk * 128:(kk + 1) * 128], fbuf[:, kc, :], start=(kc == 0), stop=(kc == NFC - 1)),
                                reads=[sb_, f_b[kc]], writes=[ps_b])
                        ws.release(1)
                    return ps_t, ps_b
                return post_groups(y_dn)

            def flush(bg):
                while bg:
                    bg.pop(0)()

            ws.start()
            load_x(0)
            ir = norm_stats(0)
            norm_apply(0, ir, C_NMP, h, h_b)
            fe_proj(0)
            fe_pool(0)
            if n_tiles > 1:
                preconvert([B_UP + i for i in range(16)] + [B_DN + i for i in range(16)])
            taps, lnth = fe_conv(0)
            flush(taps)
            flush(lnth)
            bg_post = []
            for tile in range(n_tiles):
                b = tile % 2
                nxt = tile + 1 if tile + 1 < n_tiles else None
                overlap = nxt is not None
                mix_E(tile, (0, 1), bg_post, 2)
                flush(bg_post)
                if tile >= 1:
                    store_x(tile - 1)
                    if nxt is not None:
                        load_x(nxt)
                mix_E(tile, (2, 3))
                if tile == 0 and nxt is not None:
                    load_x(nxt)
                bg_h = []
                if overlap:
                    irn = norm_stats(nxt % 2)
                    norm_apply(nxt % 2, irn, C_NMP, h, h_b, bg_h)
                ir1 = mix_wo(tile, bg_h)
                flush(bg_h)
                post_chain(b, ir1, C_NMPOST)
                if overlap:
                    fe_proj(nxt)
                ir2 = norm_stats(b)
                norm_apply(b, ir2, C_NLP, h2, h2_b)
                if overlap:
                    fe_pool(nxt)
                bgs = fe_conv(nxt) if overlap else ([], [])
                mlp_up(tile, range(16), bgs[0])
                ir3 = mlp_down(tile, bgs)
                flush(bgs[0])
                flush(bgs[1])
                bg_post = []
                post_chain(b, ir3, C_NLPOST, bg_post)
            flush(bg_post)
            store_x(n_tiles - 1)
            return None

        rec = WS()
        emit(Prog(), rec)
        ws = WS(order=list(rec.rec))
        P = Prog()
        info = emit(P, ws)
        assert ws.acq == len(ws.order)
        P.finalize()
        final_waits = []
        for bb in range(2):
            n = P.dma_n.get(id(x_st[bb]), 0)
            if n:
                final_waits.append((x_st[bb], 16 * n))

        block = es.enter_context(nc.Block())

        @block.sync
        def _(eng):
            P.run(SP, eng, sems, final_waits)

        @block.tensor
        def _(eng):
            P.run(PE, eng, sems)

        @block.scalar
        def _(eng):
            P.run(ACT, eng, sems)

        @block.vector
        def _(eng):
            P.run(DVE, eng, sems)

        @block.gpsimd
        def _(eng):
            P.run(POOL, eng, sems)

    return nc


def _img(Wm):
    K, N = Wm.shape
    nk, nn = K // 128, N // 128
    return np.ascontiguousarray(Wm.reshape(nk, 128, nn, 128).transpose(1, 2, 0, 3).reshape(128, nn * nk * 128))


def _cols(v, n):
    return np.ascontiguousarray(np.asarray(v, np.float32).reshape(n, 128).T)


def pack_weights(w_in, pool_w, w_pool_out, w_conv_out, w_o, w_up, w_down):
    wimg = np.zeros((128, NB_W * SLOTW), np.float32)
    wimg[:, B_IN * SLOTW:B_IN * SLOTW + 28672] = _img(w_in)
    wimg[:, B_PW * SLOTW:B_PW * SLOTW + 512] = np.asarray(pool_w).transpose(1, 0, 2).reshape(128, 512)
    ipo, ico = _img(w_pool_out), _img(w_conv_out)
    for q in range(4):
        wimg[:, (B_PO + q) * SLOTW:(B_PO + q) * SLOTW + 1024] = ipo[:, q * 1024:(q + 1) * 1024]
        wimg[:, (B_CO + q) * SLOTW:(B_CO + q) * SLOTW + 1024] = ico[:, q * 1024:(q + 1) * 1024]
    wimg[:, B_WO * SLOTW:B_WO * SLOTW + 8192] = _img(w_o)
    wimg[:, B_UP * SLOTW:B_UP * SLOTW + 32768] = _img(w_up)
    wimg[:, B_DN * SLOTW:B_DN * SLOTW + 32768] = _img(w_down)
    return wimg


def pack_small(norm_mix_pre, norm_mix_post, norm_mlp_pre, norm_mlp_post, pool_scale, conv_b, conv_ln_g, conv_ln_b, conv_w):
    small = np.zeros((128, NSMALL), np.float32)
    small[:, C_NMP:C_NMP + 8] = _cols(norm_mix_pre, 8)
    small[:, C_NMPOST:C_NMPOST + 8] = _cols(norm_mix_post, 8)
    small[:, C_NLP:C_NLP + 8] = _cols(norm_mlp_pre, 8)
    small[:, C_NLPOST:C_NLPOST + 8] = _cols(norm_mlp_post, 8)
    small[:, C_PSC:C_PSC + 4] = _cols(pool_scale, 4)
    small[:, C_CB:C_CB + 4] = _cols(conv_b, 4)
    small[:, C_LG:C_LG + 4] = _cols(conv_ln_g, 4)
    small[:, C_LB:C_LB + 4] = _cols(conv_ln_b, 4)
    cw = np.asarray(conv_w, np.float32).reshape(CONV_K, 4, 128)
    small[:, C_CW:C_CW + 4 * CONV_K] = cw.transpose(2, 1, 0).reshape(128, 4 * CONV_K)
    return small


def kernel(x, norm_mix_pre, w_in, pool_w, pool_scale, w_pool_out, conv_w, conv_b,
           conv_ln_g, conv_ln_b, w_conv_out, w_o, norm_mix_post, norm_mlp_pre,
           w_up, w_down, norm_mlp_post):
    x = np.asarray(x, np.float32)
    B = x.shape[0]
    f = lambda a: np.asarray(a, np.float32)[0]
    wimg = pack_weights(f(w_in), f(pool_w), f(w_pool_out), f(w_conv_out), f(w_o), f(w_up), f(w_down))
    small = pack_small(f(norm_mix_pre), f(norm_mix_post), f(norm_mlp_pre), f(norm_mlp_post), f(pool_scale),
                       f(conv_b), f(conv_ln_g), f(conv_ln_b), f(conv_w))
    nc = build_nc(x.shape[1] // T)
    in_maps = [{"xT": np.ascontiguousarray(x[b].T), "wimg": wimg, "small": small} for b in range(B)]
    res = run_bass_kernel_spmd(nc, in_maps, core_ids=list(range(B)))
    out = np.stack([np.ascontiguousarray(res.results[b]["outT"].T) for b in range(B)], axis=0)
    return out.astype(np.float32)
```

```python
import numpy as np
from contextlib import ExitStack
import concourse.bass as bass
import concourse.mybir as mybir
from concourse.bass_utils import run_bass_kernel_spmd

F32 = mybir.dt.float32
BF16 = mybir.dt.bfloat16
I32 = mybir.dt.int32
AF = mybir.ActivationFunctionType
ALU = mybir.AluOpType

D = 1024
SEQ = 4096
T = 512
NKC = 8
DFF = 4096
NFC = 32
EPS = 1e-6
CONV_K = 31
HIST_U = 16
HIST_V = 32
NSLOT = 8
SLOTW = 2048

PE, ACT, DVE, POOL, SP = "pe", "act", "dve", "pool", "sp"
ENGINES = (PE, ACT, DVE, POOL, SP)

C_NMP, C_NMPOST, C_NLP, C_NLPOST = 0, 8, 16, 24
C_PSC, C_CB, C_LG, C_LB, C_CW = 32, 36, 40, 44, 48
NSMALL = C_CW + 4 * CONV_K

B_IN, B_PW, B_PO, B_CO, B_WO, B_UP, B_DN = 0, 14, 15, 19, 23, 27, 43
NB_W = 59
NB_ALL = 59


def blk_ncols(blk):
    if blk == B_PW:
        return 512
    if B_PO <= blk < B_WO:
        return 1024
    return SLOTW


class Buf:
    __slots__ = ("name", "last_w", "readers")

    def __init__(self, name):
        self.name = name
        self.last_w = None
        self.readers = []


class Op:
    __slots__ = ("eng", "fn", "pos", "waits", "signals", "dma_sem", "dma_val", "count")


class Prog:
    def __init__(self):
        self.ops = {e: [] for e in ENGINES}
        self.waited = {e: {} for e in ENGINES}
        self.dma_n = {}

    def add(self, eng, fn, reads=(), writes=(), dma_sem=None):
        op = Op()
        op.eng, op.fn, op.pos = eng, fn, len(self.ops[eng])
        op.waits, op.signals, op.dma_sem, op.dma_val, op.count = [], False, dma_sem, 0, 0
        deps = []
        for b in reads:
            if b.last_w is not None:
                deps.append(b.last_w)
        for b in writes:
            if b.last_w is not None:
                deps.append(b.last_w)
            deps.extend(b.readers)
        w = self.waited[eng]
        best = {}
        dma_deps = []
        for d in deps:
            if d.dma_sem is not None:
                dma_deps.append(d)
            elif d.eng not in best or best[d.eng].pos < d.pos:
                best[d.eng] = d
        deps = dma_deps + list(best.values())
        for d in deps:
            if d.dma_sem is not None:
                key = ("dma", id(d.dma_sem))
                if w.get(key, 0) >= d.dma_val:
                    continue
                w[key] = d.dma_val
                op.waits.append(d)
            else:
                if d.eng == eng and eng in (PE, SP):
                    continue
                if w.get(d.eng, -1) >= d.pos:
                    continue
                w[d.eng] = d.pos
                d.signals = True
                op.waits.append(d)
        if dma_sem is not None:
            n = self.dma_n.get(id(dma_sem), 0) + 1
            self.dma_n[id(dma_sem)] = n
            op.dma_val = 16 * n
        for b in reads:
            b.readers.append(op)
        for b in writes:
            b.last_w = op
            b.readers = []
        self.ops[eng].append(op)
        return op

    def finalize(self):
        for e in ENGINES:
            c = 0
            for op in self.ops[e]:
                if op.signals and op.dma_sem is None:
                    c += 1
                op.count = c

    def run(self, eng_name, eng, sems, final_waits=()):
        for op in self.ops[eng_name]:
            for d in op.waits:
                if d.dma_sem is not None:
                    eng.wait_ge(d.dma_sem, d.dma_val)
                else:
                    eng.wait_ge(sems[d.eng], d.count)
            ins = op.fn(eng)
            if op.dma_sem is not None:
                ins.then_inc(op.dma_sem, 16)
            elif op.signals:
                ins.then_inc(sems[eng_name], 1)
        for sem, val in final_waits:
            eng.wait_ge(sem, val)


class WS:
    def __init__(self, order=None):
        self.order = order
        self.rec = []
        self.next_load = 0
        self.acq = 0
        self.emit_load = None

    def acquire(self, blk):
        if self.order is None:
            self.rec.append(blk)
            return (len(self.rec) - 1) % NSLOT
        sq_n = self.acq
        assert self.order[sq_n] == blk, (sq_n, self.order[sq_n], blk)
        assert sq_n < self.next_load, (sq_n, self.next_load)
        self.acq += 1
        return sq_n % NSLOT

    def start(self):
        if self.order is None:
            return
        while self.next_load < min(NSLOT, len(self.order)):
            self.emit_load(self.next_load)
            self.next_load += 1

    def release(self, n=1):
        if self.order is None:
            return
        for _ in range(n):
            if self.next_load < len(self.order):
                self.emit_load(self.next_load)
                self.next_load += 1


def build_nc(n_tiles=SEQ // T):
    seq = n_tiles * T
    nc = bass.Bass("TRN2", target_bir_lowering=False)
    xT = nc.dram_tensor("xT", [D, seq], F32, kind="ExternalInput").ap()
    wimg = nc.dram_tensor("wimg", [128, NB_W * SLOTW], F32, kind="ExternalInput").ap()
    smalld = nc.dram_tensor("small", [128, NSMALL], F32, kind="ExternalInput").ap()
    outT = nc.dram_tensor("outT", [D, seq], F32, kind="ExternalOutput").ap()
    wscr = nc.dram_tensor("wscr", [128, NB_ALL * SLOTW], BF16, kind="Internal").ap()
    xT3 = xT.rearrange("(kc p) t -> p kc t", p=128)
    outT3 = outT.rearrange("(kc p) t -> p kc t", p=128)

    with ExitStack() as es:
        def sb(name, shape, dt):
            return es.enter_context(nc.sbuf_tensor(name, shape, dt))

        def sem(name):
            return es.enter_context(nc.semaphore(name))

        xb = [sb(f"xb{i}", [128, NKC, T], F32) for i in range(2)]
        h = sb("h", [128, NKC, T], BF16)
        h2 = sb("h2", [128, NKC, T], BF16)
        NSQ = 4
        sq = sb("sq", [128, NSQ, T], BF16)
        ybuf = sb("ybuf", [128, NKC, T], F32)
        NST = 4
        stt = sb("stt", [128, NST, T], F32)
        onesD = sb("onesD", [128, 128], BF16)
        onesC = sb("onesC", [128, 128], BF16)
        small = sb("smallsb", [128, NSMALL], F32)
        epsT = sb("epsT", [128, 1], F32)
        invci = sb("invci", [128, HIST_U], I32)
        invc = sb("invc", [128, 4, HIST_U], F32)
        fix16 = sb("fix16", [128, 4, HIST_U], F32)
        ring = [sb(f"ring{i}", [128, SLOTW], BF16) for i in range(NSLOT)]
        upool = sb("upool", [128, 4, HIST_U + T], F32)
        ptmp = sb("ptmp", [128, 2, HIST_U + T], F32)
        pooled = sb("pooled", [128, 4, T], BF16)
        mixed = sb("mixed", [128, 4, T], BF16)
        vb = sb("vb", [128, 4, HIST_V + T], BF16)
        sg = sb("sg", [128, 2, T], F32)
        cc = sb("cc", [128, 4, T], F32)
        cbf = sb("cbf", [128, 2, T], BF16)
        csq = sb("csq", [128, 2, T], BF16)
        lnt = sb("lnt", [128, 2, T], F32)
        convact = sb("convact", [128, 4, T], BF16)
        NMT = 4
        mtmp = sb("mtmp", [128, NMT, T], F32)
        fbuf = sb("fbuf", [128, NFC, T], BF16)
        MG0 = NFC - NKC

        banks = [es.enter_context(nc.psum_tensor(f"ps{i}", [128, T], F32)) for i in range(8)]

        sems = {e: sem(f"s_{e}") for e in (PE, ACT, DVE, POOL)}
        ring_ld = [sem(f"rl{i}") for i in range(NSLOT)]
        ring_st = [sem(f"rs{i}") for i in range(NSLOT)]
        stage_ld = [sem(f"sl{i}") for i in range(4)]
        x_ld = [sem(f"xl{i}") for i in range(2)]
        x_st = [sem(f"xs{i}") for i in range(2)]
        small_ld = sem("smld")
        pc_st = [sem(f"pc{i}") for i in range(2)]

        def emit(P, ws):
            xb_b = [[Buf(f"xb{i}_{k}") for k in range(NKC)] for i in range(2)]
            h_b = [Buf(f"h{k}") for k in range(NKC)]
            h2_b = [Buf(f"h2{k}") for k in range(NKC)]
            sq_b = [Buf(f"sq{k}") for k in range(NSQ)]
            y_b = [Buf(f"y{k}") for k in range(NKC)]
            st_b = [Buf(f"st{k}") for k in range(NST)]
            const_b = Buf("consts")
            fix_b = Buf("fix16")
            ring_b = [Buf(f"ring{i}") for i in range(NSLOT)]
            up_b = [Buf(f"up{g}") for g in range(4)]
            pt_b = [Buf("ptA"), Buf("ptB")]
            pl_b = [Buf(f"pl{g}") for g in range(4)]
            mx_b = [Buf(f"mx{g}") for g in range(4)]
            vb_b = [Buf(f"vb{c}") for c in range(4)]
            sg_b = [Buf("sg0"), Buf("sg1")]
            cc_b = [Buf(f"cc{c}") for c in range(4)]
            cbf_b = [Buf("cbf0"), Buf("cbf1")]
            csq_b = [Buf("csq0"), Buf("csq1")]
            ln_b = [Buf("ln0"), Buf("ln1")]
            ca_b = [Buf(f"ca{c}") for c in range(4)]
            mt_b = [Buf(f"mt{k}") for k in range(NMT)]
            f_b = [Buf(f"f{k}") for k in range(NFC)]
            bank_b = [Buf(f"ps{i}") for i in range(8)]
            scr_b = [Buf(f"scr{i}") for i in range(NB_ALL)]
            bank_i = [0]
            rr = {}

            def rot(key, n):
                i = rr.get(key, 0)
                rr[key] = i + 1
                return i % n

            def new_bank():
                i = bank_i[0] % 5
                bank_i[0] += 1
                return banks[i], bank_b[i]

            def stat_bank(i):
                return banks[6 + i], bank_b[6 + i]

            def rms_bank():
                return banks[5], bank_b[5]

            merged_ap = lambda kc: fbuf[:, MG0 + kc, :]
            mg_b = f_b[MG0:]

            P.add(SP, lambda e: e.dma_start(out=small[:, :], in_=smalld[:, :]), writes=[const_b], dma_sem=small_ld)
            P.add(POOL, lambda e: e.memset(onesD[:, :], 1.0 / D), writes=[const_b])
            P.add(POOL, lambda e: e.memset(onesC[:, :], 1.0 / 512.0), writes=[const_b])
            P.add(POOL, lambda e: e.memset(epsT[:, :], EPS), writes=[const_b])
            P.add(POOL, lambda e: e.iota(invci[:, :], pattern=[[1, HIST_U]], base=1, channel_multiplier=0), writes=[const_b])
            for g in range(4):
                wdw = float(2 ** (g + 1))
                P.add(DVE, lambda e, g=g, wdw=wdw: e.tensor_single_scalar(out=invc[:, g, :], in_=invci[:, :], scalar=wdw, op=ALU.min),
                      reads=[const_b], writes=[const_b])
            P.add(DVE, lambda e: e.reciprocal(out=invc[:, :, :], in_=invc[:, :, :]), reads=[const_b], writes=[const_b])
            for g in range(4):
                P.add(POOL, lambda e, g=g: e.memset(upool[:, g, 0:HIST_U], 0.0), writes=[up_b[g]])
                P.add(POOL, lambda e, g=g: e.memset(vb[:, g, 0:HIST_V], 0.0), writes=[vb_b[g]])

            converted = set()
            pend_cast = []

            stage_mode = ["xb1"]

            def stage_ap(s):
                if s < 2:
                    return xb[1][:, s * 4:s * 4 + 4, :]
                return ybuf[:, (s - 2) * 4:(s - 2) * 4 + 4, :]

            def stage_bufs(s):
                if s < 2:
                    return xb_b[1][s * 4:s * 4 + 4]
                return y_b[(s - 2) * 4:(s - 2) * 4 + 4]

            def set_stage_mode(m):
                pend_cast.clear()
                stage_mode[0] = m

            def emit_stage_load(blk):
                assert stage_mode[0] in ("xb1", "ybuf"), (stage_mode[0], blk)
                s = rot("stage", 2) + (0 if stage_mode[0] == "xb1" else 2)
                src = wimg[:, blk * SLOTW:(blk + 1) * SLOTW].rearrange("p (a b) -> p a b", b=T)
                P.add(SP, lambda e, s=s, src=src: e.dma_start(out=stage_ap(s), in_=src),
                      writes=stage_bufs(s), dma_sem=stage_ld[s])
                return s

            def next_unconverted_weight(from_seq):
                for q in range(from_seq, len(ws.order)):
                    b_ = ws.order[q]
                    if b_ < NB_W and b_ not in converted and all(b_ != pb for pb, _ in pend_cast):
                        return b_
                return None

            def emit_load(sq_n):
                blk = ws.order[sq_n]
                slot = sq_n % NSLOT
                if blk in converted:
                    ncols = blk_ncols(blk)
                    P.add(SP, lambda e, slot=slot, blk=blk, ncols=ncols: e.dma_start(
                        out=ring[slot][:, 0:ncols], in_=wscr[:, blk * SLOTW:blk * SLOTW + ncols]),
                        reads=[scr_b[blk]], writes=[ring_b[slot]], dma_sem=ring_ld[slot])
                    return
                converted.add(blk)
                if True:
                    if pend_cast and pend_cast[0][0] == blk:
                        _, s = pend_cast.pop(0)
                    else:
                        s = emit_stage_load(blk)
                    if not pend_cast:
                        nb = next_unconverted_weight(sq_n + 1)
                        if nb is not None:
                            pend_cast.append((nb, emit_stage_load(nb)))
                    dst = ring[slot][:, :].rearrange("p (a b) -> p a b", b=T)
                    if rot("cast", 3) == 2:
                        P.add(ACT, lambda e, s=s, dst=dst: e.activation(out=dst, in_=stage_ap(s), func=AF.Copy),
                              reads=stage_bufs(s), writes=[ring_b[slot]])
                    else:
                        P.add(DVE, lambda e, s=s, dst=dst: e.tensor_copy(out=dst, in_=stage_ap(s)),
                              reads=stage_bufs(s), writes=[ring_b[slot]])
                P.add(SP, lambda e, slot=slot, blk=blk: e.dma_start(out=wscr[:, blk * SLOTW:(blk + 1) * SLOTW], in_=ring[slot][:, :]),
                      reads=[ring_b[slot]], writes=[scr_b[blk]], dma_sem=ring_st[slot])

            def preconvert(blks):
                pend_cast.clear()
                st_of = {}
                for n in range(min(2, len(blks))):
                    st_of[n] = emit_stage_load(blks[n])
                for n, blk in enumerate(blks):
                    s_ = st_of[n]
                    r = n % 2
                    bb = f_b[r * 4:(r + 1) * 4]
                    dst = fbuf[:, r * 4:(r + 1) * 4, :]
                    P.add(ACT, lambda e, s_=s_, dst=dst: e.activation(out=dst, in_=stage_ap(s_), func=AF.Copy),
                          reads=stage_bufs(s_), writes=bb)
                    P.add(SP, lambda e, blk=blk, dst=dst: e.dma_start(
                        out=wscr[:, blk * SLOTW:(blk + 1) * SLOTW].rearrange("p (a b) -> p a b", b=T), in_=dst),
                        reads=bb, writes=[scr_b[blk]], dma_sem=pc_st[r])
                    converted.add(blk)
                    if n + 2 < len(blks):
                        st_of[n + 2] = emit_stage_load(blks[n + 2])

            ws.emit_load = emit_load

            def blockref(blk):
                s = ws.acquire(blk)
                return ring[s], ring_b[s]

            def load_x(tile):
                b = tile % 2
                P.add(SP, lambda e, b=b, tile=tile: e.dma_start(out=xb[b][:, :, :], in_=xT3[:, :, tile * T:(tile + 1) * T]),
                      writes=xb_b[b], dma_sem=x_ld[b])

            def store_x(tile):
                b = tile % 2
                P.add(SP, lambda e, b=b, tile=tile: e.dma_start(out=outT3[:, :, tile * T:(tile + 1) * T], in_=xb[b][:, :, :]),
                      reads=xb_b[b], dma_sem=x_st[b])

            def rms_rstd(ps_t, ps_b):
                i1 = rot("st", NST)
                P.add(ACT, lambda e, i1=i1, ps_t=ps_t: e.activation(out=stt[:, i1, :], in_=ps_t[:, :], func=AF.Ln, bias=epsT[:, 0:1]),
                      reads=[ps_b, const_b], writes=[st_b[i1]])
                i2 = rot("st", NST)
                P.add(ACT, lambda e, i1=i1, i2=i2: e.activation(out=stt[:, i2, :], in_=stt[:, i1, :], func=AF.Exp, scale=-0.5),
                      reads=[st_b[i1]], writes=[st_b[i2]])
                return i2

            def norm_stats(b):
                ps_t, ps_b = rms_bank()
                for kc in range(NKC):
                    i = rot("sq", NSQ)
                    P.add(ACT, lambda e, i=i, kc=kc: e.activation(out=sq[:, i, :], in_=xb[b][:, kc, :], func=AF.Square),
                          reads=[xb_b[b][kc]], writes=[sq_b[i]])
                    P.add(PE, lambda e, i=i, kc=kc, ps_t=ps_t: e.matmul(ps_t[:, :], onesD[:, :], sq[:, i, :], start=(kc == 0), stop=(kc == NKC - 1)),
                          reads=[sq_b[i], const_b], writes=[ps_b])
                return rms_rstd(ps_t, ps_b)

            def norm_apply(b, ir, gcol, ht, hb, bg=None):
                for kc in range(NKC):
                    if bg is not None:
                        bg.append(lambda kc=kc: norm_chunk(b, ir, gcol, ht, hb, kc))
                    else:
                        norm_chunk(b, ir, gcol, ht, hb, kc)

            def norm_chunk(b, ir, gcol, ht, hb, kc):
                if True:
                    P.add(DVE, lambda e, kc=kc: e.scalar_tensor_tensor(
                        out=ht[:, kc, :], in0=xb[b][:, kc, :], scalar=small[:, gcol + kc:gcol + kc + 1], in1=stt[:, ir, :],
                        op0=ALU.mult, op1=ALU.mult),
                        reads=[xb_b[b][kc], st_b[ir], const_b], writes=[hb[kc]])

            def proj_group(slot_t, slot_b, col0, nk, rhs_fn, rhs_bufs):
                ps_t, ps_b = new_bank()
                for kc in range(nk):
                    P.add(PE, lambda e, kc=kc, ps_t=ps_t: e.matmul(
                        ps_t[:, :], slot_t[:, col0 + kc * 128:col0 + (kc + 1) * 128], rhs_fn(kc), start=(kc == 0), stop=(kc == nk - 1)),
                        reads=[slot_b, rhs_bufs[kc]], writes=[ps_b])
                return ps_t, ps_b

            def drain(bg, n=1):
                for _ in range(n):
                    if bg:
                        bg.pop(0)()

            def post_groups(yps, bg=None):
                st_t, st_bk = rms_bank()
                pend = None
                for jo in range(NKC):
                    if bg is not None and jo >= 1:
                        drain(bg, 1)
                    ps_t, ps_b = yps(jo)
                    i = rot("sq", NSQ)
                    P.add(ACT, lambda e, jo=jo, ps_t=ps_t: e.activation(out=ybuf[:, jo, :], in_=ps_t[:, :], func=AF.Copy),
                          reads=[ps_b], writes=[y_b[jo]])
                    P.add(ACT, lambda e, i=i, jo=jo: e.activation(out=sq[:, i, :], in_=ybuf[:, jo, :], func=AF.Square),
                          reads=[y_b[jo]], writes=[sq_b[i]])
                    if pend is not None:
                        pi, pj = pend
                        P.add(PE, lambda e, pi=pi, pj=pj: e.matmul(st_t[:, :], onesD[:, :], sq[:, pi, :], start=(pj == 0), stop=False),
                              reads=[sq_b[pi], const_b], writes=[st_bk])
                    pend = (i, jo)
                pi, pj = pend
                P.add(PE, lambda e, pi=pi, pj=pj: e.matmul(st_t[:, :], onesD[:, :], sq[:, pi, :], start=False, stop=True),
                      reads=[sq_b[pi], const_b], writes=[st_bk])
                return rms_rstd(st_t, st_bk)

            def post_chain(b, ir, gcol, bg=None):
                for jo in range(NKC):
                    if bg is not None:
                        bg.append(lambda jo=jo: post_chunk(b, ir, gcol, jo))
                    else:
                        post_chunk(b, ir, gcol, jo)

            def post_chunk(b, ir, gcol, jo):
                if True:
                    P.add(DVE, lambda e, jo=jo: e.scalar_tensor_tensor(
                        out=ybuf[:, jo, :], in0=ybuf[:, jo, :], scalar=small[:, gcol + jo:gcol + jo + 1], in1=stt[:, ir, :],
                        op0=ALU.mult, op1=ALU.mult),
                        reads=[y_b[jo], st_b[ir], const_b], writes=[y_b[jo]])
                    P.add(POOL, lambda e, jo=jo: e.tensor_tensor(out=xb[b][:, jo, :], in0=xb[b][:, jo, :], in1=ybuf[:, jo, :], op=ALU.add),
                          reads=[xb_b[b][jo], y_b[jo]], writes=[xb_b[b][jo]])

            hrhs = lambda kc: h[:, kc, :]
            h2rhs = lambda kc: h2[:, kc, :]

            def fe_proj(tile):
                slots_in = [blockref(B_IN + i) for i in range(6)]
                for j in range(4):
                    st_, sb_ = slots_in[j // 2]
                    ps_t, ps_b = proj_group(st_, sb_, (j % 2) * 1024, NKC, hrhs, h_b)
                    P.add(ACT, lambda e, j=j, ps_t=ps_t: e.activation(out=upool[:, j, HIST_U:HIST_U + T], in_=ps_t[:, :], func=AF.Copy),
                          reads=[ps_b], writes=[up_b[j]])
                ws.release(2)
                for jj in range(4):
                    jg, ja = 8 + jj, 4 + jj
                    st_, sb_ = slots_in[jg // 2]
                    psg_t, psg_b = proj_group(st_, sb_, (jg % 2) * 1024, NKC, hrhs, h_b)
                    st_, sb_ = slots_in[ja // 2]
                    psa_t, psa_b = proj_group(st_, sb_, (ja % 2) * 1024, NKC, hrhs, h_b)
                    i = rot("sg", 2)
                    P.add(ACT, lambda e, i=i, psg_t=psg_t: e.activation(out=sg[:, i, :], in_=psg_t[:, :], func=AF.Sigmoid),
                          reads=[psg_b], writes=[sg_b[i]])
                    P.add(DVE, lambda e, i=i, jj=jj, psa_t=psa_t: e.tensor_tensor(
                        out=vb[:, jj, HIST_V:HIST_V + T], in0=psa_t[:, :], in1=sg[:, i, :], op=ALU.mult),
                        reads=[psa_b, sg_b[i]], writes=[vb_b[jj]])
                ws.release(4)

            def fe_pool(tile):
                L = HIST_U + T
                for g in range(4):
                    cur = None
                    for s_ in range(g + 1):
                        sh = 1 << s_
                        lo = (1 << (s_ + 1)) - 1
                        dst_i = s_ % 2
                        if cur is None:
                            in_a = lambda e_lo, e_hi, g=g: upool[:, g, e_lo:e_hi]
                            in_buf = up_b[g]
                        else:
                            in_a = lambda e_lo, e_hi, ci=cur: ptmp[:, ci, e_lo:e_hi]
                            in_buf = pt_b[cur]
                        P.add(POOL, lambda e, in_a=in_a, dst_i=dst_i, lo=lo, sh=sh: e.tensor_tensor(
                            out=ptmp[:, dst_i, lo:L], in0=in_a(lo, L), in1=in_a(lo - sh, L - sh), op=ALU.add),
                            reads=[in_buf], writes=[pt_b[dst_i]])
                        cur = dst_i
                    wdw = float(2 ** (g + 1))
                    P.add(DVE, lambda e, g=g, cur=cur, wdw=wdw: e.scalar_tensor_tensor(
                        out=pooled[:, g, :], in0=ptmp[:, cur, HIST_U:L], scalar=1.0 / wdw, in1=upool[:, g, HIST_U:L],
                        op0=ALU.mult, op1=ALU.subtract),
                        reads=[pt_b[cur], up_b[g]], writes=[pl_b[g]])
                    if tile == 0:
                        P.add(POOL, lambda e, g=g, cur=cur: e.tensor_tensor(
                            out=ptmp[:, cur, HIST_U:2 * HIST_U], in0=ptmp[:, cur, HIST_U:2 * HIST_U], in1=invc[:, g, :], op=ALU.mult),
                            reads=[pt_b[cur], const_b], writes=[pt_b[cur]])
                        P.add(POOL, lambda e, g=g, cur=cur: e.tensor_tensor(
                            out=pooled[:, g, 0:HIST_U], in0=ptmp[:, cur, HIST_U:2 * HIST_U], in1=upool[:, g, HIST_U:2 * HIST_U], op=ALU.subtract),
                            reads=[pt_b[cur], up_b[g], pl_b[g]], writes=[pl_b[g]])
                    P.add(POOL, lambda e, g=g: e.tensor_copy(out=upool[:, g, 0:HIST_U], in_=upool[:, g, T:T + HIST_U]),
                          reads=[up_b[g]], writes=[up_b[g]])

            def fe_conv(tile):
                mu_t, mu_b = stat_bank(0)
                e2_t, e2_b = stat_bank(1)
                pm = []

                def t_pmap_all():
                    pw_t, pw_b = blockref(B_PW)
                    for g in range(4):
                        ps_t, ps_b = new_bank()
                        P.add(PE, lambda e, g=g, ps_t=ps_t: e.matmul(ps_t[:, :], pw_t[:, g * 128:(g + 1) * 128], pooled[:, g, :], start=True, stop=True),
                              reads=[pw_b, pl_b[g]], writes=[ps_b])
                        P.add(ACT, lambda e, g=g, ps_t=ps_t: e.activation(out=mixed[:, g, :], in_=ps_t[:, :], func=AF.Copy,
                                                                             scale=small[:, C_PSC + g:C_PSC + g + 1]),
                              reads=[ps_b, const_b], writes=[mx_b[g]])
                    ws.release(1)
                pm.append(t_pmap_all)

                th = []
                idxs = {}

                def t_tap(c, k):
                    src = vb[:, c, 2 + k:2 + k + T]
                    wcol = small[:, C_CW + c * CONV_K + k:C_CW + c * CONV_K + k + 1]
                    if k == 0:
                        P.add(DVE, lambda e, c=c: e.tensor_scalar(out=cc[:, c, :], in0=src, scalar1=wcol,
                                                                  scalar2=small[:, C_CB + c:C_CB + c + 1], op0=ALU.mult, op1=ALU.add),
                              reads=[vb_b[c], const_b], writes=[cc_b[c]])
                    else:
                        P.add(DVE, lambda e, c=c: e.scalar_tensor_tensor(out=cc[:, c, :], in0=src, scalar=wcol, in1=cc[:, c, :],
                                                                         op0=ALU.mult, op1=ALU.add),
                              reads=[vb_b[c], cc_b[c], const_b], writes=[cc_b[c]])
                    if k == CONV_K - 1:
                        P.add(POOL, lambda e, c=c: e.tensor_copy(out=vb[:, c, 0:HIST_V], in_=vb[:, c, T:T + HIST_V]),
                              reads=[vb_b[c]], writes=[vb_b[c]])
                for k in range(CONV_K):
                    for c in range(4):
                        th.append(lambda c=c, k=k: t_tap(c, k))
                taps = th
                th = list(pm)

                def t_cast(c):
                    i = rot("cb", 2)
                    idxs[c] = i
                    P.add(ACT, lambda e, c=c, i=i: e.activation(out=csq[:, i, :], in_=cc[:, c, :], func=AF.Square),
                          reads=[cc_b[c]], writes=[csq_b[i]])
                    P.add(DVE, lambda e, c=c, i=i: e.tensor_copy(out=cbf[:, i, :], in_=cc[:, c, :]),
                          reads=[cc_b[c]], writes=[cbf_b[i]])

                def t_stat(c):
                    i = idxs[c]
                    P.add(PE, lambda e, c=c, i=i: e.matmul(mu_t[:, :], onesC[:, :], cbf[:, i, :], start=(c == 0), stop=(c == 3)),
                          reads=[cbf_b[i], const_b], writes=[mu_b])
                    P.add(PE, lambda e, c=c, i=i: e.matmul(e2_t[:, :], onesC[:, :], csq[:, i, :], start=(c == 0), stop=(c == 3)),
                          reads=[csq_b[i], const_b], writes=[e2_b])
                for c in range(4):
                    th.append(lambda c=c: t_cast(c))
                    if c >= 1:
                        th.append(lambda c=c: t_stat(c - 1))
                th.append(lambda: t_stat(3))
                th.append(lambda: P.add(ACT, lambda e: e.activation(out=lnt[:, 0, :], in_=mu_t[:, :], func=AF.Copy), reads=[mu_b], writes=[ln_b[0]]))
                th.append(lambda: P.add(POOL, lambda e: e.tensor_tensor(out=lnt[:, 1, :], in0=lnt[:, 0, :], in1=lnt[:, 0, :], op=ALU.mult),
                                        reads=[ln_b[0]], writes=[ln_b[1]]))
                th.append(lambda: P.add(DVE, lambda e: e.scalar_tensor_tensor(out=lnt[:, 1, :], in0=e2_t[:, :], scalar=EPS, in1=lnt[:, 1, :],
                                                                              op0=ALU.add, op1=ALU.subtract),
                                        reads=[e2_b, ln_b[1]], writes=[ln_b[1]]))
                th.append(lambda: P.add(ACT, lambda e: e.activation(out=lnt[:, 1, :], in_=lnt[:, 1, :], func=AF.Ln), reads=[ln_b[1]], writes=[ln_b[1]]))
                th.append(lambda: P.add(ACT, lambda e: e.activation(out=lnt[:, 1, :], in_=lnt[:, 1, :], func=AF.Exp, scale=-0.5), reads=[ln_b[1]], writes=[ln_b[1]]))

                def t_sub(c):
                    P.add(DVE, lambda e, c=c: e.tensor_tensor(out=cc[:, c, :], in0=cc[:, c, :], in1=lnt[:, 0, :], op=ALU.subtract),
                          reads=[cc_b[c], ln_b[0]], writes=[cc_b[c]])

                def t_mul(c):
                    P.add(DVE, lambda e, c=c: e.tensor_tensor(out=cc[:, c, :], in0=cc[:, c, :], in1=lnt[:, 1, :], op=ALU.mult),
                          reads=[cc_b[c], ln_b[1]], writes=[cc_b[c]])

                def t_act(c):
                    i = rot("sg", 2)
                    idxs[("s", c)] = i
                    P.add(ACT, lambda e, c=c, i=i: e.activation(out=sg[:, i, :], in_=cc[:, c, :], func=AF.Sigmoid,
                                                                   bias=small[:, C_LB + c:C_LB + c + 1], scale=small[:, C_LG + c:C_LG + c + 1]),
                          reads=[cc_b[c], const_b], writes=[sg_b[i]])

                def t_aff(c):
                    i = idxs[("s", c)]
                    P.add(DVE, lambda e, c=c: e.tensor_scalar(out=cc[:, c, :], in0=cc[:, c, :], scalar1=small[:, C_LG + c:C_LG + c + 1],
                                                              scalar2=small[:, C_LB + c:C_LB + c + 1], op0=ALU.mult, op1=ALU.add),
                          reads=[cc_b[c], const_b, sg_b[i]], writes=[cc_b[c]])

                def t_out(c):
                    i = idxs[("s", c)]
                    P.add(POOL, lambda e, c=c, i=i: e.tensor_tensor(out=convact[:, c, :], in0=cc[:, c, :], in1=sg[:, i, :], op=ALU.mult),
                          reads=[cc_b[c], sg_b[i]], writes=[ca_b[c]])
                for c in range(4):
                    th.append(lambda c=c: t_sub(c))
                for c in range(4):
                    th.append(lambda c=c: t_mul(c))
                for c0 in (0, 2):
                    for c in (c0, c0 + 1):
                        th.append(lambda c=c: t_act(c))
                    for c in (c0, c0 + 1):
                        th.append(lambda c=c: t_aff(c))
                    for c in (c0, c0 + 1):
                        th.append(lambda c=c: t_out(c))
                return taps, th

            def mix_E(tile, jps, bg=None, nd=2):
                for jp in jps:
                    ga = blockref(B_IN + 6 + jp)
                    po = blockref(B_PO + jp)
                    gb = blockref(B_IN + 10 + jp)
                    co = blockref(B_CO + jp)
                    for jj in range(2):
                        j = jp * 2 + jj
                        pga_t, pga_b = proj_group(ga[0], ga[1], jj * 1024, NKC, hrhs, h_b)
                        pA_t, pA_b = proj_group(po[0], po[1], jj * 512, 4, lambda kc: mixed[:, kc, :], mx_b)
                        pgb_t, pgb_b = proj_group(gb[0], gb[1], jj * 1024, NKC, hrhs, h_b)
                        pB_t, pB_b = proj_group(co[0], co[1], jj * 512, 4, lambda kc: convact[:, kc, :], ca_b)
                        ia = rot("mt", NMT)
                        ib = rot("mt", NMT)
                        P.add(ACT, lambda e, ia=ia, pga_t=pga_t: e.activation(out=mtmp[:, ia, :], in_=pga_t[:, :], func=AF.Sigmoid),
                              reads=[pga_b], writes=[mt_b[ia]])
                        P.add(ACT, lambda e, ib=ib, pgb_t=pgb_t: e.activation(out=mtmp[:, ib, :], in_=pgb_t[:, :], func=AF.Sigmoid),
                              reads=[pgb_b], writes=[mt_b[ib]])
                        P.add(DVE, lambda e, ia=ia, pA_t=pA_t: e.tensor_tensor(out=mtmp[:, ia, :], in0=pA_t[:, :], in1=mtmp[:, ia, :], op=ALU.mult),
                              reads=[pA_b, mt_b[ia]], writes=[mt_b[ia]])
                        P.add(DVE, lambda e, ib=ib, pB_t=pB_t: e.tensor_tensor(out=mtmp[:, ib, :], in0=pB_t[:, :], in1=mtmp[:, ib, :], op=ALU.mult),
                              reads=[pB_b, mt_b[ib]], writes=[mt_b[ib]])
                        P.add(POOL, lambda e, ia=ia, ib=ib, j=j: e.tensor_tensor(out=merged_ap(j), in0=mtmp[:, ia, :], in1=mtmp[:, ib, :], op=ALU.add),
                              reads=[mt_b[ia], mt_b[ib]], writes=[mg_b[j]])
                        if bg is not None:
                            drain(bg, nd)
                    ws.release(4)

            def mix_wo(tile, bg=None):
                wo_slots = [blockref(B_WO + i) for i in range(4)]

                def y_wo(jo):
                    st_, sb_ = wo_slots[jo // 2]
                    r_ = proj_group(st_, sb_, (jo % 2) * 1024, NKC, merged_ap, mg_b)
                    if jo % 2 == 1:
                        ws.release(1)
                    return r_
                return post_groups(y_wo, bg)

            def mlp_up(tile, blks, bg=None):
                for blk in blks:
                    st_, sb_ = blockref(B_UP + blk)
                    for jj in range(2):
                        if bg is not None:
                            drain(bg, 4)
                        jf = blk * 2 + jj
                        ps_t, ps_b = proj_group(st_, sb_, jj * 1024, NKC, h2rhs, h2_b)
                        i = rot("mt", NMT)
                        P.add(ACT, lambda e, i=i, ps_t=ps_t: e.activation(out=mtmp[:, i, :], in_=ps_t[:, :], func=AF.Relu),
                              reads=[ps_b], writes=[mt_b[i]])
                        P.add(ACT, lambda e, i=i, jf=jf: e.activation(out=fbuf[:, jf, :], in_=mtmp[:, i, :], func=AF.Square),
                              reads=[mt_b[i]], writes=[f_b[jf]])
                    ws.release(1)

            def mlp_down(tile, bg=None):
                def y_dn(jo):
                    ps_t, ps_b = new_bank()
                    for half in range(2):
                        if bg is not None:
                            drain(bg[0], 8)
                            if jo * 2 + half >= 9:
                                drain(bg[1], 6)
                        st_, sb_ = blockref(B_DN + jo * 2 + half)
                        for kk in range(16):
                            kc = half * 16 + kk
                            P.add(PE, lambda e, kk=kk, kc=kc, st_=st_, ps_t=ps_t: e.matmul(
                                ps_t[:, :], st_[:, kk * 128:(kk + 1) * 128], fbuf[:, kc, :], start=(kc == 0), stop=(kc == NFC - 1)),
                                reads=[sb_, f_b[kc]], writes=[ps_b])
                        ws.release(1)
                    return ps_t, ps_b
                return post_groups(y_dn)

            def flush(bg):
                while bg:
                    bg.pop(0)()

            load_x(0)
            ws.start()
            ir = norm_stats(0)
            norm_apply(0, ir, C_NMP, h, h_b)
            fe_proj(0)
            fe_pool(0)
            if n_tiles > 1:
                preconvert([B_DN + i for i in range(16)] + [B_UP + i for i in range(2)])
            taps, lnth = fe_conv(0)
            flush(taps)
            flush(lnth)
            bg_post = []
            for tile in range(n_tiles):
                b = tile % 2
                nxt = tile + 1 if tile + 1 < n_tiles else None
                overlap = nxt is not None
                mix_E(tile, (0, 1), bg_post, 2)
                flush(bg_post)
                if tile >= 1:
                    store_x(tile - 1)
                    if nxt is not None:
                        load_x(nxt)
                mix_E(tile, (2, 3))
                if tile == 0 and nxt is not None:
                    set_stage_mode("none")
                    load_x(nxt)
                bg_h = []
                if overlap:
                    irn = norm_stats(nxt % 2)
                    norm_apply(nxt % 2, irn, C_NMP, h, h_b, bg_h)
                ir1 = mix_wo(tile, bg_h)
                flush(bg_h)
                post_chain(b, ir1, C_NMPOST)
                if tile == 0 and nxt is not None:
                    set_stage_mode("ybuf")
                bgs = ([], [])
                if overlap:
                    fe_proj(nxt)
                    bgs = fe_conv(nxt)
                    drain(bgs[0], 12)
                ir2 = norm_stats(b)
                norm_apply(b, ir2, C_NLP, h2, h2_b)
                if overlap:
                    fe_pool(nxt)
                mlp_up(tile, range(16), bgs[0])
                ir3 = mlp_down(tile, bgs)
                flush(bgs[0])
                flush(bgs[1])
                bg_post = []
                post_chain(b, ir3, C_NLPOST, bg_post)
            flush(bg_post)
            store_x(n_tiles - 1)
            return None

        rec = WS()
        emit(Prog(), rec)
        ws = WS(order=list(rec.rec))
        P = Prog()
        info = emit(P, ws)
        assert ws.acq == len(ws.order)
        P.finalize()
        final_waits = []
        for bb in range(2):
            n = P.dma_n.get(id(x_st[bb]), 0)
            if n:
                final_waits.append((x_st[bb], 16 * n))

        block = es.enter_context(nc.Block())

        @block.sync
        def _(eng):
            P.run(SP, eng, sems, final_waits)

        @block.tensor
        def _(eng):
            P.run(PE, eng, sems)

        @block.scalar
        def _(eng):
            P.run(ACT, eng, sems)

        @block.vector
        def _(eng):
            P.run(DVE, eng, sems)

        @block.gpsimd
        def _(eng):
            P.run(POOL, eng, sems)

    return nc


def _img(Wm):
    K, N = Wm.shape
    nk, nn = K // 128, N // 128
    return np.ascontiguousarray(Wm.reshape(nk, 128, nn, 128).transpose(1, 2, 0, 3).reshape(128, nn * nk * 128))


def _cols(v, n):
    return np.ascontiguousarray(np.asarray(v, np.float32).reshape(n, 128).T)


def pack_weights(w_in, pool_w, w_pool_out, w_conv_out, w_o, w_up, w_down):
    wimg = np.zeros((128, NB_W * SLOTW), np.float32)
    wimg[:, B_IN * SLOTW:B_IN * SLOTW + 28672] = _img(w_in)
    wimg[:, B_PW * SLOTW:B_PW * SLOTW + 512] = np.asarray(pool_w).transpose(1, 0, 2).reshape(128, 512)
    ipo, ico = _img(w_pool_out), _img(w_conv_out)
    for q in range(4):
        wimg[:, (B_PO + q) * SLOTW:(B_PO + q) * SLOTW + 1024] = ipo[:, q * 1024:(q + 1) * 1024]
        wimg[:, (B_CO + q) * SLOTW:(B_CO + q) * SLOTW + 1024] = ico[:, q * 1024:(q + 1) * 1024]
    wimg[:, B_WO * SLOTW:B_WO * SLOTW + 8192] = _img(w_o)
    wimg[:, B_UP * SLOTW:B_UP * SLOTW + 32768] = _img(w_up)
    wimg[:, B_DN * SLOTW:B_DN * SLOTW + 32768] = _img(w_down)
    return wimg


def pack_small(norm_mix_pre, norm_mix_post, norm_mlp_pre, norm_mlp_post, pool_scale, conv_b, conv_ln_g, conv_ln_b, conv_w):
    small = np.zeros((128, NSMALL), np.float32)
    small[:, C_NMP:C_NMP + 8] = _cols(norm_mix_pre, 8)
    small[:, C_NMPOST:C_NMPOST + 8] = _cols(norm_mix_post, 8)
    small[:, C_NLP:C_NLP + 8] = _cols(norm_mlp_pre, 8)
    small[:, C_NLPOST:C_NLPOST + 8] = _cols(norm_mlp_post, 8)
    small[:, C_PSC:C_PSC + 4] = _cols(pool_scale, 4)
    small[:, C_CB:C_CB + 4] = _cols(conv_b, 4)
    small[:, C_LG:C_LG + 4] = _cols(conv_ln_g, 4)
    small[:, C_LB:C_LB + 4] = _cols(conv_ln_b, 4)
    cw = np.asarray(conv_w, np.float32).reshape(CONV_K, 4, 128)
    small[:, C_CW:C_CW + 4 * CONV_K] = cw.transpose(2, 1, 0).reshape(128, 4 * CONV_K)
    return small


def kernel(x, norm_mix_pre, w_in, pool_w, pool_scale, w_pool_out, conv_w, conv_b,
           conv_ln_g, conv_ln_b, w_conv_out, w_o, norm_mix_post, norm_mlp_pre,
           w_up, w_down, norm_mlp_post):
    x = np.asarray(x, np.float32)
    B = x.shape[0]
    f = lambda a: np.asarray(a, np.float32)[0]
    wimg = pack_weights(f(w_in), f(pool_w), f(w_pool_out), f(w_conv_out), f(w_o), f(w_up), f(w_down))
    small = pack_small(f(norm_mix_pre), f(norm_mix_post), f(norm_mlp_pre), f(norm_mlp_post), f(pool_scale),
                       f(conv_b), f(conv_ln_g), f(conv_ln_b), f(conv_w))
    nc = build_nc(x.shape[1] // T)
    in_maps = [{"xT": np.ascontiguousarray(x[b].T), "wimg": wimg, "small": small} for b in range(B)]
    res = run_bass_kernel_spmd(nc, in_maps, core_ids=list(range(B)))
    out = np.stack([np.ascontiguousarray(res.results[b]["outT"].T) for b in range(B)], axis=0)
    return out.astype(np.float32)
```

```python
import numpy as np
from contextlib import ExitStack
import concourse.bass as bass
import concourse.mybir as mybir
from concourse.bass_utils import run_bass_kernel_spmd

F32 = mybir.dt.float32
BF16 = mybir.dt.bfloat16
I32 = mybir.dt.int32
AF = mybir.ActivationFunctionType
ALU = mybir.AluOpType

D = 1024
SEQ = 4096
T = 512
NKC = 8
DFF = 4096
NFC = 32
EPS = 1e-6
CONV_K = 31
HIST_U = 16
HIST_V = 32
NSLOT = 8
SLOTW = 2048

PE, ACT, DVE, POOL, SP = "pe", "act", "dve", "pool", "sp"
ENGINES = (PE, ACT, DVE, POOL, SP)

C_NMP, C_NMPOST, C_NLP, C_NLPOST = 0, 8, 16, 24
C_PSC, C_CB, C_LG, C_LB, C_CW = 32, 36, 40, 44, 48
NSMALL = C_CW + 4 * CONV_K

B_IN, B_PW, B_PO, B_CO, B_WO, B_UP, B_DN = 0, 14, 15, 19, 23, 27, 43
NB_W = 59
NB_ALL = 59


def blk_ncols(blk):
    if blk == B_PW:
        return 512
    if B_PO <= blk < B_WO:
        return 1024
    return SLOTW


class Buf:
    __slots__ = ("name", "last_w", "readers")

    def __init__(self, name):
        self.name = name
        self.last_w = None
        self.readers = []


class Op:
    __slots__ = ("eng", "fn", "pos", "waits", "signals", "dma_sem", "dma_val", "count")


class Prog:
    def __init__(self):
        self.ops = {e: [] for e in ENGINES}
        self.waited = {e: {} for e in ENGINES}
        self.dma_n = {}

    def add(self, eng, fn, reads=(), writes=(), dma_sem=None):
        op = Op()
        op.eng, op.fn, op.pos = eng, fn, len(self.ops[eng])
        op.waits, op.signals, op.dma_sem, op.dma_val, op.count = [], False, dma_sem, 0, 0
        deps = []
        for b in reads:
            if b.last_w is not None:
                deps.append(b.last_w)
        for b in writes:
            if b.last_w is not None:
                deps.append(b.last_w)
            deps.extend(b.readers)
        w = self.waited[eng]
        best = {}
        dma_deps = []
        for d in deps:
            if d.dma_sem is not None:
                dma_deps.append(d)
            elif d.eng not in best or best[d.eng].pos < d.pos:
                best[d.eng] = d
        deps = dma_deps + list(best.values())
        for d in deps:
            if d.dma_sem is not None:
                key = ("dma", id(d.dma_sem))
                if w.get(key, 0) >= d.dma_val:
                    continue
                w[key] = d.dma_val
                op.waits.append(d)
            else:
                if d.eng == eng and eng in (PE, SP):
                    continue
                if w.get(d.eng, -1) >= d.pos:
                    continue
                w[d.eng] = d.pos
                d.signals = True
                op.waits.append(d)
        if dma_sem is not None:
            n = self.dma_n.get(id(dma_sem), 0) + 1
            self.dma_n[id(dma_sem)] = n
            op.dma_val = 16 * n
        for b in reads:
            b.readers.append(op)
        for b in writes:
            b.last_w = op
            b.readers = []
        self.ops[eng].append(op)
        return op

    def finalize(self):
        for e in ENGINES:
            c = 0
            for op in self.ops[e]:
                if op.signals and op.dma_sem is None:
                    c += 1
                op.count = c

    def run(self, eng_name, eng, sems, final_waits=()):
        for op in self.ops[eng_name]:
            for d in op.waits:
                if d.dma_sem is not None:
                    eng.wait_ge(d.dma_sem, d.dma_val)
                else:
                    eng.wait_ge(sems[d.eng], d.count)
            ins = op.fn(eng)
            if op.dma_sem is not None:
                ins.then_inc(op.dma_sem, 16)
            elif op.signals:
                ins.then_inc(sems[eng_name], 1)
        for sem, val in final_waits:
            eng.wait_ge(sem, val)


class WS:
    def __init__(self, order=None):
        self.order = order
        self.rec = []
        self.next_load = 0
        self.acq = 0
        self.emit_load = None

    def acquire(self, blk):
        if self.order is None:
            self.rec.append(blk)
            return (len(self.rec) - 1) % NSLOT
        sq_n = self.acq
        assert self.order[sq_n] == blk, (sq_n, self.order[sq_n], blk)
        assert sq_n < self.next_load, (sq_n, self.next_load)
        self.acq += 1
        return sq_n % NSLOT

    def start(self):
        if self.order is None:
            return
        while self.next_load < min(NSLOT, len(self.order)):
            self.emit_load(self.next_load)
            self.next_load += 1

    def release(self, n=1):
        if self.order is None:
            return
        for _ in range(n):
            if self.next_load < len(self.order):
                self.emit_load(self.next_load)
                self.next_load += 1


def build_nc(n_tiles=SEQ // T):
    seq = n_tiles * T
    nc = bass.Bass("TRN2", target_bir_lowering=False)
    xT = nc.dram_tensor("xT", [D, seq], F32, kind="ExternalInput").ap()
    wimg = nc.dram_tensor("wimg", [128, NB_W * SLOTW], F32, kind="ExternalInput").ap()
    smalld = nc.dram_tensor("small", [128, NSMALL], F32, kind="ExternalInput").ap()
    outT = nc.dram_tensor("outT", [D, seq], F32, kind="ExternalOutput").ap()
    wscr = nc.dram_tensor("wscr", [128, NB_ALL * SLOTW], BF16, kind="Internal").ap()
    xT3 = xT.rearrange("(kc p) t -> p kc t", p=128)
    outT3 = outT.rearrange("(kc p) t -> p kc t", p=128)

    with ExitStack() as es:
        def sb(name, shape, dt):
            return es.enter_context(nc.sbuf_tensor(name, shape, dt))

        def sem(name):
            return es.enter_context(nc.semaphore(name))

        xb = [sb(f"xb{i}", [128, NKC, T], F32) for i in range(2)]
        h = sb("h", [128, NKC, T], BF16)
        h2 = sb("h2", [128, NKC, T], BF16)
        NSQ = 4
        sq = sb("sq", [128, NSQ, T], BF16)
        ybuf = sb("ybuf", [128, NKC, T], F32)
        NST = 4
        stt = sb("stt", [128, NST, T], F32)
        onesD = sb("onesD", [128, 128], BF16)
        onesC = sb("onesC", [128, 128], BF16)
        small = sb("smallsb", [128, NSMALL], F32)
        epsT = sb("epsT", [128, 1], F32)
        invci = sb("invci", [128, HIST_U], I32)
        invc = sb("invc", [128, 4, HIST_U], F32)
        fix16 = sb("fix16", [128, 4, HIST_U], F32)
        ring = [sb(f"ring{i}", [128, SLOTW], BF16) for i in range(NSLOT)]
        upool = sb("upool", [128, 4, HIST_U + T], F32)
        ptmp = sb("ptmp", [128, 2, HIST_U + T], F32)
        pooled = sb("pooled", [128, 4, T], BF16)
        mixed = sb("mixed", [128, 4, T], BF16)
        vb = sb("vb", [128, 4, HIST_V + T], BF16)
        sg = sb("sg", [128, 2, T], F32)
        cc = sb("cc", [128, 4, T], F32)
        cbf = sb("cbf", [128, 2, T], BF16)
        csq = sb("csq", [128, 2, T], BF16)
        lnt = sb("lnt", [128, 2, T], F32)
        convact = sb("convact", [128, 4, T], BF16)
        NMT = 4
        mtmp = sb("mtmp", [128, NMT, T], F32)
        fbuf = sb("fbuf", [128, NFC, T], BF16)
        MG0 = NFC - NKC

        banks = [es.enter_context(nc.psum_tensor(f"ps{i}", [128, T], F32)) for i in range(8)]

        sems = {e: sem(f"s_{e}") for e in (PE, ACT, DVE, POOL)}
        ring_ld = [sem(f"rl{i}") for i in range(NSLOT)]
        ring_st = [sem(f"rs{i}") for i in range(NSLOT)]
        stage_ld = [sem(f"sl{i}") for i in range(4)]
        x_ld = [sem(f"xl{i}") for i in range(2)]
        x_st = [sem(f"xs{i}") for i in range(2)]
        small_ld = sem("smld")
        pc_st = [sem(f"pc{i}") for i in range(2)]

        def emit(P, ws):
            xb_b = [[Buf(f"xb{i}_{k}") for k in range(NKC)] for i in range(2)]
            h_b = [Buf(f"h{k}") for k in range(NKC)]
            h2_b = [Buf(f"h2{k}") for k in range(NKC)]
            sq_b = [Buf(f"sq{k}") for k in range(NSQ)]
            y_b = [Buf(f"y{k}") for k in range(NKC)]
            st_b = [Buf(f"st{k}") for k in range(NST)]
            const_b = Buf("consts")
            fix_b = Buf("fix16")
            ring_b = [Buf(f"ring{i}") for i in range(NSLOT)]
            up_b = [Buf(f"up{g}") for g in range(4)]
            pt_b = [Buf("ptA"), Buf("ptB")]
            pl_b = [Buf(f"pl{g}") for g in range(4)]
            mx_b = [Buf(f"mx{g}") for g in range(4)]
            vb_b = [Buf(f"vb{c}") for c in range(4)]
            sg_b = [Buf("sg0"), Buf("sg1")]
            cc_b = [Buf(f"cc{c}") for c in range(4)]
            cbf_b = [Buf("cbf0"), Buf("cbf1")]
            csq_b = [Buf("csq0"), Buf("csq1")]
            ln_b = [Buf("ln0"), Buf("ln1")]
            ca_b = [Buf(f"ca{c}") for c in range(4)]
            mt_b = [Buf(f"mt{k}") for k in range(NMT)]
            f_b = [Buf(f"f{k}") for k in range(NFC)]
            bank_b = [Buf(f"ps{i}") for i in range(8)]
            scr_b = [Buf(f"scr{i}") for i in range(NB_ALL)]
            bank_i = [0]
            rr = {}

            def rot(key, n):
                i = rr.get(key, 0)
                rr[key] = i + 1
                return i % n

            def new_bank():
                i = bank_i[0] % 5
                bank_i[0] += 1
                return banks[i], bank_b[i]

            def stat_bank(i):
                return banks[6 + i], bank_b[6 + i]

            def rms_bank():
                return banks[5], bank_b[5]

            merged_ap = lambda kc: fbuf[:, MG0 + kc, :]
            mg_b = f_b[MG0:]

            P.add(SP, lambda e: e.dma_start(out=small[:, :], in_=smalld[:, :]), writes=[const_b], dma_sem=small_ld)
            P.add(POOL, lambda e: e.memset(onesD[:, :], 1.0 / D), writes=[const_b])
            P.add(POOL, lambda e: e.memset(onesC[:, :], 1.0 / 512.0), writes=[const_b])
            P.add(POOL, lambda e: e.memset(epsT[:, :], EPS), writes=[const_b])
            P.add(POOL, lambda e: e.iota(invci[:, :], pattern=[[1, HIST_U]], base=1, channel_multiplier=0), writes=[const_b])
            for g in range(4):
                wdw = float(2 ** (g + 1))
                P.add(DVE, lambda e, g=g, wdw=wdw: e.tensor_single_scalar(out=invc[:, g, :], in_=invci[:, :], scalar=wdw, op=ALU.min),
                      reads=[const_b], writes=[const_b])
            P.add(DVE, lambda e: e.reciprocal(out=invc[:, :, :], in_=invc[:, :, :]), reads=[const_b], writes=[const_b])
            for g in range(4):
                P.add(POOL, lambda e, g=g: e.memset(upool[:, g, 0:HIST_U], 0.0), writes=[up_b[g]])
                P.add(POOL, lambda e, g=g: e.memset(vb[:, g, 0:HIST_V], 0.0), writes=[vb_b[g]])

            converted = set()
            pend_cast = []

            stage_mode = ["xb1"]

            def stage_ap(s):
                if s < 2:
                    return xb[1][:, s * 4:s * 4 + 4, :]
                return ybuf[:, (s - 2) * 4:(s - 2) * 4 + 4, :]

            def stage_bufs(s):
                if s < 2:
                    return xb_b[1][s * 4:s * 4 + 4]
                return y_b[(s - 2) * 4:(s - 2) * 4 + 4]

            def set_stage_mode(m):
                pend_cast.clear()
                stage_mode[0] = m

            def emit_stage_load(blk):
                assert stage_mode[0] in ("xb1", "ybuf"), (stage_mode[0], blk)
                if stage_mode[0] == "xb1":
                    s = rot("stage4", 4)
                else:
                    s = rot("stage", 2) + 2
                src = wimg[:, blk * SLOTW:(blk + 1) * SLOTW].rearrange("p (a b) -> p a b", b=T)
                P.add(SP, lambda e, s=s, src=src: e.dma_start(out=stage_ap(s), in_=src),
                      writes=stage_bufs(s), dma_sem=stage_ld[s])
                return s

            def next_unconverted_weight(from_seq):
                for q in range(from_seq, len(ws.order)):
                    b_ = ws.order[q]
                    if b_ < NB_W and b_ not in converted and all(b_ != pb for pb, _ in pend_cast):
                        return b_
                return None

            def emit_load(sq_n):
                blk = ws.order[sq_n]
                slot = sq_n % NSLOT
                if blk in converted:
                    ncols = blk_ncols(blk)
                    P.add(SP, lambda e, slot=slot, blk=blk, ncols=ncols: e.dma_start(
                        out=ring[slot][:, 0:ncols], in_=wscr[:, blk * SLOTW:blk * SLOTW + ncols]),
                        reads=[scr_b[blk]], writes=[ring_b[slot]], dma_sem=ring_ld[slot])
                    return
                converted.add(blk)
                if True:
                    if pend_cast and pend_cast[0][0] == blk:
                        _, s = pend_cast.pop(0)
                    else:
                        s = emit_stage_load(blk)
                    depth = 3 if stage_mode[0] == "xb1" else 1
                    while len(pend_cast) < depth:
                        nb = next_unconverted_weight(sq_n + 1)
                        if nb is None:
                            break
                        pend_cast.append((nb, emit_stage_load(nb)))
                    dst = ring[slot][:, :].rearrange("p (a b) -> p a b", b=T)
                    if rot("cast", 3) == 2:
                        P.add(ACT, lambda e, s=s, dst=dst: e.activation(out=dst, in_=stage_ap(s), func=AF.Copy),
                              reads=stage_bufs(s), writes=[ring_b[slot]])
                    else:
                        P.add(DVE, lambda e, s=s, dst=dst: e.tensor_copy(out=dst, in_=stage_ap(s)),
                              reads=stage_bufs(s), writes=[ring_b[slot]])
                P.add(SP, lambda e, slot=slot, blk=blk: e.dma_start(out=wscr[:, blk * SLOTW:(blk + 1) * SLOTW], in_=ring[slot][:, :]),
                      reads=[ring_b[slot]], writes=[scr_b[blk]], dma_sem=ring_st[slot])

            def preconvert(blks):
                pend_cast.clear()
                st_of = {}
                for n in range(min(2, len(blks))):
                    st_of[n] = emit_stage_load(blks[n])
                for n, blk in enumerate(blks):
                    s_ = st_of[n]
                    r = n % 2
                    bb = f_b[r * 4:(r + 1) * 4]
                    dst = fbuf[:, r * 4:(r + 1) * 4, :]
                    P.add(ACT, lambda e, s_=s_, dst=dst: e.activation(out=dst, in_=stage_ap(s_), func=AF.Copy),
                          reads=stage_bufs(s_), writes=bb)
                    P.add(SP, lambda e, blk=blk, dst=dst: e.dma_start(
                        out=wscr[:, blk * SLOTW:(blk + 1) * SLOTW].rearrange("p (a b) -> p a b", b=T), in_=dst),
                        reads=bb, writes=[scr_b[blk]], dma_sem=pc_st[r])
                    converted.add(blk)
                    if n + 2 < len(blks):
                        st_of[n + 2] = emit_stage_load(blks[n + 2])

            ws.emit_load = emit_load

            def blockref(blk):
                s = ws.acquire(blk)
                return ring[s], ring_b[s]

            def load_x(tile):
                b = tile % 2
                P.add(SP, lambda e, b=b, tile=tile: e.dma_start(out=xb[b][:, :, :], in_=xT3[:, :, tile * T:(tile + 1) * T]),
                      writes=xb_b[b], dma_sem=x_ld[b])

            def store_x(tile):
                b = tile % 2
                P.add(SP, lambda e, b=b, tile=tile: e.dma_start(out=outT3[:, :, tile * T:(tile + 1) * T], in_=xb[b][:, :, :]),
                      reads=xb_b[b], dma_sem=x_st[b])

            def rms_rstd(ps_t, ps_b):
                i1 = rot("st", NST)
                P.add(ACT, lambda e, i1=i1, ps_t=ps_t: e.activation(out=stt[:, i1, :], in_=ps_t[:, :], func=AF.Ln, bias=epsT[:, 0:1]),
                      reads=[ps_b, const_b], writes=[st_b[i1]])
                i2 = rot("st", NST)
                P.add(ACT, lambda e, i1=i1, i2=i2: e.activation(out=stt[:, i2, :], in_=stt[:, i1, :], func=AF.Exp, scale=-0.5),
                      reads=[st_b[i1]], writes=[st_b[i2]])
                return i2

            def norm_stats(b):
                ps_t, ps_b = rms_bank()
                for kc in range(NKC):
                    i = rot("sq", NSQ)
                    P.add(ACT, lambda e, i=i, kc=kc: e.activation(out=sq[:, i, :], in_=xb[b][:, kc, :], func=AF.Square),
                          reads=[xb_b[b][kc]], writes=[sq_b[i]])
                    P.add(PE, lambda e, i=i, kc=kc, ps_t=ps_t: e.matmul(ps_t[:, :], onesD[:, :], sq[:, i, :], start=(kc == 0), stop=(kc == NKC - 1)),
                          reads=[sq_b[i], const_b], writes=[ps_b])
                return rms_rstd(ps_t, ps_b)

            def norm_apply(b, ir, gcol, ht, hb, bg=None):
                for kc in range(NKC):
                    if bg is not None:
                        bg.append(lambda kc=kc: norm_chunk(b, ir, gcol, ht, hb, kc))
                    else:
                        norm_chunk(b, ir, gcol, ht, hb, kc)

            def norm_chunk(b, ir, gcol, ht, hb, kc):
                if True:
                    P.add(DVE, lambda e, kc=kc: e.scalar_tensor_tensor(
                        out=ht[:, kc, :], in0=xb[b][:, kc, :], scalar=small[:, gcol + kc:gcol + kc + 1], in1=stt[:, ir, :],
                        op0=ALU.mult, op1=ALU.mult),
                        reads=[xb_b[b][kc], st_b[ir], const_b], writes=[hb[kc]])

            def proj_group(slot_t, slot_b, col0, nk, rhs_fn, rhs_bufs):
                ps_t, ps_b = new_bank()
                for kc in range(nk):
                    P.add(PE, lambda e, kc=kc, ps_t=ps_t: e.matmul(
                        ps_t[:, :], slot_t[:, col0 + kc * 128:col0 + (kc + 1) * 128], rhs_fn(kc), start=(kc == 0), stop=(kc == nk - 1)),
                        reads=[slot_b, rhs_bufs[kc]], writes=[ps_b])
                return ps_t, ps_b

            def drain(bg, n=1):
                for _ in range(n):
                    if bg:
                        bg.pop(0)()

            def post_groups(yps, bg=None):
                st_t, st_bk = rms_bank()
                pend = None
                for jo in range(NKC):
                    if bg is not None and jo >= 1:
                        drain(bg, 1)
                    ps_t, ps_b = yps(jo)
                    i = rot("sq", NSQ)
                    P.add(ACT, lambda e, jo=jo, ps_t=ps_t: e.activation(out=ybuf[:, jo, :], in_=ps_t[:, :], func=AF.Copy),
                          reads=[ps_b], writes=[y_b[jo]])
                    P.add(ACT, lambda e, i=i, jo=jo: e.activation(out=sq[:, i, :], in_=ybuf[:, jo, :], func=AF.Square),
                          reads=[y_b[jo]], writes=[sq_b[i]])
                    if pend is not None:
                        pi, pj = pend
                        P.add(PE, lambda e, pi=pi, pj=pj: e.matmul(st_t[:, :], onesD[:, :], sq[:, pi, :], start=(pj == 0), stop=False),
                              reads=[sq_b[pi], const_b], writes=[st_bk])
                    pend = (i, jo)
                pi, pj = pend
                P.add(PE, lambda e, pi=pi, pj=pj: e.matmul(st_t[:, :], onesD[:, :], sq[:, pi, :], start=False, stop=True),
                      reads=[sq_b[pi], const_b], writes=[st_bk])
                return rms_rstd(st_t, st_bk)

            def post_chain(b, ir, gcol, bg=None):
                for jo in range(NKC):
                    if bg is not None:
                        bg.append(lambda jo=jo: post_chunk(b, ir, gcol, jo))
                    else:
                        post_chunk(b, ir, gcol, jo)

            def post_chunk(b, ir, gcol, jo):
                if True:
                    P.add(DVE, lambda e, jo=jo: e.scalar_tensor_tensor(
                        out=ybuf[:, jo, :], in0=ybuf[:, jo, :], scalar=small[:, gcol + jo:gcol + jo + 1], in1=stt[:, ir, :],
                        op0=ALU.mult, op1=ALU.mult),
                        reads=[y_b[jo], st_b[ir], const_b], writes=[y_b[jo]])
                    P.add(POOL, lambda e, jo=jo: e.tensor_tensor(out=xb[b][:, jo, :], in0=xb[b][:, jo, :], in1=ybuf[:, jo, :], op=ALU.add),
                          reads=[xb_b[b][jo], y_b[jo]], writes=[xb_b[b][jo]])

            hrhs = lambda kc: h[:, kc, :]
            h2rhs = lambda kc: h2[:, kc, :]

            def fe_proj(tile):
                slots_in = [blockref(B_IN + i) for i in range(6)]
                for j in range(4):
                    st_, sb_ = slots_in[j // 2]
                    ps_t, ps_b = proj_group(st_, sb_, (j % 2) * 1024, NKC, hrhs, h_b)
                    P.add(ACT, lambda e, j=j, ps_t=ps_t: e.activation(out=upool[:, j, HIST_U:HIST_U + T], in_=ps_t[:, :], func=AF.Copy),
                          reads=[ps_b], writes=[up_b[j]])
                ws.release(2)
                for jj in range(4):
                    jg, ja = 8 + jj, 4 + jj
                    st_, sb_ = slots_in[jg // 2]
                    psg_t, psg_b = proj_group(st_, sb_, (jg % 2) * 1024, NKC, hrhs, h_b)
                    st_, sb_ = slots_in[ja // 2]
                    psa_t, psa_b = proj_group(st_, sb_, (ja % 2) * 1024, NKC, hrhs, h_b)
                    i = rot("sg", 2)
                    P.add(ACT, lambda e, i=i, psg_t=psg_t: e.activation(out=sg[:, i, :], in_=psg_t[:, :], func=AF.Sigmoid),
                          reads=[psg_b], writes=[sg_b[i]])
                    P.add(DVE, lambda e, i=i, jj=jj, psa_t=psa_t: e.tensor_tensor(
                        out=vb[:, jj, HIST_V:HIST_V + T], in0=psa_t[:, :], in1=sg[:, i, :], op=ALU.mult),
                        reads=[psa_b, sg_b[i]], writes=[vb_b[jj]])
                ws.release(4)

            def fe_pool(tile):
                L = HIST_U + T
                for g in range(4):
                    cur = None
                    for s_ in range(g + 1):
                        sh = 1 << s_
                        lo = (1 << (s_ + 1)) - 1
                        dst_i = s_ % 2
                        if cur is None:
                            in_a = lambda e_lo, e_hi, g=g: upool[:, g, e_lo:e_hi]
                            in_buf = up_b[g]
                        else:
                            in_a = lambda e_lo, e_hi, ci=cur: ptmp[:, ci, e_lo:e_hi]
                            in_buf = pt_b[cur]
                        P.add(POOL, lambda e, in_a=in_a, dst_i=dst_i, lo=lo, sh=sh: e.tensor_tensor(
                            out=ptmp[:, dst_i, lo:L], in0=in_a(lo, L), in1=in_a(lo - sh, L - sh), op=ALU.add),
                            reads=[in_buf], writes=[pt_b[dst_i]])
                        cur = dst_i
                    wdw = float(2 ** (g + 1))
                    P.add(DVE, lambda e, g=g, cur=cur, wdw=wdw: e.scalar_tensor_tensor(
                        out=pooled[:, g, :], in0=ptmp[:, cur, HIST_U:L], scalar=1.0 / wdw, in1=upool[:, g, HIST_U:L],
                        op0=ALU.mult, op1=ALU.subtract),
                        reads=[pt_b[cur], up_b[g]], writes=[pl_b[g]])
                    if tile == 0:
                        P.add(POOL, lambda e, g=g, cur=cur: e.tensor_tensor(
                            out=ptmp[:, cur, HIST_U:2 * HIST_U], in0=ptmp[:, cur, HIST_U:2 * HIST_U], in1=invc[:, g, :], op=ALU.mult),
                            reads=[pt_b[cur], const_b], writes=[pt_b[cur]])
                        P.add(POOL, lambda e, g=g, cur=cur: e.tensor_tensor(
                            out=pooled[:, g, 0:HIST_U], in0=ptmp[:, cur, HIST_U:2 * HIST_U], in1=upool[:, g, HIST_U:2 * HIST_U], op=ALU.subtract),
                            reads=[pt_b[cur], up_b[g], pl_b[g]], writes=[pl_b[g]])
                    P.add(POOL, lambda e, g=g: e.tensor_copy(out=upool[:, g, 0:HIST_U], in_=upool[:, g, T:T + HIST_U]),
                          reads=[up_b[g]], writes=[up_b[g]])

            def fe_conv(tile):
                mu_t, mu_b = stat_bank(0)
                e2_t, e2_b = stat_bank(1)
                pm = []

                def t_pmap_all():
                    pw_t, pw_b = blockref(B_PW)
                    for g in range(4):
                        ps_t, ps_b = new_bank()
                        P.add(PE, lambda e, g=g, ps_t=ps_t: e.matmul(ps_t[:, :], pw_t[:, g * 128:(g + 1) * 128], pooled[:, g, :], start=True, stop=True),
                              reads=[pw_b, pl_b[g]], writes=[ps_b])
                        P.add(ACT, lambda e, g=g, ps_t=ps_t: e.activation(out=mixed[:, g, :], in_=ps_t[:, :], func=AF.Copy,
                                                                             scale=small[:, C_PSC + g:C_PSC + g + 1]),
                              reads=[ps_b, const_b], writes=[mx_b[g]])
                    ws.release(1)
                pm.append(t_pmap_all)

                th = []
                idxs = {}

                def t_tap(c, k):
                    src = vb[:, c, 2 + k:2 + k + T]
                    wcol = small[:, C_CW + c * CONV_K + k:C_CW + c * CONV_K + k + 1]
                    if k == 0:
                        P.add(DVE, lambda e, c=c: e.tensor_scalar(out=cc[:, c, :], in0=src, scalar1=wcol,
                                                                  scalar2=small[:, C_CB + c:C_CB + c + 1], op0=ALU.mult, op1=ALU.add),
                              reads=[vb_b[c], const_b], writes=[cc_b[c]])
                    else:
                        P.add(DVE, lambda e, c=c: e.scalar_tensor_tensor(out=cc[:, c, :], in0=src, scalar=wcol, in1=cc[:, c, :],
                                                                         op0=ALU.mult, op1=ALU.add),
                              reads=[vb_b[c], cc_b[c], const_b], writes=[cc_b[c]])
                    if k == CONV_K - 1:
                        P.add(POOL, lambda e, c=c: e.tensor_copy(out=vb[:, c, 0:HIST_V], in_=vb[:, c, T:T + HIST_V]),
                              reads=[vb_b[c]], writes=[vb_b[c]])
                for k in range(CONV_K):
                    for c in range(4):
                        th.append(lambda c=c, k=k: t_tap(c, k))
                taps = th
                th = list(pm)

                def t_cast(c):
                    i = rot("cb", 2)
                    idxs[c] = i
                    P.add(ACT, lambda e, c=c, i=i: e.activation(out=csq[:, i, :], in_=cc[:, c, :], func=AF.Square),
                          reads=[cc_b[c]], writes=[csq_b[i]])
                    P.add(DVE, lambda e, c=c, i=i: e.tensor_copy(out=cbf[:, i, :], in_=cc[:, c, :]),
                          reads=[cc_b[c]], writes=[cbf_b[i]])

                def t_stat(c):
                    i = idxs[c]
                    P.add(PE, lambda e, c=c, i=i: e.matmul(mu_t[:, :], onesC[:, :], cbf[:, i, :], start=(c == 0), stop=(c == 3)),
                          reads=[cbf_b[i], const_b], writes=[mu_b])
                    P.add(PE, lambda e, c=c, i=i: e.matmul(e2_t[:, :], onesC[:, :], csq[:, i, :], start=(c == 0), stop=(c == 3)),
                          reads=[csq_b[i], const_b], writes=[e2_b])
                for c in range(4):
                    th.append(lambda c=c: t_cast(c))
                    if c >= 1:
                        th.append(lambda c=c: t_stat(c - 1))
                th.append(lambda: t_stat(3))
                th.append(lambda: P.add(ACT, lambda e: e.activation(out=lnt[:, 0, :], in_=mu_t[:, :], func=AF.Copy), reads=[mu_b], writes=[ln_b[0]]))
                th.append(lambda: P.add(POOL, lambda e: e.tensor_tensor(out=lnt[:, 1, :], in0=lnt[:, 0, :], in1=lnt[:, 0, :], op=ALU.mult),
                                        reads=[ln_b[0]], writes=[ln_b[1]]))
                th.append(lambda: P.add(DVE, lambda e: e.scalar_tensor_tensor(out=lnt[:, 1, :], in0=e2_t[:, :], scalar=EPS, in1=lnt[:, 1, :],
                                                                              op0=ALU.add, op1=ALU.subtract),
                                        reads=[e2_b, ln_b[1]], writes=[ln_b[1]]))
                th.append(lambda: P.add(ACT, lambda e: e.activation(out=lnt[:, 1, :], in_=lnt[:, 1, :], func=AF.Ln), reads=[ln_b[1]], writes=[ln_b[1]]))
                th.append(lambda: P.add(ACT, lambda e: e.activation(out=lnt[:, 1, :], in_=lnt[:, 1, :], func=AF.Exp, scale=-0.5), reads=[ln_b[1]], writes=[ln_b[1]]))

                def t_sub(c):
                    P.add(DVE, lambda e, c=c: e.tensor_tensor(out=cc[:, c, :], in0=cc[:, c, :], in1=lnt[:, 0, :], op=ALU.subtract),
                          reads=[cc_b[c], ln_b[0]], writes=[cc_b[c]])

                def t_mul(c):
                    P.add(DVE, lambda e, c=c: e.tensor_tensor(out=cc[:, c, :], in0=cc[:, c, :], in1=lnt[:, 1, :], op=ALU.mult),
                          reads=[cc_b[c], ln_b[1]], writes=[cc_b[c]])

                def t_act(c):
                    i = rot("sg", 2)
                    idxs[("s", c)] = i
                    P.add(ACT, lambda e, c=c, i=i: e.activation(out=sg[:, i, :], in_=cc[:, c, :], func=AF.Sigmoid,
                                                                   bias=small[:, C_LB + c:C_LB + c + 1], scale=small[:, C_LG + c:C_LG + c + 1]),
                          reads=[cc_b[c], const_b], writes=[sg_b[i]])

                def t_aff(c):
                    i = idxs[("s", c)]
                    P.add(DVE, lambda e, c=c: e.tensor_scalar(out=cc[:, c, :], in0=cc[:, c, :], scalar1=small[:, C_LG + c:C_LG + c + 1],
                                                              scalar2=small[:, C_LB + c:C_LB + c + 1], op0=ALU.mult, op1=ALU.add),
                          reads=[cc_b[c], const_b, sg_b[i]], writes=[cc_b[c]])

                def t_out(c):
                    i = idxs[("s", c)]
                    P.add(POOL, lambda e, c=c, i=i: e.tensor_tensor(out=convact[:, c, :], in0=cc[:, c, :], in1=sg[:, i, :], op=ALU.mult),
                          reads=[cc_b[c], sg_b[i]], writes=[ca_b[c]])
                for c in range(4):
                    th.append(lambda c=c: t_sub(c))
                for c in range(4):
                    th.append(lambda c=c: t_mul(c))
                for c0 in (0, 2):
                    for c in (c0, c0 + 1):
                        th.append(lambda c=c: t_act(c))
                    for c in (c0, c0 + 1):
                        th.append(lambda c=c: t_aff(c))
                    for c in (c0, c0 + 1):
                        th.append(lambda c=c: t_out(c))
                return taps, th

            def mix_E(tile, jps, bg=None, nd=2):
                for jp in jps:
                    ga = blockref(B_IN + 6 + jp)
                    po = blockref(B_PO + jp)
                    gb = blockref(B_IN + 10 + jp)
                    co = blockref(B_CO + jp)
                    for jj in range(2):
                        j = jp * 2 + jj
                        pga_t, pga_b = proj_group(ga[0], ga[1], jj * 1024, NKC, hrhs, h_b)
                        pA_t, pA_b = proj_group(po[0], po[1], jj * 512, 4, lambda kc: mixed[:, kc, :], mx_b)
                        pgb_t, pgb_b = proj_group(gb[0], gb[1], jj * 1024, NKC, hrhs, h_b)
                        pB_t, pB_b = proj_group(co[0], co[1], jj * 512, 4, lambda kc: convact[:, kc, :], ca_b)
                        ia = rot("mt", NMT)
                        ib = rot("mt", NMT)
                        P.add(ACT, lambda e, ia=ia, pga_t=pga_t: e.activation(out=mtmp[:, ia, :], in_=pga_t[:, :], func=AF.Sigmoid),
                              reads=[pga_b], writes=[mt_b[ia]])
                        P.add(ACT, lambda e, ib=ib, pgb_t=pgb_t: e.activation(out=mtmp[:, ib, :], in_=pgb_t[:, :], func=AF.Sigmoid),
                              reads=[pgb_b], writes=[mt_b[ib]])
                        P.add(DVE, lambda e, ia=ia, pA_t=pA_t: e.tensor_tensor(out=mtmp[:, ia, :], in0=pA_t[:, :], in1=mtmp[:, ia, :], op=ALU.mult),
                              reads=[pA_b, mt_b[ia]], writes=[mt_b[ia]])
                        P.add(DVE, lambda e, ib=ib, pB_t=pB_t: e.tensor_tensor(out=mtmp[:, ib, :], in0=pB_t[:, :], in1=mtmp[:, ib, :], op=ALU.mult),
                              reads=[pB_b, mt_b[ib]], writes=[mt_b[ib]])
                        P.add(POOL, lambda e, ia=ia, ib=ib, j=j: e.tensor_tensor(out=merged_ap(j), in0=mtmp[:, ia, :], in1=mtmp[:, ib, :], op=ALU.add),
                              reads=[mt_b[ia], mt_b[ib]], writes=[mg_b[j]])
                        if bg is not None:
                            drain(bg, nd)
                    ws.release(4)

            def mix_wo(tile, bg=None):
                wo_slots = [blockref(B_WO + i) for i in range(4)]

                def y_wo(jo):
                    st_, sb_ = wo_slots[jo // 2]
                    r_ = proj_group(st_, sb_, (jo % 2) * 1024, NKC, merged_ap, mg_b)
                    if jo % 2 == 1:
                        ws.release(1)
                    return r_
                return post_groups(y_wo, bg)

            def mlp_up(tile, blks, bg=None):
                for blk in blks:
                    st_, sb_ = blockref(B_UP + blk)
                    for jj in range(2):
                        if bg is not None:
                            drain(bg, 4)
                        jf = blk * 2 + jj
                        ps_t, ps_b = proj_group(st_, sb_, jj * 1024, NKC, h2rhs, h2_b)
                        i = rot("mt", NMT)
                        P.add(ACT, lambda e, i=i, ps_t=ps_t: e.activation(out=mtmp[:, i, :], in_=ps_t[:, :], func=AF.Relu),
                              reads=[ps_b], writes=[mt_b[i]])
                        P.add(ACT, lambda e, i=i, jf=jf: e.activation(out=fbuf[:, jf, :], in_=mtmp[:, i, :], func=AF.Square),
                              reads=[mt_b[i]], writes=[f_b[jf]])
                    ws.release(1)

            def mlp_down(tile, bg=None):
                def y_dn(jo):
                    ps_t, ps_b = new_bank()
                    for half in range(2):
                        if bg is not None:
                            drain(bg[0], 8)
                            if jo * 2 + half >= 9:
                                drain(bg[1], 6)
                        st_, sb_ = blockref(B_DN + jo * 2 + half)
                        for kk in range(16):
                            kc = half * 16 + kk
                            P.add(PE, lambda e, kk=kk, kc=kc, st_=st_, ps_t=ps_t: e.matmul(
                                ps_t[:, :], st_[:, kk * 128:(kk + 1) * 128], fbuf[:, kc, :], start=(kc == 0), stop=(kc == NFC - 1)),
                                reads=[sb_, f_b[kc]], writes=[ps_b])
                        ws.release(1)
                    return ps_t, ps_b
                return post_groups(y_dn)

            def flush(bg):
                while bg:
                    bg.pop(0)()

            load_x(0)
            ws.start()
            ir = norm_stats(0)
            norm_apply(0, ir, C_NMP, h, h_b)
            fe_proj(0)
            fe_pool(0)
            if n_tiles > 1:
                preconvert([B_DN + i for i in range(16)] + [B_UP + i for i in range(2)])
            taps, lnth = fe_conv(0)
            flush(taps)
            flush(lnth)
            bg_post = []
            for tile in range(n_tiles):
                b = tile % 2
                nxt = tile + 1 if tile + 1 < n_tiles else None
                overlap = nxt is not None
                mix_E(tile, (0, 1), bg_post, 2)
                flush(bg_post)
                if tile >= 1:
                    store_x(tile - 1)
                    if nxt is not None:
                        load_x(nxt)
                mix_E(tile, (2, 3))
                if tile == 0 and nxt is not None:
                    set_stage_mode("none")
                    load_x(nxt)
                bg_h = []
                if overlap:
                    irn = norm_stats(nxt % 2)
                    norm_apply(nxt % 2, irn, C_NMP, h, h_b, bg_h)
                ir1 = mix_wo(tile, bg_h)
                flush(bg_h)
                post_chain(b, ir1, C_NMPOST)
                if tile == 0 and nxt is not None:
                    set_stage_mode("ybuf")
                bgs = ([], [])
                if overlap:
                    fe_proj(nxt)
                    bgs = fe_conv(nxt)
                    drain(bgs[0], 12)
                ir2 = norm_stats(b)
                norm_apply(b, ir2, C_NLP, h2, h2_b)
                if overlap:
                    fe_pool(nxt)
                mlp_up(tile, range(16), bgs[0])
                ir3 = mlp_down(tile, bgs)
                flush(bgs[0])
                flush(bgs[1])
                bg_post = []
                post_chain(b, ir3, C_NLPOST, bg_post)
            flush(bg_post)
            store_x(n_tiles - 1)
            return None

        rec = WS()
        emit(Prog(), rec)
        ws = WS(order=list(rec.rec))
        P = Prog()
        info = emit(P, ws)
        assert ws.acq == len(ws.order)
        P.finalize()
        final_waits = []
        for bb in range(2):
            n = P.dma_n.get(id(x_st[bb]), 0)
            if n:
                final_waits.append((x_st[bb], 16 * n))

        block = es.enter_context(nc.Block())

        @block.sync
        def _(eng):
            P.run(SP, eng, sems, final_waits)

        @block.tensor
        def _(eng):
            P.run(PE, eng, sems)

        @block.scalar
        def _(eng):
            P.run(ACT, eng, sems)

        @block.vector
        def _(eng):
            P.run(DVE, eng, sems)

        @block.gpsimd
        def _(eng):
            P.run(POOL, eng, sems)

    return nc


def _img(Wm):
    K, N = Wm.shape
    nk, nn = K // 128, N // 128
    return np.ascontiguousarray(Wm.reshape(nk, 128, nn, 128).transpose(1, 2, 0, 3).reshape(128, nn * nk * 128))


def _cols(v, n):
    return np.ascontiguousarray(np.asarray(v, np.float32).reshape(n, 128).T)


def pack_weights(w_in, pool_w, w_pool_out, w_conv_out, w_o, w_up, w_down):
    wimg = np.zeros((128, NB_W * SLOTW), np.float32)
    wimg[:, B_IN * SLOTW:B_IN * SLOTW + 28672] = _img(w_in)
    wimg[:, B_PW * SLOTW:B_PW * SLOTW + 512] = np.asarray(pool_w).transpose(1, 0, 2).reshape(128, 512)
    ipo, ico = _img(w_pool_out), _img(w_conv_out)
    for q in range(4):
        wimg[:, (B_PO + q) * SLOTW:(B_PO + q) * SLOTW + 1024] = ipo[:, q * 1024:(q + 1) * 1024]
        wimg[:, (B_CO + q) * SLOTW:(B_CO + q) * SLOTW + 1024] = ico[:, q * 1024:(q + 1) * 1024]
    wimg[:, B_WO * SLOTW:B_WO * SLOTW + 8192] = _img(w_o)
    wimg[:, B_UP * SLOTW:B_UP * SLOTW + 32768] = _img(w_up)
    wimg[:, B_DN * SLOTW:B_DN * SLOTW + 32768] = _img(w_down)
    return wimg


def pack_small(norm_mix_pre, norm_mix_post, norm_mlp_pre, norm_mlp_post, pool_scale, conv_b, conv_ln_g, conv_ln_b, conv_w):
    small = np.zeros((128, NSMALL), np.float32)
    small[:, C_NMP:C_NMP + 8] = _cols(norm_mix_pre, 8)
    small[:, C_NMPOST:C_NMPOST + 8] = _cols(norm_mix_post, 8)
    small[:, C_NLP:C_NLP + 8] = _cols(norm_mlp_pre, 8)
    small[:, C_NLPOST:C_NLPOST + 8] = _cols(norm_mlp_post, 8)
    small[:, C_PSC:C_PSC + 4] = _cols(pool_scale, 4)
    small[:, C_CB:C_CB + 4] = _cols(conv_b, 4)
    small[:, C_LG:C_LG + 4] = _cols(conv_ln_g, 4)
    small[:, C_LB:C_LB + 4] = _cols(conv_ln_b, 4)
    cw = np.asarray(conv_w, np.float32).reshape(CONV_K, 4, 128)
    small[:, C_CW:C_CW + 4 * CONV_K] = cw.transpose(2, 1, 0).reshape(128, 4 * CONV_K)
    return small


def kernel(x, norm_mix_pre, w_in, pool_w, pool_scale, w_pool_out, conv_w, conv_b,
           conv_ln_g, conv_ln_b, w_conv_out, w_o, norm_mix_post, norm_mlp_pre,
           w_up, w_down, norm_mlp_post):
    x = np.asarray(x, np.float32)
    B = x.shape[0]
    f = lambda a: np.asarray(a, np.float32)[0]
    wimg = pack_weights(f(w_in), f(pool_w), f(w_pool_out), f(w_conv_out), f(w_o), f(w_up), f(w_down))
    small = pack_small(f(norm_mix_pre), f(norm_mix_post), f(norm_mlp_pre), f(norm_mlp_post), f(pool_scale),
                       f(conv_b), f(conv_ln_g), f(conv_ln_b), f(conv_w))
    nc = build_nc(x.shape[1] // T)
    in_maps = [{"xT": np.ascontiguousarray(x[b].T), "wimg": wimg, "small": small} for b in range(B)]
    res = run_bass_kernel_spmd(nc, in_maps, core_ids=list(range(B)))
    out = np.stack([np.ascontiguousarray(res.results[b]["outT"].T) for b in range(B)], axis=0)
    return out.astype(np.float32)
```
